# Optimizing a Trainium2 kernel written in Bass

```python
import jax, jax.numpy as jnp
from jax import lax
import numpy as np

D_MODEL = 2048
BATCH = 16
SEQ = 2048
DEPTH = 4
DEC_BATCH = 4
DEC_SEQ = 4096
PAST_LEN = 128

N_META = 16
N_MIXERS = 2
N_RET = (DEPTH + N_MIXERS - 1) // N_MIXERS
N_CONV = DEPTH // N_MIXERS
RET_HEADS = 8
RET_DK = D_MODEL // RET_HEADS
RET_DV = 2 * D_MODEL // RET_HEADS
CHUNK = 128
ROPE_BASE = 10000.0
CONV_K = 31
FFN_HIDDEN = ((8 * D_MODEL // 3 + 127) // 128) * 128
FFN_CONV_K = 3
EPS = 1e-6

kernel_name = "bidir_retention_conformer_hybrid_encoder"


def rms_norm(x, g):
    xf = x.astype(jnp.float32)
    y = xf * lax.rsqrt(jnp.mean(xf * xf, axis=-1, keepdims=True) + EPS)
    return (y * g.astype(jnp.float32)).astype(x.dtype)


def layer_norm(x, g, b):
    xf = x.astype(jnp.float32)
    mu = jnp.mean(xf, axis=-1, keepdims=True)
    var = jnp.mean(jnp.square(xf - mu), axis=-1, keepdims=True)
    y = (xf - mu) * lax.rsqrt(var + EPS)
    return (y * g.astype(jnp.float32) + b.astype(jnp.float32)).astype(x.dtype)


def depthwise_conv(x, w, b):
    k = w.shape[0]
    y = lax.conv_general_dilated(
        x, w[:, None, :].astype(x.dtype), window_strides=(1,),
        padding=[((k - 1) // 2, (k - 1) // 2)],
        dimension_numbers=('NWC', 'WIO', 'NWC'),
        feature_group_count=x.shape[-1])
    return y + b.astype(x.dtype)


def rotary(x, pos):
    half = x.shape[-1] // 2
    inv = ROPE_BASE ** (-jnp.arange(half, dtype=jnp.float32) / half)
    ang = pos.astype(jnp.float32)[:, None] * inv[None, :]
    cos = jnp.cos(ang)[None, :, None, :]
    sin = jnp.sin(ang)[None, :, None, :]
    xf = x.astype(jnp.float32)
    x1, x2 = xf[..., :half], xf[..., half:]
    return jnp.concatenate([x1 * cos - x2 * sin, x2 * cos + x1 * sin], axis=-1).astype(x.dtype)


def retention(h, w_in, decay_f, decay_b, w_out):
    b, L, _ = h.shape
    hk, hv = RET_HEADS * RET_DK, RET_HEADS * RET_DV
    proj = h @ w_in
    qkv, g = proj[..., :2 * hk + hv], proj[..., 2 * hk + hv:]
    pad = (-L) % CHUNK
    qkv = jnp.pad(qkv, ((0, 0), (pad, 0), (0, 0)))
    lp = L + pad
    nc = lp // CHUNK
    pos = jnp.arange(lp)
    q = rotary(qkv[..., :hk].reshape(b, lp, RET_HEADS, RET_DK), pos)
    k = rotary(qkv[..., hk:2 * hk].reshape(b, lp, RET_HEADS, RET_DK), pos) * (RET_DK ** -0.5)
    v = qkv[..., 2 * hk:].reshape(b, lp, RET_HEADS, RET_DV)
    q = q.reshape(b, nc, CHUNK, RET_HEADS, RET_DK)
    k = k.reshape(b, nc, CHUNK, RET_HEADS, RET_DK)
    v = v.reshape(b, nc, CHUNK, RET_HEADS, RET_DV)
    dt = q.dtype

    lgf = jax.nn.log_sigmoid(decay_f.astype(jnp.float32))
    lgb = jax.nn.log_sigmoid(decay_b.astype(jnp.float32))
    idx = jnp.arange(CHUNK, dtype=jnp.float32)
    dist = idx[:, None] - idx[None, :]
    adist = jnp.abs(dist)
    dmat = jnp.where(dist[None] >= 0,
                     jnp.exp(adist[None] * lgf[:, None, None]),
                     jnp.exp(adist[None] * lgb[:, None, None]))

    scores = jnp.einsum('bcihd,bcjhd->bchij', q, k) * dmat.astype(dt)[None, None]
    o = jnp.einsum('bchij,bcjhe->bcihe', scores, v)

    qf = q * jnp.exp((idx + 1.0)[:, None] * lgf[None, :])[:, :, None].astype(dt)
    kf = k * jnp.exp((CHUNK - 1.0 - idx)[:, None] * lgf[None, :])[:, :, None].astype(dt)
    qb = q * jnp.exp((CHUNK - idx)[:, None] * lgb[None, :])[:, :, None].astype(dt)
    kb = k * jnp.exp(idx[:, None] * lgb[None, :])[:, :, None].astype(dt)
    cdf = jnp.exp(CHUNK * lgf)[None, :, None, None]
    cdb = jnp.exp(CHUNK * lgb)[None, :, None, None]

    def make_step(cd):
        def step(s, xs):
            qc, kc, vc = xs
            out = jnp.einsum('bihd,bhde->bihe', qc, s.astype(qc.dtype))
            s = s * cd + jnp.einsum('bjhd,bjhe->bhde', kc, vc).astype(jnp.float32)
            return s, out
        return step

    s0 = jnp.zeros((b, RET_HEADS, RET_DK, RET_DV), jnp.float32)
    tm = lambda t: jnp.moveaxis(t, 1, 0)
    _, of = lax.scan(make_step(cdf), s0, (tm(qf), tm(kf), tm(v)))
    _, ob = lax.scan(make_step(cdb), s0, (tm(qb), tm(kb), tm(v)), reverse=True)
    o = o + jnp.moveaxis(of + ob, 0, 1)
    o = o.reshape(b, lp, RET_HEADS, RET_DV)[:, pad:]

    of32 = o.astype(jnp.float32)
    mu = jnp.mean(of32, axis=-1, keepdims=True)
    var = jnp.mean(jnp.square(of32 - mu), axis=-1, keepdims=True)
    o = ((of32 - mu) * lax.rsqrt(var + EPS)).astype(h.dtype).reshape(b, L, hv)
    return (jax.nn.silu(g) * o) @ w_out


def conformer_conv(h, w_pw1, b_pw1, w_dw, b_dw, ln_g, ln_b, w_pw2, b_pw2):
    a = h @ w_pw1 + b_pw1
    u, gate = jnp.split(a, 2, axis=-1)
    u = u * jax.nn.sigmoid(gate)
    u = depthwise_conv(u, w_dw, b_dw)
    u = jax.nn.silu(layer_norm(u, ln_g, ln_b))
    return u @ w_pw2 + b_pw2


def conv_ffn(h, w_up, w_dw, b_dw, w_down):
    a, val = jnp.split(h @ w_up, 2, axis=-1)
    a = depthwise_conv(a, w_dw, b_dw)
    return (jax.nn.gelu(a, approximate=True) * val) @ w_down


def setup_inputs(seed: int = 0) -> dict:
    key = jax.random.key(seed)
    ks = jax.random.split(key, 24)
    f32 = jnp.float32
    nrm = lambda k, shape, s: jax.random.normal(k, shape, f32) * s
    D = D_MODEL
    hk, hv = RET_HEADS * RET_DK, RET_HEADS * RET_DV
    base = jnp.log(2.0 ** (5.0 + jnp.arange(RET_HEADS, dtype=f32)) - 1.0)
    return {
        "x_prompt": nrm(ks[0], (BATCH, SEQ, D), 1.0),
        "x_sample": nrm(ks[1], (DEC_BATCH, DEC_SEQ, D), 1.0),
        "meta_tokens": nrm(ks[2], (N_META, D), 1.0),
        "norm_pre_mix": 1.0 + nrm(ks[3], (DEPTH, D), 0.02),
        "norm_post_mix": 1.0 + nrm(ks[4], (DEPTH, D), 0.02),
        "norm_pre_ffn": 1.0 + nrm(ks[5], (DEPTH, D), 0.02),
        "norm_post_ffn": 1.0 + nrm(ks[6], (DEPTH, D), 0.02),
        "ret_w_in": nrm(ks[7], (N_RET, D, 2 * hk + 2 * hv), D ** -0.5),
        "ret_decay_fwd": base[None, :] + nrm(ks[8], (N_RET, RET_HEADS), 0.05),
        "ret_decay_bwd": base[None, :] + nrm(ks[9], (N_RET, RET_HEADS), 0.05),
        "ret_w_out": nrm(ks[10], (N_RET, hv, D), hv ** -0.5),
        "conv_w_pw1": nrm(ks[11], (N_CONV, D, 2 * D), D ** -0.5),
        "conv_b_pw1": nrm(ks[12], (N_CONV, 2 * D), 0.01),
        "conv_w_dw": nrm(ks[13], (N_CONV, CONV_K, D), CONV_K ** -0.5),
        "conv_b_dw": nrm(ks[14], (N_CONV, D), 0.01),
        "conv_ln_g": 1.0 + nrm(ks[15], (N_CONV, D), 0.02),
        "conv_ln_b": nrm(ks[16], (N_CONV, D), 0.01),
        "conv_w_pw2": nrm(ks[17], (N_CONV, D, D), D ** -0.5),
        "conv_b_pw2": nrm(ks[18], (N_CONV, D), 0.01),
        "ffn_w_up": nrm(ks[19], (DEPTH, D, 2 * FFN_HIDDEN), D ** -0.5),
        "ffn_w_dw": nrm(ks[20], (DEPTH, FFN_CONV_K, FFN_HIDDEN), FFN_CONV_K ** -0.5),
        "ffn_b_dw": nrm(ks[21], (DEPTH, FFN_HIDDEN), 0.01),
        "ffn_w_down": nrm(ks[22], (DEPTH, FFN_HIDDEN, D), FFN_HIDDEN ** -0.5),
    }


def reference(x_prompt, x_sample, meta_tokens, norm_pre_mix, norm_post_mix, norm_pre_ffn,
              norm_post_ffn, ret_w_in, ret_decay_fwd, ret_decay_bwd, ret_w_out,
              conv_w_pw1, conv_b_pw1, conv_w_dw, conv_b_dw, conv_ln_g, conv_ln_b,
              conv_w_pw2, conv_b_pw2, ffn_w_up, ffn_w_dw, ffn_b_dw, ffn_w_down):
    def encode(x):
        b = x.shape[0]
        meta = jnp.broadcast_to(meta_tokens[None].astype(x.dtype), (b, N_META, D_MODEL))
        h = jnp.concatenate([meta, x], axis=1)
        for i in range(DEPTH):
            j = i // N_MIXERS
            hn = rms_norm(h, norm_pre_mix[i])
            if i % N_MIXERS == 0:
                m = retention(hn, ret_w_in[j], ret_decay_fwd[j], ret_decay_bwd[j], ret_w_out[j])
            else:
                m = conformer_conv(hn, conv_w_pw1[j], conv_b_pw1[j], conv_w_dw[j], conv_b_dw[j],
                                   conv_ln_g[j], conv_ln_b[j], conv_w_pw2[j], conv_b_pw2[j])
            h = h + rms_norm(m, norm_post_mix[i])
            f = conv_ffn(rms_norm(h, norm_pre_ffn[i]), ffn_w_up[i], ffn_w_dw[i], ffn_b_dw[i],
                         ffn_w_down[i])
            h = h + rms_norm(f, norm_post_ffn[i])
        return h[:, N_META:]

    y_prompt = encode(x_prompt)
    y_sample = encode(x_sample)
    return (y_prompt, y_sample)
```

```python
import numpy as np
from contextlib import ExitStack
import concourse.bass as bass
import concourse.mybir as mybir
from concourse.bass_utils import run_bass_kernel_spmd

F32 = mybir.dt.float32
BF16 = mybir.dt.bfloat16
I32 = mybir.dt.int32
AF = mybir.ActivationFunctionType
ALU = mybir.AluOpType

D = 2048
KC = 16
H = 8
HV = 4096
FH = 5504
FC = 43
CK = 31
NMETA = 16
EPS = 1e-6
LN16 = float(np.log(1.0 / 16.0))


_UID = [0]


def _next_uid():
    _UID[0] += 1
    return _UID[0]


class Q:
    def __init__(self, nc, eng, name, es, step=1):
        self.uid = _next_uid()
        self.e = eng
        self.sem = es.enter_context(nc.semaphore(name))
        self.n = 0
        self.step = step
        self.seen = {}

    def mark(self, ins):
        ins.then_inc(self.sem, self.step)
        self.n += self.step
        return (self, self.n)

    def wait(self, *marks):
        for m in marks:
            if m is None:
                continue
            if isinstance(m, dict):
                self.wait(*m.values())
                continue
            src, n = m
            if self.seen.get(src.uid, 0) >= n:
                continue
            self.e.wait_ge(src.sem, n)
            self.seen[src.uid] = n


class DS:
    def __init__(self, nc, name, es):
        self.uid = _next_uid()
        self.sem = es.enter_context(nc.semaphore(name))
        self.n = 0

    def mark(self, ins):
        ins.then_inc(self.sem, 16)
        self.n += 16
        return (self, self.n)


def _merge(d, m):
    if m is None:
        return
    src, n = m
    k = src.uid
    if k not in d or d[k][1] < n:
        d[k] = (src, n)


class Buf:
    def __init__(self, t, ds=None):
        self.t = t
        self.w = {}
        self.r = {}
        self.prev = {}
        self.ds = ds

    def begin(self):
        self.prev = {}
        for m in list(self.w.values()) + list(self.r.values()):
            _merge(self.prev, m)
        self.w = {}
        self.r = {}

    def wrote(self, m):
        _merge(self.w, m)

    def read(self, m):
        _merge(self.r, m)


class Ring:
    def __init__(self, bufs):
        self.b = bufs
        self.i = 0

    def next(self):
        b = self.b[self.i % len(self.b)]
        self.i += 1
        return b


def build_program(SEGC, DEPTH):
    NCH = 3 * (SEGC + 1)
    T = NCH * 128
    NRET = (DEPTH + 1) // 2
    NCONV = DEPTH // 2
    SPECIAL = sorted({0, 1 + SEGC, 2 + 2 * SEGC, 1 + 2 * SEGC, 2 + 3 * SEGC})
    pc = {}
    off = 0
    for i in range(DEPTH):
        for nm, w in (("g_pre_mix", KC), ("g_post_mix", KC), ("g_pre_ffn", KC), ("g_post_ffn", KC),
                      ("ffn_w_dw", FC * 3), ("ffn_b_dw", FC)):
            pc[(nm, i)] = off
            off += w
    for j in range(NCONV):
        for nm, w in (("b_pw1", 2 * KC), ("w_dw", KC * CK), ("b_dw", KC), ("ln_g", KC), ("ln_b", KC), ("b_pw2", KC)):
            pc[(nm, j)] = off
            off += w
    NP = off

    nc = bass.Bass("TRN2", target_bir_lowering=False)
    dt = nc.dram_tensor

    def din(name, shape, dtp=F32):
        return dt(name, shape, dtp, kind="ExternalInput").ap()

    def dsc(name, shape, dtp):
        return dt(name, shape, dtp, kind="Internal").ap()

    XT = din("xt", [D, T])
    PRM = din("prm", [128, NP])
    DEC = din("dec", [128, NRET * 16])
    KPD = din("kp", [128, 2 * NCH + 2])
    MSKD = din("msk", [128, len(SPECIAL) * 128])
    COSF = din("cosf", [128, T])
    SINF = din("sinf", [128, T])
    COST = din("cost", [T, 128])
    SINT = din("sint", [T, 128])
    W_IN = din("ret_w_in", [NRET * D * 12288 // 2048, 2048])
    W_OUT = din("ret_w_out", [NRET * HV * D // 2048, 2048])
    W_PW1 = din("conv_w_pw1", [max(NCONV, 1) * D * 2 * D // 2048, 2048])
    W_PW2 = din("conv_w_pw2", [max(NCONV, 1) * D * D // 2048, 2048])
    W_UP = din("ffn_w_up", [DEPTH * D * 2 * FH // 2048, 2048])
    W_DN = din("ffn_w_down", [DEPTH * FH * D // 2048, 2048])
    YT = dt("yt", [D, T], F32, kind="ExternalOutput").ap()

    WB_IN = [dsc(f"wb_in{j}", [D * 12288 // 2048, 2048], BF16) for j in range(NRET)]
    WB_OUT = [dsc(f"wb_out{j}", [HV * D // 2048, 2048], BF16) for j in range(NRET)]
    WB_PW1 = [dsc(f"wb_pw1{j}", [D * 2 * D // 2048, 2048], BF16) for j in range(NCONV)]
    WB_PW2 = [dsc(f"wb_pw2{j}", [D * D // 2048, 2048], BF16) for j in range(NCONV)]
    WB_UP = [dsc(f"wb_up{i}", [D * 2 * FH // 2048, 2048], BF16) for i in range(DEPTH)]
    WB_DN = [dsc(f"wb_dn{i}", [FH * D // 2048, 2048], BF16) for i in range(DEPTH)]
    HA = dsc("ha", [D, T], F32)
    HB = dsc("hb", [D, T], F32)
    UT = dsc("ut", [D, T], F32)
    QTd = dsc("qtd", [NCH, 128, D], BF16)
    KTd = dsc("ktd", [NCH, 128, D], BF16)
    KFd = dsc("kfd", [T, D], BF16)
    KBd = dsc("kbd", [T, D], BF16)
    Vd = dsc("vd", [T, HV], BF16)
    Gd = dsc("gd", [T, HV], BF16)
    SBd = dsc("sbd", [NCH, 128, 16 * 512], BF16)
    GTd = dsc("gtd", [NCH, 128, HV], BF16)

    es = ExitStack()
    with es:
        def sb(name, shape, dtp):
            return es.enter_context(nc.sbuf_tensor("sb_" + name, shape, dtp))

        PE = Q(nc, nc.tensor, "s_pe", es)
        ACT = Q(nc, nc.scalar, "s_act", es)
        DVE = Q(nc, nc.vector, "s_dve", es)
        POOL = Q(nc, nc.gpsimd, "s_pool", es)
        SP = Q(nc, nc.sync, "s_sp", es)
        dcount = [0]

        def newds():
            dcount[0] += 1
            return DS(nc, f"ds{dcount[0]}", es)

        def load(buf, out_ap, in_ap, extra=()):
            SP.wait(buf.prev, *extra)
            m = buf.ds.mark(nc.sync.dma_start(out=out_ap, in_=in_ap))
            buf.wrote(m)
            return m

        store_marks = {}

        def store(buf, out_ap, in_ap, sds):
            POOL.wait(buf.w)
            m = sds.mark(nc.gpsimd.dma_start(out=out_ap, in_=in_ap))
            buf.read(m)
            _merge(store_marks, m)
            return m

        def phase_barrier():
            SP.wait(store_marks)

        cast_q = []
        wready = {}

        def plan_cast(key, src, row0, nrows, dst):
            ds_ = newds()
            n = 0
            for r0 in range(0, nrows, 2048):
                rn = min(2048, nrows - r0)
                cast_q.append((ds_, dst[r0:r0 + rn, :], src[row0 + r0:row0 + r0 + rn, :]))
                n += 16
            wready[key] = (ds_, n)

        for i in range(DEPTH):
            j = i // 2
            if i % 2 == 0:
                plan_cast(("in", j), W_IN, j * (D * 12288 // 2048), D * 12288 // 2048, WB_IN[j])
                plan_cast(("out", j), W_OUT, j * (HV * D // 2048), HV * D // 2048, WB_OUT[j])
            else:
                plan_cast(("pw1", j), W_PW1, j * (D * 2 * D // 2048), D * 2 * D // 2048, WB_PW1[j])
                plan_cast(("pw2", j), W_PW2, j * (D * D // 2048), D * D // 2048, WB_PW2[j])
            plan_cast(("up", i), W_UP, i * (D * 2 * FH // 2048), D * 2 * FH // 2048, WB_UP[i])
            plan_cast(("dn", i), W_DN, i * (FH * D // 2048), FH * D // 2048, WB_DN[i])
        cast_pos = [0]

        def pump(n):
            for _ in range(n):
                if cast_pos[0] >= len(cast_q):
                    return
                ds_, o, i_ = cast_q[cast_pos[0]]
                cast_pos[0] += 1
                ds_.mark(nc.gpsimd.dma_start(out=o, in_=i_))

        def wwait(key):
            ds_, n = wready[key]
            while ds_.n < n:
                pump(1)
            SP.wait((ds_, n))

        prm = Buf(sb("prm", [128, NP], F32), newds())
        dec = Buf(sb("dec", [128, NRET * 16], F32), newds())
        kp = Buf(sb("kp", [128, 2 * NCH + 2], F32), newds())
        msk = Buf(sb("msk", [128, len(SPECIAL) * 128], F32), newds())
        ones = Buf(sb("ones", [128, 128], BF16))
        ident = Buf(sb("ident", [128, 128], BF16))
        iof = Buf(sb("iof", [128, 128], F32))
        iop = Buf(sb("iop", [128, 1], F32))
        ioi = Buf(sb("ioi", [128, 128], I32))
        ipi = Buf(sb("ipi", [128, 1], I32))
        load(prm, prm.t[:], PRM)
        load(dec, dec.t[:], DEC)
        load(kp, kp.t[:], KPD)
        load(msk, msk.t[:], MSKD)
        ones.wrote(DVE.mark(nc.vector.memset(ones.t[:], 1.0)))
        ioi.wrote(POOL.mark(nc.gpsimd.iota(ioi.t[:], pattern=[[1, 128]], base=0, channel_multiplier=0)))
        ipi.wrote(POOL.mark(nc.gpsimd.iota(ipi.t[:], pattern=[[0, 1]], base=0, channel_multiplier=1)))
        DVE.wait(ioi.w, ipi.w)
        iof.wrote(DVE.mark(nc.vector.tensor_copy(out=iof.t[:], in_=ioi.t[:])))
        iop.wrote(DVE.mark(nc.vector.tensor_copy(out=iop.t[:], in_=ipi.t[:])))
        DVE.wait(iof.w, iop.w)
        ident.wrote(DVE.mark(nc.vector.tensor_scalar(out=ident.t[:], in0=iof.t[:], scalar1=iop.t[:, 0:1],
                                                     scalar2=None, op0=ALU.is_equal)))

        def P(nm, idx, col=0, n=1):
            o = pc[(nm, idx)] + col
            return prm.t[:, o:o + n]

        pg = Ring([Buf(es.enter_context(nc.psum_tensor(f"pg{i}", [128, 4, 512], F32))) for i in range(2)])
        ARENA = 194 * 1024
        COMMON = 104 * 1024
        TBL = 20 * 1024
        work = sb("arena", [128, ARENA // 4], F32)

        def carve_at(base, limit, spec):
            res = {}
            o = base // 4
            for nm, shp, dtp in spec:
                nel = int(np.prod(shp))
                nbytes = nel * (4 if dtp in (F32, I32) else 2)
                nw = (nbytes + 3) // 4
                ap = work[:, o:o + nw]
                if dtp != F32:
                    ap = ap.bitcast(dtp)[:, 0:nel]
                names = "abcd"[:len(shp)]
                if len(shp) > 1:
                    kw = {names[i]: shp[i] for i in range(len(shp) - 1)}
                    ap = ap.rearrange("p (" + " ".join(names) + ") -> p " + " ".join(names), **kw)
                res[nm] = ap
                o += nw
            assert o * 4 <= limit, (o * 4, limit)
            return res

        cm = carve_at(0, COMMON, [("xm", [KC, 512], F32), ("hn", [KC, 512], BF16), ("w0", [16, 512], BF16),
                                  ("w1", [16, 512], BF16), ("w2", [16, 512], BF16), ("rstd", [512], F32),
                                  ("h0", [512], F32), ("h1", [512], F32), ("h2", [512], F32)])
        wring = Ring([Buf(cm[f"w{i}"], newds()) for i in range(3)])
        xm = Buf(cm["xm"], newds())
        hn = Buf(cm["hn"])
        rstd = Buf(cm["rstd"])
        hres = Ring([Buf(cm[f"h{i}"], newds()) for i in range(3)])
        xm_sds = newds()

        def carve(spec):
            return carve_at(COMMON, ARENA - TBL, spec)

        work_guard = Buf(work)

        def phase_sync_all():
            marks = []
            for q in (PE, ACT, DVE, POOL):
                if q.n > 0:
                    q.e.wait_ge(q.sem, q.n)
                marks.append(q.mark(q.e.nop(nofuse=True)))
            for q in (PE, ACT, DVE, POOL, SP):
                q.wait(*marks)
                q.wait(store_marks)

        def front(src, c0, N, gname, li):
            xm.begin()
            load(xm, xm.t[:, :, 0:N], src.rearrange("(kc p) t -> p kc t", p=128)[:, :, c0:c0 + N])
            hn.begin()
            ACT.wait(xm.w, hn.prev)
            for kc in range(KC):
                ins = nc.scalar.activation(out=hn.t[:, kc, 0:N], in_=xm.t[:, kc, 0:N], func=AF.Square)
            m = ACT.mark(ins)
            hn.wrote(m)
            g = pg.next()
            g.begin()
            PE.wait(hn.w, g.prev, ones.w)
            for kc in range(KC):
                ins = nc.tensor.matmul(g.t[:, 0, 0:N], lhsT=ones.t[:], rhs=hn.t[:, kc, 0:N],
                                       start=(kc == 0), stop=(kc == KC - 1))
            pm = PE.mark(ins)
            hn.read(pm)
            rstd.begin()
            ACT.wait(pm, rstd.prev)
            m0 = ACT.mark(nc.scalar.activation(out=rstd.t[:, 0:N], in_=g.t[:, 0, 0:N], func=AF.Sqrt, scale=1.0 / D,
                                               bias=EPS))
            DVE.wait(m0)
            m1 = DVE.mark(nc.vector.reciprocal(out=rstd.t[:, 0:N], in_=rstd.t[:, 0:N]))
            g.read(m0)
            g.read(m1)
            rstd.wrote(m1)
            hn.begin()
            DVE.wait(m1, hn.prev, prm.w)
            for kc in range(KC):
                ins = nc.vector.scalar_tensor_tensor(out=hn.t[:, kc, 0:N], in0=xm.t[:, kc, 0:N],
                                                     scalar=P(gname, li, kc), in1=rstd.t[:, 0:N],
                                                     op0=ALU.mult, op1=ALU.mult)
            m2 = DVE.mark(ins)
            hn.wrote(m2)
            xm.read(m2)
            rstd.read(m2)

        def back(N, c0, gname, li, hsrc, dst, special_mask):
            hn.begin()
            ACT.wait(xm.w, hn.prev)
            for kc in range(KC):
                ins = nc.scalar.activation(out=hn.t[:, kc, 0:N], in_=xm.t[:, kc, 0:N], func=AF.Square)
            m = ACT.mark(ins)
            hn.wrote(m)
            xm.read(m)
            g = pg.next()
            g.begin()
            PE.wait(hn.w, g.prev)
            for kc in range(KC):
                ins = nc.tensor.matmul(g.t[:, 0, 0:N], lhsT=ones.t[:], rhs=hn.t[:, kc, 0:N],
                                       start=(kc == 0), stop=(kc == KC - 1))
            pm = PE.mark(ins)
            hn.read(pm)
            rstd.begin()
            ACT.wait(pm, rstd.prev)
            m0 = ACT.mark(nc.scalar.activation(out=rstd.t[:, 0:N], in_=g.t[:, 0, 0:N], func=AF.Sqrt, scale=1.0 / D,
                                               bias=EPS))
            DVE.wait(m0)
            m1 = DVE.mark(nc.vector.reciprocal(out=rstd.t[:, 0:N], in_=rstd.t[:, 0:N]))
            g.read(m0)
            g.read(m1)
            rstd.wrote(m1)
            hs = hsrc.rearrange("(kc p) t -> p kc t", p=128)
            last = None
            for kc in range(KC):
                hb = hres.next()
                hb.begin()
                load(hb, hb.t[:, 0:N], hs[:, kc, c0:c0 + N])
                DVE.wait(m1, hb.w, xm.w, last)
                ma = DVE.mark(nc.vector.scalar_tensor_tensor(out=xm.t[:, kc, 0:N], in0=xm.t[:, kc, 0:N],
                                                             scalar=P(gname, li, kc), in1=rstd.t[:, 0:N],
                                                             op0=ALU.mult, op1=ALU.mult))
                DVE.wait(ma)
                last = DVE.mark(nc.vector.tensor_tensor(out=xm.t[:, kc, 0:N], in0=xm.t[:, kc, 0:N],
                                                        in1=hb.t[:, 0:N], op=ALU.add))
                hb.read(last)
                if special_mask:
                    for si, sc in enumerate(SPECIAL):
                        lo = sc * 128 - c0
                        if 0 <= lo < N:
                            DVE.wait(last, msk.w)
                            last = DVE.mark(nc.vector.tensor_tensor(
                                out=xm.t[:, kc, lo:lo + 128], in0=xm.t[:, kc, lo:lo + 128],
                                in1=msk.t[:, si * 128:(si + 1) * 128], op=ALU.mult))
            xm.wrote(last)
            rstd.read(last)
            store(xm, dst.rearrange("(kc p) t -> p kc t", p=128)[:, :, c0:c0 + N], xm.t[:, :, 0:N], xm_sds)

        def load_slab(WB, ncolsW, k0, kn, pieces, wkey):
            W2 = WB.rearrange("a b -> (a b)").rearrange("(k n) -> k n", n=ncolsW)
            sl = wring.next()
            sl.begin()
            o = 0
            for (cc, ncol) in pieces:
                load(sl, sl.t[:, 0:kn, o:o + ncol],
                     W2[k0 * 128:(k0 + kn) * 128, cc:cc + ncol].rearrange("(kc p) n -> p kc n", p=128))
                o += ncol
            return sl

        def linear_fm(xrhs, xbuf, KCin, WB, ncolsW, groups, N, evac, wkey):
            wwait(wkey)
            for gi, pieces in enumerate(groups):
                pump(1 if gi % 4 == 0 else 0)
                ncols = sum(p[1] for p in pieces)
                nm = ncols // 128
                g = pg.next()
                g.begin()
                first = True
                for k0 in range(0, KCin, 16):
                    kn = min(16, KCin - k0)
                    sl = load_slab(WB, ncolsW, k0, kn, pieces, wkey)
                    PE.wait(sl.w, xbuf.w)
                    if first:
                        PE.wait(g.prev)
                        first = False
                    for m in range(nm):
                        for kc in range(kn):
                            ins = nc.tensor.matmul(g.t[:, m, 0:N], lhsT=sl.t[:, kc, m * 128:(m + 1) * 128],
                                                   rhs=xrhs(k0 + kc), start=(k0 + kc == 0),
                                                   stop=(k0 + kc == KCin - 1))
                    pm = PE.mark(ins)
                    sl.read(pm)
                xbuf.read(pm)
                g.wrote(pm)
                evac(gi, g, nm)

        def linear_tok(xbuf, nchunk, WB, ncolsW, groups, evac, wkey):
            wwait(wkey)
            for gi, pieces in enumerate(groups):
                pump(1 if gi % 4 == 0 else 0)
                g = pg.next()
                g.begin()
                sl = load_slab(WB, ncolsW, 0, KC, pieces, wkey)
                PE.wait(sl.w, xbuf.w, g.prev)
                for ci in range(nchunk):
                    for kc in range(KC):
                        ins = nc.tensor.matmul(g.t[:, ci, :], lhsT=xbuf.t[:, kc, ci * 128:(ci + 1) * 128],
                                               rhs=sl.t[:, kc, :], start=(kc == 0), stop=(kc == KC - 1))
                pm = PE.mark(ins)
                sl.read(pm)
                xbuf.read(pm)
                g.wrote(pm)
                evac(gi, g, nchunk)

        blocks = [(c0, min(512, T - c0)) for c0 in range(0, T, 512)]

        def retention_layer(j, li, hin, hout):
            tb = carve_at(ARENA - TBL, ARENA, [("lg", [16], F32), ("nlg", [16], F32), ("bq", [16], F32), ("t1", [128], F32),
                        ("t2", [128], F32), ("t3", [128], F32), ("dmt", [8, 128], F32),
                        ("fq", [16, 128], BF16), ("fb", [16, 128], BF16), ("dkf", [8], F32), ("dkb", [8], F32),
                        ("cd", [16], F32), ("cdkf", [NCH, 8], F32), ("cdkb", [NCH, 8], F32), ("pj", [2], F32),
                        ("fq32", [128], F32)])
            TB = Buf(work)
            TB.begin()
            d0 = j * 16
            for q in (ACT, DVE):
                q.wait(dec.w, kp.w, iof.w, iop.w)
            a1 = ACT.mark(nc.scalar.activation(out=tb["lg"][:, :], in_=dec.t[:, d0:d0 + 16], func=AF.Exp, scale=-1.0))
            ACT.wait(a1)
            a2 = ACT.mark(nc.scalar.activation(out=tb["nlg"][:, :], in_=tb["lg"][:, :], func=AF.Ln, bias=1.0))
            DVE.wait(a2)
            v1 = DVE.mark(nc.vector.tensor_scalar(out=tb["lg"][:, :], in0=tb["nlg"][:, :], scalar1=-1.0,
                                                  scalar2=None, op0=ALU.mult))
            DVE.wait(v1)
            nc.vector.tensor_scalar(out=tb["pj"][:, 0:1], in0=iop.t[:, 0:1], scalar1=-1.0, scalar2=127.0,
                                    op0=ALU.mult, op1=ALU.add)
            nc.vector.tensor_copy(out=tb["pj"][:, 1:2], in_=iop.t[:, 0:1])
            nc.vector.tensor_scalar(out=tb["bq"][:, 0:8], in0=tb["lg"][:, 0:8], scalar1=LN16, scalar2=None, op0=ALU.add)
            nc.vector.tensor_scalar(out=tb["bq"][:, 8:16], in0=tb["lg"][:, 8:16], scalar1=128.0, scalar2=LN16,
                                    op0=ALU.mult, op1=ALU.add)
            v2 = DVE.mark(nc.vector.tensor_scalar(out=tb["cd"][:, :], in0=tb["lg"][:, :], scalar1=128.0, scalar2=None,
                                                  op0=ALU.mult))
            DVE.wait(v2)
            nc.vector.tensor_scalar(out=tb["dkf"][:, :], in0=tb["lg"][:, 0:8], scalar1=tb["pj"][:, 0:1], scalar2=None,
                                    op0=ALU.mult)
            nc.vector.tensor_scalar(out=tb["dkb"][:, :], in0=tb["lg"][:, 8:16], scalar1=tb["pj"][:, 1:2], scalar2=None,
                                    op0=ALU.mult)
            v3 = DVE.mark(nc.vector.tensor_scalar(out=tb["t3"][:, :], in0=iof.t[:, :], scalar1=iop.t[:, 0:1],
                                                  scalar2=None, op0=ALU.subtract))
            DVE.wait(v3)
            nc.vector.tensor_scalar(out=tb["t1"][:, :], in0=tb["t3"][:, :], scalar1=0.0, scalar2=None, op0=ALU.max)
            v4 = DVE.mark(nc.vector.tensor_scalar(out=tb["t2"][:, :], in0=tb["t3"][:, :], scalar1=-1.0, scalar2=0.0,
                                                  op0=ALU.mult, op1=ALU.max))
            ACT.wait(v4)
            e1 = ACT.mark(nc.scalar.activation(out=tb["cd"][:, :], in_=tb["cd"][:, :], func=AF.Exp))
            nc.scalar.activation(out=tb["dkf"][:, :], in_=tb["dkf"][:, :], func=AF.Exp)
            e2 = ACT.mark(nc.scalar.activation(out=tb["dkb"][:, :], in_=tb["dkb"][:, :], func=AF.Exp))
            for h in range(H):
                DVE.wait(v4, e2)
                nc.vector.tensor_scalar(out=tb["t3"][:, :], in0=tb["t1"][:, :], scalar1=tb["lg"][:, h:h + 1],
                                        scalar2=None, op0=ALU.mult)
                vv = DVE.mark(nc.vector.tensor_scalar(out=tb["fq32"][:, :], in0=tb["t2"][:, :],
                                                      scalar1=tb["lg"][:, 8 + h:9 + h], scalar2=None, op0=ALU.mult))
                DVE.wait(vv)
                vv = DVE.mark(nc.vector.tensor_tensor(out=tb["t3"][:, :], in0=tb["t3"][:, :], in1=tb["fq32"][:, :],
                                                      op=ALU.add))
                ACT.wait(vv)
                e2 = ACT.mark(nc.scalar.activation(out=tb["dmt"][:, h, :], in_=tb["t3"][:, :], func=AF.Exp,
                                                   bias=LN16 if False else 0.0))
                for dcc in range(2):
                    nc.scalar.activation(out=tb["fq"][:, 2 * h + dcc, :], in_=iof.t[:, :], func=AF.Exp,
                                         scale=tb["lg"][:, h:h + 1], bias=tb["bq"][:, h:h + 1])
                    e2 = ACT.mark(nc.scalar.activation(out=tb["fb"][:, 2 * h + dcc, :], in_=iof.t[:, :], func=AF.Exp,
                                                       scale=tb["nlg"][:, 8 + h:9 + h], bias=tb["bq"][:, 8 + h:9 + h]))
            DVE.wait(e1, e2)
            vv = DVE.mark(nc.vector.tensor_scalar(out=tb["dmt"][:, :, :], in0=tb["dmt"][:, :, :], scalar1=1.0 / 16.0,
                                                  scalar2=None, op0=ALU.mult))
            for c in range(NCH):
                nc.vector.tensor_scalar(out=tb["cdkf"][:, c, :], in0=tb["cd"][:, 0:8], scalar1=kp.t[:, c:c + 1],
                                        scalar2=None, op0=ALU.mult)
                vv = DVE.mark(nc.vector.tensor_scalar(out=tb["cdkb"][:, c, :], in0=tb["cd"][:, 8:16],
                                                      scalar1=kp.t[:, NCH + 1 + c:NCH + 2 + c], scalar2=None,
                                                      op0=ALU.mult))
            TBM = vv
            for q in (ACT, DVE, POOL, PE):
                q.wait(TBM, e2)
            def carve2(spec, full=False):
                return carve_at(0 if full else COMMON, ARENA - TBL, spec)

            ra = carve2([("qo", [4, 16, 128], BF16), ("ko", [4, 16, 128], BF16),
                         ("vt0", [4, 512], BF16),
                         ("kf0", [4, 512], BF16),
                         ("kb0", [4, 512], BF16),
                         ("orot", [4, 2, 2, 128], F32), ("ta", [512], F32), ("tb", [512], F32),
                         ("tc", [512], F32), ("td", [512], F32),
                         ("cf", [512], F32), ("sf", [512], F32), ("ct", [4, 128], F32), ("st", [4, 128], F32)])
            qo = Buf(ra["qo"]); ko = Buf(ra["ko"])
            vtr = Ring([Buf(ra["vt0"])])
            kfr = Ring([Buf(ra["kf0"])])
            kbr = Ring([Buf(ra["kb0"])])
            orot = Buf(ra["orot"])
            tmp = Buf(ra["ta"])
            cf = Buf(ra["cf"], newds()); sf = Buf(ra["sf"], newds())
            ct = Buf(ra["ct"], newds()); st = Buf(ra["st"], newds())
            sd = {k: newds() for k in ("q", "k", "v", "g", "kf", "kb")}
            WB = WB_IN[j]
            for (c0, N) in blocks:
                nchunk = N // 128
                cb = c0 // 128
                pump(2)
                front(hin, c0, N, "g_pre_mix", li)
                for b_, src_ in ((cf, COSF), (sf, SINF)):
                    b_.begin()
                    load(b_, b_.t[:, 0:N], src_[:, c0:c0 + N])
                for b_, src_ in ((ct, COST), (st, SINT)):
                    b_.begin()
                    load(b_, b_.t[:, 0:nchunk, :], src_[c0:c0 + N, :].rearrange("(c p) f -> p c f", p=128))

                def ev_v(gi, g, nch_, dst=Vd, ring=vtr, func=AF.Copy, key="v"):
                    vt = ring.next()
                    vt.begin()
                    ACT.wait(g.w, vt.prev)
                    m = ACT.mark(nc.scalar.activation(out=vt.t[:, 0:nch_, :], in_=g.t[:, 0:nch_, :], func=func))
                    g.read(m)
                    vt.wrote(m)
                    store(vt, dst[c0:c0 + N, gi * 512:(gi + 1) * 512].rearrange("(c p) f -> p c f", p=128),
                          vt.t[:, 0:nch_, :], sd[key])

                linear_tok(hn, nchunk, WB, 12288, [[(4096 + 512 * gi, 512)] for gi in range(8)], ev_v, ("in", j))
                linear_tok(hn, nchunk, WB, 12288, [[(8192 + 512 * gi, 512)] for gi in range(8)],
                           lambda gi, g, n_: ev_v(gi, g, n_, Gd, vtr, AF.Silu, "g"), ("in", j))

                def ev_ktok(gi, g, nch_):
                    gv = g.t.rearrange("p c (a b f) -> p c a b f", a=2, b=2)
                    orot.begin()
                    tmp.begin()
                    DVE.wait(g.w, orot.prev, tmp.prev, ct.w, st.w)
                    last = None
                    for hh in range(2):
                        x1 = gv[:, 0:nch_, hh, 0, :]
                        x2 = gv[:, 0:nch_, hh, 1, :]
                        cs = ct.t[:, 0:nch_, :]
                        sn = st.t[:, 0:nch_, :]
                        ta = ra["ta"].rearrange("p (c f) -> p c f", c=4)[:, 0:nch_, :]
                        tbb = ra["tb"].rearrange("p (c f) -> p c f", c=4)[:, 0:nch_, :]
                        tcc = ra["tc"].rearrange("p (c f) -> p c f", c=4)[:, 0:nch_, :]
                        tdd = ra["td"].rearrange("p (c f) -> p c f", c=4)[:, 0:nch_, :]
                        DVE.wait(last)
                        nc.vector.tensor_tensor(out=ta, in0=x1, in1=cs, op=ALU.mult)
                        nc.vector.tensor_tensor(out=tbb, in0=x2, in1=sn, op=ALU.mult)
                        nc.vector.tensor_tensor(out=tcc, in0=x2, in1=cs, op=ALU.mult)
                        mm = DVE.mark(nc.vector.tensor_tensor(out=tdd, in0=x1, in1=sn, op=ALU.mult))
                        DVE.wait(mm)
                        nc.vector.tensor_tensor(out=orot.t[:, 0:nch_, hh, 0, :], in0=ta, in1=tbb, op=ALU.subtract)
                        last = DVE.mark(nc.vector.tensor_tensor(out=orot.t[:, 0:nch_, hh, 1, :], in0=tcc, in1=tdd,
                                                                op=ALU.add))
                    g.read(last)
                    ct.read(last); st.read(last)
                    orot.wrote(last)
                    tmp.wrote(last)
                    kf = kfr.next(); kb = kbr.next()
                    kf.begin(); kb.begin()
                    POOL.wait(orot.w, kf.prev, kb.prev, TBM)
                    for hh in range(2):
                        h = 2 * gi + hh
                        src_ = orot.t[:, 0:nch_, hh, :, :]
                        nc.gpsimd.tensor_scalar(out=kf.t.rearrange("p c (a b f) -> p c a b f", a=2, b=2)[:, 0:nch_, hh, :, :],
                                                in0=src_, scalar1=tb["dkf"][:, h:h + 1], scalar2=None, op0=ALU.mult)
                        mk = POOL.mark(nc.gpsimd.tensor_scalar(
                            out=kb.t.rearrange("p c (a b f) -> p c a b f", a=2, b=2)[:, 0:nch_, hh, :, :],
                            in0=src_, scalar1=tb["dkb"][:, h:h + 1], scalar2=None, op0=ALU.mult))
                    orot.read(mk)
                    kf.wrote(mk); kb.wrote(mk)
                    store(kf, KFd[c0:c0 + N, gi * 512:(gi + 1) * 512].rearrange("(c p) f -> p c f", p=128),
                          kf.t[:, 0:nch_, :], sd["kf"])
                    store(kb, KBd[c0:c0 + N, gi * 512:(gi + 1) * 512].rearrange("(c p) f -> p c f", p=128),
                          kb.t[:, 0:nch_, :], sd["kb"])

                linear_tok(hn, nchunk, WB, 12288, [[(2048 + 512 * gi, 512)] for gi in range(4)], ev_ktok, ("in", j))

                def mk_ev_fm(ob):
                    def ev(gi, g, nm):
                        tmp.begin()
                        DVE.wait(g.w, tmp.prev, cf.w, sf.w, ob.prev)
                        last = None
                        for hh in range(2):
                            x1 = g.t[:, 2 * hh, 0:N]
                            x2 = g.t[:, 2 * hh + 1, 0:N]
                            DVE.wait(last)
                            nc.vector.tensor_tensor(out=ra["ta"][:, 0:N], in0=x1, in1=cf.t[:, 0:N], op=ALU.mult)
                            nc.vector.tensor_tensor(out=ra["tb"][:, 0:N], in0=x2, in1=sf.t[:, 0:N], op=ALU.mult)
                            nc.vector.tensor_tensor(out=ra["tc"][:, 0:N], in0=x2, in1=cf.t[:, 0:N], op=ALU.mult)
                            mm = DVE.mark(nc.vector.tensor_tensor(out=ra["td"][:, 0:N], in0=x1, in1=sf.t[:, 0:N],
                                                                  op=ALU.mult))
                            DVE.wait(mm)
                            kc1 = gi * 4 + 2 * hh
                            nc.vector.tensor_tensor(out=ob.t[:, 0:nchunk, kc1, :],
                                                    in0=ra["ta"][:, 0:N].rearrange("p (c f) -> p c f", f=128),
                                                    in1=ra["tb"][:, 0:N].rearrange("p (c f) -> p c f", f=128),
                                                    op=ALU.subtract)
                            last = DVE.mark(nc.vector.tensor_tensor(
                                out=ob.t[:, 0:nchunk, kc1 + 1, :],
                                in0=ra["tc"][:, 0:N].rearrange("p (c f) -> p c f", f=128),
                                in1=ra["td"][:, 0:N].rearrange("p (c f) -> p c f", f=128), op=ALU.add))
                        g.read(last)
                        tmp.wrote(last)
                        ob.wrote(last)
                    return ev

                qo.begin(); ko.begin()
                linear_fm(lambda kc: hn.t[:, kc, 0:N], hn, KC, WB, 12288,
                          [[(512 * gi, 512)] for gi in range(4)], N, mk_ev_fm(qo), ("in", j))
                linear_fm(lambda kc: hn.t[:, kc, 0:N], hn, KC, WB, 12288,
                          [[(2048 + 512 * gi, 512)] for gi in range(4)], N, mk_ev_fm(ko), ("in", j))
                cf.read(DVE.mark(nc.vector.engine_nop())) if False else None
                for b_ in (cf, sf):
                    b_.read((DVE, DVE.n))
                store(qo, QTd[cb:cb + nchunk].rearrange("c p (k f) -> p c k f", k=16), qo.t[:, 0:nchunk, :, :], sd["q"])
                store(ko, KTd[cb:cb + nchunk].rearrange("c p (k f) -> p c k f", k=16), ko.t[:, 0:nchunk, :, :], sd["k"])
            phase_sync_all()

            rb = carve2(full=True, spec=[("S", [16, 512], F32), ("so", [16, 512], BF16), ("kb0", [2048], BF16), ("kb1", [2048], BF16),
                         ("v0", [4096], BF16), ("v1", [4096], BF16)])
            S = Buf(rb["S"]); so = Buf(rb["so"])
            kbl = Ring([Buf(rb["kb0"], newds()), Buf(rb["kb1"], newds())])
            vl = Ring([Buf(rb["v0"], newds()), Buf(rb["v1"], newds())])
            so_sds = newds()
            S.begin()
            DVE.wait(S.prev)
            S.wrote(DVE.mark(nc.vector.memset(S.t[:, :, :], 0.0)))

            def state_update(S, kx, vx, cdk, c):
                for hg in range(4):
                    g = pg.next()
                    g.begin()
                    PE.wait(kx.w, vx.w, g.prev)
                    for hl in range(2):
                        h = 2 * hg + hl
                        for dcc in range(2):
                            ins = nc.tensor.matmul(g.t[:, 2 * hl + dcc, :],
                                                   lhsT=kx.t[:, h * 256 + dcc * 128:h * 256 + (dcc + 1) * 128],
                                                   rhs=vx.t[:, h * 512:(h + 1) * 512], start=True, stop=True)
                    pm = PE.mark(ins)
                    g.wrote(pm)
                    kx.read(pm); vx.read(pm)
                    DVE.wait(pm, S.w, S.r)
                    for hl in range(2):
                        h = 2 * hg + hl
                        ins = nc.vector.scalar_tensor_tensor(out=S.t[:, 2 * h:2 * h + 2, :], in0=S.t[:, 2 * h:2 * h + 2, :],
                                                             scalar=cdk[:, c, h:h + 1],
                                                             in1=g.t[:, 2 * hl:2 * hl + 2, :], op0=ALU.mult, op1=ALU.add)
                    m = DVE.mark(ins)
                    g.read(m)
                    S.wrote(m)

            for c in range(NCH - 1, -1, -1):
                kx = kbl.next(); vx = vl.next()
                kx.begin(); vx.begin()
                load(kx, kx.t[:, :], KBd[c * 128:(c + 1) * 128, :])
                load(vx, vx.t[:, :], Vd[c * 128:(c + 1) * 128, :])
                so.begin()
                ACT.wait(S.w, so.prev)
                for q4 in range(4):
                    ins = nc.scalar.activation(out=so.t[:, 4 * q4:4 * q4 + 4, :], in_=S.t[:, 4 * q4:4 * q4 + 4, :],
                                               func=AF.Copy, scale=kp.t[:, NCH + 1 + c:NCH + 2 + c])
                m = ACT.mark(ins)
                so.wrote(m)
                S.read(m)
                store(so, SBd[c], so.t.rearrange("p a b -> p (a b)"), so_sds)
                state_update(S, kx, vx, tb["cdkb"], c)
            phase_sync_all()

            rc = carve2(full=True, spec=[("S", [16, 512], F32), ("sfb", [16, 512], BF16), ("sbt", [16, 512], BF16),
                         ("qt0", [16, 128], BF16), ("qt1", [16, 128], BF16), ("kt0", [16, 128], BF16),
                         ("kt1", [16, 128], BF16), ("kf0", [2048], BF16), ("kf1", [2048], BF16),
                         ("v0", [4096], BF16), ("g0", [4096], BF16),
                         ("qf", [16, 128], BF16), ("qb", [16, 128], BF16), ("sc", [8, 128], BF16),
                         ("on", [4, 512], F32), ("gated", [4096], BF16), ("gT", [32, 128], BF16),
                         ("stt", [8, 6], F32), ("mv", [8, 2], F32), ("rs", [8], F32), ("nb", [8], F32)])
            S = Buf(rc["S"]); sfb = Buf(rc["sfb"]); sbt = Buf(rc["sbt"], newds())
            qtl = Ring([Buf(rc["qt0"], newds()), Buf(rc["qt1"], newds())])
            ktl = Ring([Buf(rc["kt0"], newds()), Buf(rc["kt1"], newds())])
            kfl = Ring([Buf(rc["kf0"], newds()), Buf(rc["kf1"], newds())])
            vl = Ring([Buf(rc["v0"], newds())])
            gl = Ring([Buf(rc["g0"], newds())])
            qf = Buf(rc["qf"]); qb = Buf(rc["qb"]); sc = Buf(rc["sc"]); on = Buf(rc["on"])
            gated = Buf(rc["gated"]); gT = Buf(rc["gT"]); stt = Buf(rc["stt"])
            gt_sds = newds()
            S.begin(); sfb.begin()
            DVE.wait(S.prev, sfb.prev)
            S.wrote(DVE.mark(nc.vector.memset(S.t[:, :, :], 0.0)))
            sfb.wrote(DVE.mark(nc.vector.memset(sfb.t[:, :, :], 0.0)))
            for c in range(NCH):
                qt = qtl.next(); kt = ktl.next(); kx = kfl.next(); vx = vl.next(); gx = gl.next()
                for b_, src_ in ((qt, QTd[c]), (kt, KTd[c])):
                    b_.begin()
                    load(b_, b_.t.rearrange("p a b -> p (a b)"), src_)
                kx.begin(); load(kx, kx.t[:, :], KFd[c * 128:(c + 1) * 128, :])
                vx.begin(); load(vx, vx.t[:, :], Vd[c * 128:(c + 1) * 128, :])
                gx.begin(); load(gx, gx.t[:, :], Gd[c * 128:(c + 1) * 128, :])
                sbt.begin(); load(sbt, sbt.t.rearrange("p a b -> p (a b)"), SBd[c])
                qf.begin(); qb.begin()
                POOL.wait(qt.w, qf.prev, qb.prev)
                nc.gpsimd.tensor_tensor(out=qf.t[:, :, :], in0=qt.t[:, :, :], in1=tb["fq"][:, :, :], op=ALU.mult)
                m = POOL.mark(nc.gpsimd.tensor_tensor(out=qb.t[:, :, :], in0=qt.t[:, :, :], in1=tb["fb"][:, :, :],
                                                      op=ALU.mult))
                qf.wrote(m); qb.wrote(m); qt.read(m)
                g = pg.next(); g.begin()
                PE.wait(qt.w, kt.w, g.prev)
                for h in range(H):
                    for dcc in range(2):
                        ins = nc.tensor.matmul(g.t[:, h // 4, (h % 4) * 128:(h % 4 + 1) * 128],
                                               lhsT=kt.t[:, 2 * h + dcc, :], rhs=qt.t[:, 2 * h + dcc, :],
                                               start=(dcc == 0), stop=(dcc == 1))
                pm = PE.mark(ins)
                g.wrote(pm); qt.read(pm); kt.read(pm)
                sc.begin()
                DVE.wait(pm, sc.prev)
                for half in range(2):
                    ins = nc.vector.tensor_tensor(out=sc.t[:, 4 * half:4 * half + 4, :],
                                                  in0=g.t[:, half, :].rearrange("p (a b) -> p a b", a=4),
                                                  in1=tb["dmt"][:, 4 * half:4 * half + 4, :], op=ALU.mult)
                m = DVE.mark(ins)
                g.read(m); sc.wrote(m)
                gated.begin()
                for hg in range(2):
                    g = pg.next(); g.begin()
                    PE.wait(sc.w, vx.w, qf.w, qb.w, sfb.w, sbt.w, g.prev)
                    for hl in range(4):
                        h = 4 * hg + hl
                        nc.tensor.matmul(g.t[:, hl, :], lhsT=sc.t[:, h, :], rhs=vx.t[:, h * 512:(h + 1) * 512],
                                         start=True, stop=False)
                        for dcc in range(2):
                            nc.tensor.matmul(g.t[:, hl, :], lhsT=qf.t[:, 2 * h + dcc, :], rhs=sfb.t[:, 2 * h + dcc, :],
                                             start=False, stop=False)
                        for dcc in range(2):
                            ins = nc.tensor.matmul(g.t[:, hl, :], lhsT=qb.t[:, 2 * h + dcc, :],
                                                   rhs=sbt.t[:, 2 * h + dcc, :], start=False, stop=(dcc == 1))
                    pm = PE.mark(ins)
                    g.wrote(pm)
                    for b_ in (sc, vx, qf, qb, sfb, sbt):
                        b_.read(pm)
                    stt.begin()
                    DVE.wait(pm, stt.prev)
                    for hl in range(4):
                        ins = nc.vector.bn_stats(out=rc["stt"][:, hl, :], in_=g.t[:, hl, :])
                    m = DVE.mark(ins)
                    DVE.wait(m)
                    for hl in range(4):
                        ins = nc.vector.bn_aggr(out=rc["mv"][:, hl, :], in_=rc["stt"][:, hl, :])
                    m = DVE.mark(ins)
                    DVE.wait(m)
                    ACT.wait(m)
                    m = ACT.mark(nc.scalar.activation(out=rc["rs"][:, 0:4], in_=rc["mv"][:, 0:4, 1], func=AF.Sqrt,
                                                      bias=EPS))
                    DVE.wait(m)
                    nc.vector.reciprocal(out=rc["rs"][:, 0:4], in_=rc["rs"][:, 0:4])
                    m = DVE.mark(nc.vector.tensor_copy(out=rc["rs"][:, 4:8], in_=rc["mv"][:, 0:4, 0]))
                    DVE.wait(m)
                    m = DVE.mark(nc.vector.scalar_tensor_tensor(out=rc["nb"][:, 0:4], in0=rc["rs"][:, 4:8], scalar=-1.0,
                                                                in1=rc["rs"][:, 0:4], op0=ALU.mult, op1=ALU.mult))
                    stt.wrote(m)
                    on.begin()
                    ACT.wait(m, on.prev)
                    for hl in range(4):
                        ins = nc.scalar.activation(out=on.t[:, hl, :], in_=g.t[:, hl, :], func=AF.Identity,
                                                   scale=rc["rs"][:, hl:hl + 1], bias=rc["nb"][:, hl:hl + 1])
                    m = ACT.mark(ins)
                    g.read(m); stt.read(m); on.wrote(m)
                    POOL.wait(m, gx.w, gated.prev)
                    m = POOL.mark(nc.gpsimd.tensor_tensor(
                        out=gated.t[:, hg * 2048:(hg + 1) * 2048].rearrange("p (a b) -> p a b", a=4),
                        in0=on.t[:, :, :], in1=gx.t[:, hg * 2048:(hg + 1) * 2048].rearrange("p (a b) -> p a b", a=4),
                        op=ALU.mult))
                    on.read(m); gx.read(m); gated.wrote(m)
                state_update(S, kx, vx, tb["cdkf"], c)
                sfb.begin()
                ACT.wait(S.w, sfb.prev)
                for q4 in range(4):
                    ins = nc.scalar.activation(out=sfb.t[:, 4 * q4:4 * q4 + 4, :], in_=S.t[:, 4 * q4:4 * q4 + 4, :],
                                               func=AF.Copy, scale=kp.t[:, c + 1:c + 2])
                m = ACT.mark(ins)
                sfb.wrote(m); S.read(m)
                g = pg.next(); g.begin()
                gb = g.t.rearrange("p a b -> p (a b)").bitcast(BF16).rearrange("p (a b) -> p a b", a=32)
                PE.wait(gated.w, g.prev, ident.w)
                for ec in range(32):
                    ins = nc.tensor.transpose(out=gb[:, ec, 0:128], in_=gated.t[:, ec * 128:(ec + 1) * 128],
                                              identity=ident.t[:, :])
                pm = PE.mark(ins)
                g.wrote(pm); gated.read(pm)
                gT.begin()
                ACT.wait(pm, gT.prev)
                DVE.wait(pm, gT.prev)
                m1 = ACT.mark(nc.scalar.activation(out=gT.t[:, 0:16, :], in_=gb[:, 0:16, 0:128], func=AF.Copy))
                m2 = DVE.mark(nc.vector.tensor_copy(out=gT.t[:, 16:32, :], in_=gb[:, 16:32, 0:128]))
                g.read(m1); g.read(m2); gT.wrote(m1); gT.wrote(m2)
                store(gT, GTd[c], gT.t.rearrange("p a b -> p (a b)"), gt_sds)
            phase_sync_all()

            rd = carve2([("gt", [4, 32, 128], BF16)])
            gtb = Buf(rd["gt"], newds())
            for (c0, N) in blocks:
                nchunk = N // 128
                cb = c0 // 128
                pump(2)
                gtb.begin()
                for ci in range(nchunk):
                    load(gtb, gtb.t[:, ci, :, :].rearrange("p e f -> p (e f)"), GTd[cb + ci])
                xm.begin()

                def ev_o(gi, g, nm):
                    ACT.wait(g.w, xm.prev)
                    for m_ in range(nm):
                        ins = nc.scalar.activation(out=xm.t[:, gi * 4 + m_, 0:N], in_=g.t[:, m_, 0:N], func=AF.Copy)
                    mk = ACT.mark(ins)
                    g.read(mk); xm.wrote(mk)

                linear_fm(lambda kc: gtb.t[:, 0:nchunk, kc, :], gtb, 32, WB_OUT[j], D,
                          [[(512 * gi, 512)] for gi in range(4)], N, ev_o, ("out", j))
                back(N, c0, "g_post_mix", li, hin, hout, False)
            phase_sync_all()

        def conformer_layer(j, li, hin, hout):
            HALO = 15
            cw = carve([("u", [KC, 512], F32), ("sig", [4, 512], F32)])
            ub = Buf(cw["u"]); sig = Buf(cw["sig"])
            u_sds = newds()
            for (c0, N) in blocks:
                pump(2)
                front(hin, c0, N, "g_pre_mix", li)
                ub.begin()
                gate_g = {}

                def ev_glu(gi, g, nm):
                    sig.begin()
                    ACT.wait(g.w, sig.prev)
                    for m_ in range(2):
                        kc = gi * 2 + m_
                        ins = nc.scalar.activation(out=sig.t[:, m_, 0:N], in_=g.t[:, 2 + m_, 0:N], func=AF.Sigmoid,
                                                   bias=P("b_pw1", j, KC + kc))
                    mk = ACT.mark(ins)
                    sig.wrote(mk)
                    DVE.wait(mk, ub.prev)
                    for m_ in range(2):
                        kc = gi * 2 + m_
                        ins = nc.vector.scalar_tensor_tensor(out=ub.t[:, kc, 0:N], in0=g.t[:, m_, 0:N],
                                                             scalar=P("b_pw1", j, kc), in1=sig.t[:, m_, 0:N],
                                                             op0=ALU.add, op1=ALU.mult)
                    mk2 = DVE.mark(ins)
                    for si, scn in enumerate(SPECIAL):
                        lo = scn * 128 - c0
                        if 0 <= lo < N:
                            DVE.wait(mk2, msk.w)
                            mk2 = DVE.mark(nc.vector.tensor_tensor(
                                out=ub.t[:, gi * 2:gi * 2 + 2, lo:lo + 128], in0=ub.t[:, gi * 2:gi * 2 + 2, lo:lo + 128],
                                in1=msk.t[:, si * 128:(si + 1) * 128].rearrange("p (a f) -> p a f", a=1).broadcast_to([128, 2, 128])
                                if False else msk.t[:, si * 128:(si + 1) * 128], op=ALU.mult)) if False else mk2
                            for m_ in range(2):
                                kc = gi * 2 + m_
                                mk2 = DVE.mark(nc.vector.tensor_tensor(out=ub.t[:, kc, lo:lo + 128],
                                                                       in0=ub.t[:, kc, lo:lo + 128],
                                                                       in1=msk.t[:, si * 128:(si + 1) * 128], op=ALU.mult))
                                DVE.wait(mk2)
                    g.read(mk2); sig.read(mk2); ub.wrote(mk2)

                linear_fm(lambda kc: hn.t[:, kc, 0:N], hn, KC, WB_PW1[j], 2 * D,
                          [[(256 * gi, 256), (D + 256 * gi, 256)] for gi in range(8)], N, ev_glu, ("pw1", j))
                store(ub, UT.rearrange("(kc p) t -> p kc t", p=128)[:, :, c0:c0 + N], ub.t[:, :, 0:N], u_sds)
            phase_sync_all()

            W_ = 512 + 2 * HALO
            cw = carve([("uh", [KC, W_], F32), ("mu", [512], F32), ("ex2", [512], F32), ("xh", [KC, 512], BF16),
                        ("t", [512], F32)])
            uh = Buf(cw["uh"], newds()); mu = Buf(cw["mu"]); xh = Buf(cw["xh"]); tt = Buf(cw["t"])
            y = xm
            UTv = UT.rearrange("(kc p) t -> p kc t", p=128)
            POOL_KC = ()
            for (c0, N) in blocks:
                pump(2)
                lo = max(c0 - HALO, 0)
                hi = min(c0 + N + HALO, T)
                uh.begin()
                o0 = lo - (c0 - HALO)
                if o0 > 0:
                    DVE.wait(uh.prev)
                    uh.wrote(DVE.mark(nc.vector.memset(uh.t[:, :, 0:o0], 0.0)))
                if hi < c0 + N + HALO:
                    DVE.wait(uh.prev)
                    uh.wrote(DVE.mark(nc.vector.memset(uh.t[:, :, o0 + hi - lo:N + 2 * HALO], 0.0)))
                load(uh, uh.t[:, :, o0:o0 + hi - lo], UTv[:, :, lo:hi])
                y.begin()
                marks = []
                for kc in range(KC):
                    q, eng = (POOL, nc.gpsimd) if kc in POOL_KC else (DVE, nc.vector)
                    q.wait(uh.w, y.prev, prm.w)
                    wcol = pc[("w_dw", j)] + kc * CK
                    mk = q.mark(eng.tensor_scalar(out=y.t[:, kc, 0:N], in0=uh.t[:, kc, 0:N],
                                                  scalar1=prm.t[:, wcol:wcol + 1], scalar2=P("b_dw", j, kc),
                                                  op0=ALU.mult, op1=ALU.add))
                    for k in range(1, CK):
                        q.wait(mk)
                        mk = q.mark(eng.scalar_tensor_tensor(out=y.t[:, kc, 0:N], in0=uh.t[:, kc, k:k + N],
                                                             scalar=prm.t[:, wcol + k:wcol + k + 1], in1=y.t[:, kc, 0:N],
                                                             op0=ALU.mult, op1=ALU.add))
                    marks.append(mk)
                for mk in marks:
                    y.wrote(mk); uh.read(mk)
                hn.begin()
                xh.begin()
                ACT.wait(y.w, hn.prev, xh.prev)
                for kc in range(KC):
                    nc.scalar.activation(out=hn.t[:, kc, 0:N], in_=y.t[:, kc, 0:N], func=AF.Square)
                    ins = nc.scalar.activation(out=xh.t[:, kc, 0:N], in_=y.t[:, kc, 0:N], func=AF.Copy)
                mk = ACT.mark(ins)
                hn.wrote(mk); xh.wrote(mk)
                g = pg.next(); g.begin()
                PE.wait(mk, g.prev)
                for kc in range(KC):
                    nc.tensor.matmul(g.t[:, 0, 0:N], lhsT=ones.t[:], rhs=xh.t[:, kc, 0:N], start=(kc == 0),
                                     stop=(kc == KC - 1))
                for kc in range(KC):
                    ins = nc.tensor.matmul(g.t[:, 1, 0:N], lhsT=ones.t[:], rhs=hn.t[:, kc, 0:N], start=(kc == 0),
                                           stop=(kc == KC - 1))
                pm = PE.mark(ins)
                hn.read(pm); xh.read(pm); g.wrote(pm)
                mu.begin(); rstd.begin(); tt.begin()
                DVE.wait(pm, mu.prev, rstd.prev, tt.prev)
                nc.vector.tensor_scalar(out=mu.t[:, 0:N], in0=g.t[:, 0, 0:N], scalar1=1.0 / D, scalar2=None, op0=ALU.mult)
                m1 = DVE.mark(nc.vector.tensor_scalar(out=cw["ex2"][:, 0:N], in0=g.t[:, 1, 0:N], scalar1=1.0 / D,
                                                      scalar2=EPS, op0=ALU.mult, op1=ALU.add))
                DVE.wait(m1)
                m1 = DVE.mark(nc.vector.tensor_tensor(out=tt.t[:, 0:N], in0=mu.t[:, 0:N], in1=mu.t[:, 0:N], op=ALU.mult))
                DVE.wait(m1)
                m1 = DVE.mark(nc.vector.tensor_tensor(out=rstd.t[:, 0:N], in0=cw["ex2"][:, 0:N], in1=tt.t[:, 0:N],
                                                      op=ALU.subtract))
                ACT.wait(m1)
                m1 = ACT.mark(nc.scalar.activation(out=rstd.t[:, 0:N], in_=rstd.t[:, 0:N], func=AF.Sqrt))
                DVE.wait(m1)
                m1 = DVE.mark(nc.vector.reciprocal(out=rstd.t[:, 0:N], in_=rstd.t[:, 0:N]))
                g.read(m1); mu.wrote(m1); rstd.wrote(m1)
                xh.begin()
                lastd = m1
                for kc in range(KC):
                    DVE.wait(lastd, y.w)
                    ma = DVE.mark(nc.vector.tensor_tensor(out=y.t[:, kc, 0:N], in0=y.t[:, kc, 0:N], in1=mu.t[:, 0:N],
                                                          op=ALU.subtract))
                    DVE.wait(ma)
                    lastd = DVE.mark(nc.vector.tensor_tensor(out=y.t[:, kc, 0:N], in0=y.t[:, kc, 0:N],
                                                             in1=rstd.t[:, 0:N], op=ALU.mult))
                    ACT.wait(lastd, xh.prev)
                    mk = ACT.mark(nc.scalar.activation(out=xh.t[:, kc, 0:N], in_=y.t[:, kc, 0:N], func=AF.Silu,
                                                       scale=P("ln_g", j, kc), bias=P("ln_b", j, kc)))
                xh.wrote(mk); y.read(mk); mu.read(lastd); rstd.read(lastd)
                xm.begin()

                def ev_p2(gi, g, nm):
                    ACT.wait(g.w, xm.prev)
                    for m_ in range(nm):
                        kc = gi * 4 + m_
                        ins = nc.scalar.activation(out=xm.t[:, kc, 0:N], in_=g.t[:, m_, 0:N], func=AF.Identity,
                                                   bias=P("b_pw2", j, kc))
                    mk_ = ACT.mark(ins)
                    g.read(mk_); xm.wrote(mk_)

                linear_fm(lambda kc: xh.t[:, kc, 0:N], xh, KC, WB_PW2[j], D,
                          [[(512 * gi, 512)] for gi in range(4)], N, ev_p2, ("pw2", j))
                back(N, c0, "g_post_mix", li, hin, hout, True)
            phase_sync_all()

        def ffn_layer(li, hin, hout):
            fw = carve([("act", [FC, 512], BF16), ("a0", [2, 514], F32), ("a1", [2, 514], F32),
                        ("y", [2, 512], F32), ("z", [2, 512], F32), ("s", [2, 512], F32)])
            act = Buf(fw["act"])
            ar = Ring([Buf(fw["a0"]), Buf(fw["a1"])])
            yb = Buf(fw["y"]); zb = Buf(fw["z"]); sbf = Buf(fw["s"])
            fblocks = [(s, min(s + 510, T)) for s in range(0, T, 510)]
            for (s0, e0) in fblocks:
                pump(2)
                lo = max(s0 - 1, 0)
                hi = min(e0 + 1, T)
                N = hi - lo
                NO = e0 - s0
                o0 = lo - (s0 - 1)
                front(hin, lo, N, "g_pre_ffn", li)
                act.begin()

                def ev_up(gi, g, nm):
                    nj = nm // 2
                    ab = ar.next()
                    ab.begin()
                    ACT.wait(g.w, ab.prev)
                    if o0 > 0:
                        nc.scalar.activation(out=ab.t[:, 0:nj, 0:1], in_=ab.t[:, 0:nj, 0:1], func=AF.Copy, scale=0.0)
                    if o0 + N < NO + 2:
                        nc.scalar.activation(out=ab.t[:, 0:nj, o0 + N:NO + 2], in_=ab.t[:, 0:nj, o0 + N:NO + 2],
                                             func=AF.Copy, scale=0.0)
                    mk = ACT.mark(nc.scalar.activation(out=ab.t[:, 0:nj, o0:o0 + N], in_=g.t[:, 0:nj, 0:N], func=AF.Copy))
                    ab.wrote(mk)
                    yb.begin(); zb.begin(); sbf.begin()
                    DVE.wait(mk, yb.prev, zb.prev, sbf.prev, act.prev)
                    wl = pc[("ffn_w_dw", li)]
                    last = None
                    for jj in range(nj):
                        jg = gi * 2 + jj
                        DVE.wait(last)
                        m0 = DVE.mark(nc.vector.tensor_scalar(out=yb.t[:, jj, 0:NO], in0=ab.t[:, jj, 0:NO],
                                                              scalar1=prm.t[:, wl + jg * 3:wl + jg * 3 + 1],
                                                              scalar2=P("ffn_b_dw", li, jg), op0=ALU.mult, op1=ALU.add))
                        DVE.wait(m0)
                        m0 = DVE.mark(nc.vector.scalar_tensor_tensor(out=yb.t[:, jj, 0:NO], in0=ab.t[:, jj, 1:NO + 1],
                                                                     scalar=prm.t[:, wl + jg * 3 + 1:wl + jg * 3 + 2],
                                                                     in1=yb.t[:, jj, 0:NO], op0=ALU.mult, op1=ALU.add))
                        DVE.wait(m0)
                        last = DVE.mark(nc.vector.scalar_tensor_tensor(out=yb.t[:, jj, 0:NO], in0=ab.t[:, jj, 2:NO + 2],
                                                                       scalar=prm.t[:, wl + jg * 3 + 2:wl + jg * 3 + 3],
                                                                       in1=yb.t[:, jj, 0:NO], op0=ALU.mult, op1=ALU.add))
                    ab.read(last)
                    ACT.wait(last)
                    m3 = ACT.mark(nc.scalar.activation(out=yb.t[:, 0:nj, 0:NO], in_=yb.t[:, 0:nj, 0:NO],
                                                       func=AF.Gelu_apprx_tanh))
                    DVE.wait(m3)
                    vo = s0 - lo
                    m4 = DVE.mark(nc.vector.tensor_tensor(out=act.t[:, gi * 2:gi * 2 + nj, 0:NO], in0=yb.t[:, 0:nj, 0:NO],
                                                          in1=g.t[:, nj:2 * nj, vo:vo + NO], op=ALU.mult))
                    g.read(m4); yb.wrote(m4); zb.wrote(m4); sbf.wrote(m4); act.wrote(m4)

                groups = []
                for gi in range((FC + 1) // 2):
                    nj = min(2, FC - 2 * gi)
                    groups.append([(256 * gi, 128 * nj), (FH + 256 * gi, 128 * nj)])
                linear_fm(lambda kc: hn.t[:, kc, 0:N], hn, KC, WB_UP[li], 2 * FH, groups, N, ev_up, ("up", li))
                xm.begin()

                def ev_dn(gi, g, nm):
                    ACT.wait(g.w, xm.prev)
                    for m_ in range(nm):
                        ins = nc.scalar.activation(out=xm.t[:, gi * 4 + m_, 0:NO], in_=g.t[:, m_, 0:NO], func=AF.Copy)
                    mk_ = ACT.mark(ins)
                    g.read(mk_); xm.wrote(mk_)

                linear_fm(lambda kc: act.t[:, kc, 0:NO], act, FC, WB_DN[li], D,
                          [[(512 * gi, 512)] for gi in range(4)], NO, ev_dn, ("dn", li))
                back(NO, s0, "g_post_ffn", li, hin, hout, False)
            phase_sync_all()

        for q in (ACT, DVE, POOL, PE):
            q.wait(prm.w, dec.w, kp.w, msk.w, ident.w, ones.w)
        cur = XT
        for li in range(DEPTH):
            j = li // 2
            mid = HB
            if li % 2 == 0:
                retention_layer(j, li, cur, mid)
            else:
                conformer_layer(j, li, cur, mid)
            nxt = YT if li == DEPTH - 1 else HA
            ffn_layer(li, mid, nxt)
            cur = nxt
        pump(10 ** 6)
        for q in (POOL,):
            q.wait(store_marks)
    return nc, pc, NP, SPECIAL, NCH, T


def _fm(vec):
    return np.ascontiguousarray(vec.reshape(-1, 128).T)


def make_core_inputs(seqs, meta, SEGC, SPECIAL, NCH, T):
    xt = np.zeros((T, D), np.float32)
    keep = np.ones(NCH + 1, np.float32)
    valid = np.zeros(T, np.float32)
    pos = np.zeros(T, np.float64)
    places = []
    c = 0
    for s in seqs:
        L = s.shape[0]
        ncs = L // 128
        keep[c] = 0.0
        r0 = c * 128 + 128 - NMETA
        xt[r0:r0 + NMETA] = meta
        xt[r0 + NMETA:r0 + NMETA + L] = s
        valid[r0:r0 + NMETA + L] = 1.0
        pos[c * 128:(c + 1 + ncs) * 128] = np.arange((1 + ncs) * 128)
        places.append((r0 + NMETA, L))
        c += 1 + ncs
    while c < NCH:
        keep[c] = 0.0
        c += 1
    keep[NCH] = 0.0
    keepb = np.concatenate([keep[1:NCH + 1], [0.0]]).astype(np.float32)
    kpv = np.concatenate([keep, keepb]).astype(np.float32)
    kp = np.ascontiguousarray(np.broadcast_to(kpv[None, :], (128, kpv.size))).astype(np.float32)
    mskv = np.concatenate([valid[sc * 128:(sc + 1) * 128] for sc in SPECIAL])
    msk = np.ascontiguousarray(np.broadcast_to(mskv[None, :], (128, mskv.size))).astype(np.float32)
    inv = 10000.0 ** (-np.arange(128, dtype=np.float32) / np.float32(128))
    ang = pos.astype(np.float32)[:, None] * inv[None, :].astype(np.float32)
    cost = np.cos(ang.astype(np.float64)).astype(np.float32)
    sint = np.sin(ang.astype(np.float64)).astype(np.float32)
    return dict(xt=np.ascontiguousarray(xt.T), kp=kp, msk=msk, cost=cost, sint=sint,
                cosf=np.ascontiguousarray(cost.T), sinf=np.ascontiguousarray(sint.T)), places


def pack_params(inp, pc, NP, DEPTH):
    prm = np.zeros((128, NP), np.float32)
    for (nm, idx), o in pc.items():
        if nm == "g_pre_mix":
            a = _fm(inp["norm_pre_mix"][idx])
        elif nm == "g_post_mix":
            a = _fm(inp["norm_post_mix"][idx])
        elif nm == "g_pre_ffn":
            a = _fm(inp["norm_pre_ffn"][idx])
        elif nm == "g_post_ffn":
            a = _fm(inp["norm_post_ffn"][idx])
        elif nm == "ffn_w_dw":
            w = inp["ffn_w_dw"][idx]
            a = np.ascontiguousarray(w.T.reshape(FC, 128, 3).transpose(1, 0, 2).reshape(128, FC * 3))
        elif nm == "ffn_b_dw":
            a = _fm(inp["ffn_b_dw"][idx])
        elif nm == "b_pw1":
            a = _fm(inp["conv_b_pw1"][idx])
        elif nm == "w_dw":
            w = inp["conv_w_dw"][idx]
            a = np.ascontiguousarray(w.T.reshape(KC, 128, CK).transpose(1, 0, 2).reshape(128, KC * CK))
        elif nm == "b_dw":
            a = _fm(inp["conv_b_dw"][idx])
        elif nm == "ln_g":
            a = _fm(inp["conv_ln_g"][idx])
        elif nm == "ln_b":
            a = _fm(inp["conv_ln_b"][idx])
        elif nm == "b_pw2":
            a = _fm(inp["conv_b_pw2"][idx])
        prm[:, o:o + a.shape[1]] = a
    return prm


_CACHE = {}


def run_model(inp, core_seqs, SEGC, DEPTH):
    key = (SEGC, DEPTH)
    if key not in _CACHE:
        _CACHE[key] = build_program(SEGC, DEPTH)
    nc, pc, NP, SPECIAL, NCH, T = _CACHE[key]
    NRET = (DEPTH + 1) // 2
    NCONV = DEPTH // 2
    prm = pack_params(inp, pc, NP, DEPTH)
    decv = np.zeros((NRET * 16,), np.float32)
    for j in range(NRET):
        decv[j * 16:j * 16 + 8] = inp["ret_decay_fwd"][j]
        decv[j * 16 + 8:j * 16 + 16] = inp["ret_decay_bwd"][j]
    dec = np.ascontiguousarray(np.broadcast_to(decv[None, :], (128, decv.size))).astype(np.float32)

    def flat(a, n):
        a = np.ascontiguousarray(a[:n]) if n > 0 else np.zeros((1,) + a.shape[1:], np.float32)
        return a.reshape(-1, 2048)

    shared = dict(prm=prm, dec=dec,
                  ret_w_in=flat(inp["ret_w_in"], NRET), ret_w_out=flat(inp["ret_w_out"], NRET),
                  conv_w_pw1=flat(inp["conv_w_pw1"], NCONV), conv_w_pw2=flat(inp["conv_w_pw2"], NCONV),
                  ffn_w_up=flat(inp["ffn_w_up"], DEPTH), ffn_w_down=flat(inp["ffn_w_down"], DEPTH))
    in_maps = []
    places = []
    for seqs in core_seqs:
        d, pl = make_core_inputs(seqs, inp["meta_tokens"], SEGC, SPECIAL, NCH, T)
        d.update(shared)
        in_maps.append(d)
        places.append(pl)
    res = run_bass_kernel_spmd(nc, in_maps, core_ids=list(range(len(core_seqs))))
    outs = []
    for ci, pl in enumerate(places):
        yt = res.results[ci]["yt"]
        y = yt.T
        outs.append([np.ascontiguousarray(y[r0:r0 + L]) for (r0, L) in pl])
    return outs


def kernel(**inp):
    xp = np.asarray(inp["x_prompt"], np.float32)
    xs = np.asarray(inp["x_sample"], np.float32)
    SEGC = xp.shape[1] // 128
    DEPTH = 4
    core_seqs = []
    for c in range(4):
        core_seqs.append([xs[c], xp[c]])
    for c in range(4):
        core_seqs.append([xp[4 + 3 * c + k] for k in range(3)])
    outs = run_model(inp, core_seqs, SEGC, DEPTH)
    yp = np.zeros_like(xp)
    ys = np.zeros_like(xs)
    for c in range(4):
        ys[c] = outs[c][0]
        yp[c] = outs[c][1]
    for c in range(4):
        for k in range(3):
            yp[4 + 3 * c + k] = outs[4 + c][k]
    return (yp, ys)
```

```python
import numpy as np
from contextlib import ExitStack
import concourse.bass as bass
import concourse.mybir as mybir
from concourse.bass_utils import run_bass_kernel_spmd

F32 = mybir.dt.float32
BF16 = mybir.dt.bfloat16
I32 = mybir.dt.int32
AF = mybir.ActivationFunctionType
ALU = mybir.AluOpType

D = 2048
KC = 16
H = 8
HV = 4096
FH = 5504
FC = 43
CK = 31
NMETA = 16
EPS = 1e-6
LN16 = float(np.log(1.0 / 16.0))


_UID = [0]


def _next_uid():
    _UID[0] += 1
    return _UID[0]


class Q:
    def __init__(self, nc, eng, name, es, step=1):
        self.uid = _next_uid()
        self.e = eng
        self.sem = es.enter_context(nc.semaphore(name))
        self.n = 0
        self.step = step
        self.seen = {}

    def mark(self, ins):
        ins.then_inc(self.sem, self.step)
        self.n += self.step
        return (self, self.n)

    def wait(self, *marks):
        for m in marks:
            if m is None:
                continue
            if isinstance(m, dict):
                self.wait(*m.values())
                continue
            src, n = m
            if self.seen.get(src.uid, 0) >= n:
                continue
            self.e.wait_ge(src.sem, n)
            self.seen[src.uid] = n


class DS:
    def __init__(self, nc, name, es):
        self.uid = _next_uid()
        self.sem = es.enter_context(nc.semaphore(name))
        self.n = 0

    def mark(self, ins):
        ins.then_inc(self.sem, 16)
        self.n += 16
        return (self, self.n)


def _merge(d, m):
    if m is None:
        return
    src, n = m
    k = src.uid
    if k not in d or d[k][1] < n:
        d[k] = (src, n)


class Buf:
    def __init__(self, t, ds=None):
        self.t = t
        self.w = {}
        self.r = {}
        self.prev = {}
        self.ds = ds

    def begin(self):
        self.prev = {}
        for m in list(self.w.values()) + list(self.r.values()):
            _merge(self.prev, m)
        self.w = {}
        self.r = {}

    def wrote(self, m):
        _merge(self.w, m)

    def read(self, m):
        _merge(self.r, m)


class Ring:
    def __init__(self, bufs):
        self.b = bufs
        self.i = 0

    def next(self):
        b = self.b[self.i % len(self.b)]
        self.i += 1
        return b


def build_program(SEGC, DEPTH):
    NCH = 3 * (SEGC + 1)
    T = NCH * 128
    NRET = (DEPTH + 1) // 2
    NCONV = DEPTH // 2
    SPECIAL = sorted({0, 1 + SEGC, 2 + 2 * SEGC, 1 + 2 * SEGC, 2 + 3 * SEGC})
    pc = {}
    off = 0
    for i in range(DEPTH):
        for nm, w in (("g_pre_mix", KC), ("g_post_mix", KC), ("g_pre_ffn", KC), ("g_post_ffn", KC),
                      ("ffn_w_dw", FC * 3), ("ffn_b_dw", FC)):
            pc[(nm, i)] = off
            off += w
    for j in range(NCONV):
        for nm, w in (("b_pw1", 2 * KC), ("w_dw", KC * CK), ("b_dw", KC), ("ln_g", KC), ("ln_b", KC), ("b_pw2", KC)):
            pc[(nm, j)] = off
            off += w
    NP = off

    nc = bass.Bass("TRN2", target_bir_lowering=False)
    dt = nc.dram_tensor

    def din(name, shape, dtp=F32):
        return dt(name, shape, dtp, kind="ExternalInput").ap()

    def dsc(name, shape, dtp):
        return dt(name, shape, dtp, kind="Internal").ap()

    XT = din("xt", [D, T])
    PRM = din("prm", [128, NP])
    DEC = din("dec", [128, NRET * 16])
    KPD = din("kp", [128, 2 * NCH + 2])
    MSKD = din("msk", [128, len(SPECIAL) * 128])
    COSF = din("cosf", [128, T])
    SINF = din("sinf", [128, T])
    COST = din("cost", [T, 128])
    SINT = din("sint", [T, 128])
    W_IN = din("ret_w_in", [NRET * D * 12288 // 2048, 2048])
    W_OUT = din("ret_w_out", [NRET * HV * D // 2048, 2048])
    W_PW1 = din("conv_w_pw1", [max(NCONV, 1) * D * 2 * D // 2048, 2048])
    W_PW2 = din("conv_w_pw2", [max(NCONV, 1) * D * D // 2048, 2048])
    W_UP = din("ffn_w_up", [DEPTH * D * 2 * FH // 2048, 2048])
    W_DN = din("ffn_w_down", [DEPTH * FH * D // 2048, 2048])
    YT = dt("yt", [D, T], F32, kind="ExternalOutput").ap()

    WB_IN = [dsc(f"wb_in{j}", [D * 12288 // 2048, 2048], BF16) for j in range(NRET)]
    WB_OUT = [dsc(f"wb_out{j}", [HV * D // 2048, 2048], BF16) for j in range(NRET)]
    WB_PW1 = [dsc(f"wb_pw1{j}", [D * 2 * D // 2048, 2048], BF16) for j in range(NCONV)]
    WB_PW2 = [dsc(f"wb_pw2{j}", [D * D // 2048, 2048], BF16) for j in range(NCONV)]
    WB_UP = [dsc(f"wb_up{i}", [D * 2 * FH // 2048, 2048], BF16) for i in range(DEPTH)]
    WB_DN = [dsc(f"wb_dn{i}", [FH * D // 2048, 2048], BF16) for i in range(DEPTH)]
    HA = dsc("ha", [D, T], F32)
    HB = dsc("hb", [D, T], F32)
    UT = dsc("ut", [D, T], BF16)
    DG = [dsc(f"dg{j}", [128, KC * CK * 128], BF16) for j in range(NCONV)]
    QTd = dsc("qtd", [NCH, 128, D], BF16)
    KTd = dsc("ktd", [NCH, 128, D], BF16)
    KFd = dsc("kfd", [T, D], BF16)
    KBd = dsc("kbd", [T, D], BF16)
    Vd = dsc("vd", [T, HV], BF16)
    Gd = dsc("gd", [T, HV], BF16)
    SBd = dsc("sbd", [NCH, 128, 16 * 512], BF16)
    GTd = dsc("gtd", [NCH, 128, HV], BF16)

    es = ExitStack()
    with es:
        def sb(name, shape, dtp):
            return es.enter_context(nc.sbuf_tensor("sb_" + name, shape, dtp))

        PE = Q(nc, nc.tensor, "s_pe", es)
        ACT = Q(nc, nc.scalar, "s_act", es)
        DVE = Q(nc, nc.vector, "s_dve", es)
        POOL = Q(nc, nc.gpsimd, "s_pool", es)
        SP = Q(nc, nc.sync, "s_sp", es)
        dcount = [0]

        def newds():
            dcount[0] += 1
            return DS(nc, f"ds{dcount[0]}", es)

        def load(buf, out_ap, in_ap, extra=()):
            SP.wait(buf.prev, *extra)
            m = buf.ds.mark(nc.sync.dma_start(out=out_ap, in_=in_ap))
            buf.wrote(m)
            return m

        store_marks = {}

        def store(buf, out_ap, in_ap, sds):
            POOL.wait(buf.w)
            m = sds.mark(nc.gpsimd.dma_start(out=out_ap, in_=in_ap))
            buf.read(m)
            _merge(store_marks, m)
            return m

        def phase_barrier():
            SP.wait(store_marks)

        cast_q = []
        wready = {}

        def plan_cast(key, src, row0, nrows, dst):
            ds_ = newds()
            n = 0
            for r0 in range(0, nrows, 2048):
                rn = min(2048, nrows - r0)
                cast_q.append((ds_, dst[r0:r0 + rn, :], src[row0 + r0:row0 + r0 + rn, :]))
                n += 16
            wready[key] = (ds_, n)

        for i in range(DEPTH):
            j = i // 2
            if i % 2 == 0:
                plan_cast(("in", j), W_IN, j * (D * 12288 // 2048), D * 12288 // 2048, WB_IN[j])
                plan_cast(("out", j), W_OUT, j * (HV * D // 2048), HV * D // 2048, WB_OUT[j])
            else:
                plan_cast(("pw1", j), W_PW1, j * (D * 2 * D // 2048), D * 2 * D // 2048, WB_PW1[j])
                plan_cast(("pw2", j), W_PW2, j * (D * D // 2048), D * D // 2048, WB_PW2[j])
            plan_cast(("up", i), W_UP, i * (D * 2 * FH // 2048), D * 2 * FH // 2048, WB_UP[i])
            plan_cast(("dn", i), W_DN, i * (FH * D // 2048), FH * D // 2048, WB_DN[i])
        cast_pos = [0]

        def pump(n):
            for _ in range(n):
                if cast_pos[0] >= len(cast_q):
                    return
                ds_, o, i_ = cast_q[cast_pos[0]]
                cast_pos[0] += 1
                ds_.mark(nc.gpsimd.dma_start(out=o, in_=i_))

        def wwait(key):
            ds_, n = wready[key]
            while ds_.n < n:
                pump(1)
            SP.wait((ds_, n))

        prm = Buf(sb("prm", [128, NP], F32), newds())
        dec = Buf(sb("dec", [128, NRET * 16], F32), newds())
        kp = Buf(sb("kp", [128, 2 * NCH + 2], F32), newds())
        msk = Buf(sb("msk", [128, len(SPECIAL) * 128], F32), newds())
        ones = Buf(sb("ones", [128, 128], BF16))
        ident = Buf(sb("ident", [128, 128], BF16))
        iof = Buf(sb("iof", [128, 128], F32))
        iop = Buf(sb("iop", [128, 1], F32))
        ioi = Buf(sb("ioi", [128, 128], I32))
        ipi = Buf(sb("ipi", [128, 1], I32))
        load(prm, prm.t[:], PRM)
        load(dec, dec.t[:], DEC)
        load(kp, kp.t[:], KPD)
        load(msk, msk.t[:], MSKD)
        ones.wrote(DVE.mark(nc.vector.memset(ones.t[:], 1.0)))
        ioi.wrote(POOL.mark(nc.gpsimd.iota(ioi.t[:], pattern=[[1, 128]], base=0, channel_multiplier=0)))
        ipi.wrote(POOL.mark(nc.gpsimd.iota(ipi.t[:], pattern=[[0, 1]], base=0, channel_multiplier=1)))
        DVE.wait(ioi.w, ipi.w)
        iof.wrote(DVE.mark(nc.vector.tensor_copy(out=iof.t[:], in_=ioi.t[:])))
        iop.wrote(DVE.mark(nc.vector.tensor_copy(out=iop.t[:], in_=ipi.t[:])))
        DVE.wait(iof.w, iop.w)
        ident.wrote(DVE.mark(nc.vector.tensor_scalar(out=ident.t[:], in0=iof.t[:], scalar1=iop.t[:, 0:1],
                                                     scalar2=None, op0=ALU.is_equal)))

        def P(nm, idx, col=0, n=1):
            o = pc[(nm, idx)] + col
            return prm.t[:, o:o + n]

        pg = Ring([Buf(es.enter_context(nc.psum_tensor(f"pg{i}", [128, 4, 512], F32))) for i in range(2)])
        ARENA = 194 * 1024
        COMMON = 104 * 1024
        TBL = 20 * 1024
        work = sb("arena", [128, ARENA // 4], F32)

        def carve_at(base, limit, spec):
            res = {}
            o = base // 4
            for nm, shp, dtp in spec:
                nel = int(np.prod(shp))
                nbytes = nel * (4 if dtp in (F32, I32) else 2)
                nw = (nbytes + 3) // 4
                ap = work[:, o:o + nw]
                if dtp != F32:
                    ap = ap.bitcast(dtp)[:, 0:nel]
                names = "abcd"[:len(shp)]
                if len(shp) > 1:
                    kw = {names[i]: shp[i] for i in range(len(shp) - 1)}
                    ap = ap.rearrange("p (" + " ".join(names) + ") -> p " + " ".join(names), **kw)
                res[nm] = ap
                o += nw
            assert o * 4 <= limit, (o * 4, limit)
            return res

        cm = carve_at(0, COMMON, [("xm", [KC, 512], F32), ("hn", [KC, 512], BF16), ("w0", [16, 512], BF16),
                                  ("w1", [16, 512], BF16), ("w2", [16, 512], BF16), ("rstd", [512], F32),
                                  ("h0", [512], F32), ("h1", [512], F32), ("h2", [512], F32)])
        wring = Ring([Buf(cm[f"w{i}"], newds()) for i in range(3)])
        xm = Buf(cm["xm"], newds())
        hn = Buf(cm["hn"])
        rstd = Buf(cm["rstd"])
        hres = Ring([Buf(cm[f"h{i}"], newds()) for i in range(3)])
        xm_sds = newds()

        def carve(spec):
            return carve_at(COMMON, ARENA - TBL, spec)

        work_guard = Buf(work)

        def phase_sync_all():
            marks = []
            for q in (PE, ACT, DVE, POOL):
                if q.n > 0:
                    q.e.wait_ge(q.sem, q.n)
                marks.append(q.mark(q.e.nop(nofuse=True)))
            for q in (PE, ACT, DVE, POOL, SP):
                q.wait(*marks)
                q.wait(store_marks)

        def front(src, c0, N, gname, li):
            xm.begin()
            load(xm, xm.t[:, :, 0:N], src.rearrange("(kc p) t -> p kc t", p=128)[:, :, c0:c0 + N])
            hn.begin()
            ACT.wait(xm.w, hn.prev)
            for kc in range(KC):
                ins = nc.scalar.activation(out=hn.t[:, kc, 0:N], in_=xm.t[:, kc, 0:N], func=AF.Square)
            m = ACT.mark(ins)
            hn.wrote(m)
            g = pg.next()
            g.begin()
            PE.wait(hn.w, g.prev, ones.w)
            for kc in range(KC):
                ins = nc.tensor.matmul(g.t[:, 0, 0:N], lhsT=ones.t[:], rhs=hn.t[:, kc, 0:N],
                                       start=(kc == 0), stop=(kc == KC - 1))
            pm = PE.mark(ins)
            hn.read(pm)
            rstd.begin()
            ACT.wait(pm, rstd.prev)
            m0 = ACT.mark(nc.scalar.activation(out=rstd.t[:, 0:N], in_=g.t[:, 0, 0:N], func=AF.Sqrt, scale=1.0 / D,
                                               bias=EPS))
            DVE.wait(m0)
            m1 = DVE.mark(nc.vector.reciprocal(out=rstd.t[:, 0:N], in_=rstd.t[:, 0:N]))
            g.read(m0)
            g.read(m1)
            rstd.wrote(m1)
            hn.begin()
            DVE.wait(m1, hn.prev, prm.w)
            for kc in range(KC):
                ins = nc.vector.scalar_tensor_tensor(out=hn.t[:, kc, 0:N], in0=xm.t[:, kc, 0:N],
                                                     scalar=P(gname, li, kc), in1=rstd.t[:, 0:N],
                                                     op0=ALU.mult, op1=ALU.mult)
            m2 = DVE.mark(ins)
            hn.wrote(m2)
            xm.read(m2)
            rstd.read(m2)

        def back(N, c0, gname, li, hsrc, dst, special_mask):
            hn.begin()
            ACT.wait(xm.w, hn.prev)
            for kc in range(KC):
                ins = nc.scalar.activation(out=hn.t[:, kc, 0:N], in_=xm.t[:, kc, 0:N], func=AF.Square)
            m = ACT.mark(ins)
            hn.wrote(m)
            xm.read(m)
            g = pg.next()
            g.begin()
            PE.wait(hn.w, g.prev)
            for kc in range(KC):
                ins = nc.tensor.matmul(g.t[:, 0, 0:N], lhsT=ones.t[:], rhs=hn.t[:, kc, 0:N],
                                       start=(kc == 0), stop=(kc == KC - 1))
            pm = PE.mark(ins)
            hn.read(pm)
            rstd.begin()
            ACT.wait(pm, rstd.prev)
            m0 = ACT.mark(nc.scalar.activation(out=rstd.t[:, 0:N], in_=g.t[:, 0, 0:N], func=AF.Sqrt, scale=1.0 / D,
                                               bias=EPS))
            DVE.wait(m0)
            m1 = DVE.mark(nc.vector.reciprocal(out=rstd.t[:, 0:N], in_=rstd.t[:, 0:N]))
            g.read(m0)
            g.read(m1)
            rstd.wrote(m1)
            hs = hsrc.rearrange("(kc p) t -> p kc t", p=128)
            last = None
            for kc in range(KC):
                hb = hres.next()
                hb.begin()
                load(hb, hb.t[:, 0:N], hs[:, kc, c0:c0 + N])
                DVE.wait(m1, hb.w, xm.w, last)
                ma = DVE.mark(nc.vector.scalar_tensor_tensor(out=xm.t[:, kc, 0:N], in0=xm.t[:, kc, 0:N],
                                                             scalar=P(gname, li, kc), in1=rstd.t[:, 0:N],
                                                             op0=ALU.mult, op1=ALU.mult))
                DVE.wait(ma)
                last = DVE.mark(nc.vector.tensor_tensor(out=xm.t[:, kc, 0:N], in0=xm.t[:, kc, 0:N],
                                                        in1=hb.t[:, 0:N], op=ALU.add))
                hb.read(last)
                if special_mask:
                    for si, sc in enumerate(SPECIAL):
                        lo = sc * 128 - c0
                        if 0 <= lo < N:
                            DVE.wait(last, msk.w)
                            last = DVE.mark(nc.vector.tensor_tensor(
                                out=xm.t[:, kc, lo:lo + 128], in0=xm.t[:, kc, lo:lo + 128],
                                in1=msk.t[:, si * 128:(si + 1) * 128], op=ALU.mult))
            xm.wrote(last)
            rstd.read(last)
            store(xm, dst.rearrange("(kc p) t -> p kc t", p=128)[:, :, c0:c0 + N], xm.t[:, :, 0:N], xm_sds)

        def load_slab(WB, ncolsW, k0, kn, pieces, wkey):
            W2 = WB.rearrange("a b -> (a b)").rearrange("(k n) -> k n", n=ncolsW)
            sl = wring.next()
            sl.begin()
            o = 0
            for (cc, ncol) in pieces:
                load(sl, sl.t[:, 0:kn, o:o + ncol],
                     W2[k0 * 128:(k0 + kn) * 128, cc:cc + ncol].rearrange("(kc p) n -> p kc n", p=128))
                o += ncol
            return sl

        def linear_fm(xrhs, xbuf, KCin, WB, ncolsW, groups, N, evac, wkey, hook=None):
            wwait(wkey)
            for gi, pieces in enumerate(groups):
                if hook is not None:
                    hook(gi)
                ncols = sum(p[1] for p in pieces)
                nm = ncols // 128
                g = pg.next()
                g.begin()
                first = True
                for k0 in range(0, KCin, 16):
                    kn = min(16, KCin - k0)
                    sl = load_slab(WB, ncolsW, k0, kn, pieces, wkey)
                    PE.wait(sl.w, xbuf.w)
                    if first:
                        PE.wait(g.prev)
                        first = False
                    for m in range(nm):
                        for kc in range(kn):
                            ins = nc.tensor.matmul(g.t[:, m, 0:N], lhsT=sl.t[:, kc, m * 128:(m + 1) * 128],
                                                   rhs=xrhs(k0 + kc), start=(k0 + kc == 0),
                                                   stop=(k0 + kc == KCin - 1))
                    pm = PE.mark(ins)
                    sl.read(pm)
                xbuf.read(pm)
                g.wrote(pm)
                evac(gi, g, nm)

        def linear_tok(xbuf, nchunk, WB, ncolsW, groups, evac, wkey):
            wwait(wkey)
            for gi, pieces in enumerate(groups):
                g = pg.next()
                g.begin()
                sl = load_slab(WB, ncolsW, 0, KC, pieces, wkey)
                PE.wait(sl.w, xbuf.w, g.prev)
                for ci in range(nchunk):
                    for kc in range(KC):
                        ins = nc.tensor.matmul(g.t[:, ci, :], lhsT=xbuf.t[:, kc, ci * 128:(ci + 1) * 128],
                                               rhs=sl.t[:, kc, :], start=(kc == 0), stop=(kc == KC - 1))
                pm = PE.mark(ins)
                sl.read(pm)
                xbuf.read(pm)
                g.wrote(pm)
                evac(gi, g, nchunk)

        blocks = [(c0, min(512, T - c0)) for c0 in range(0, T, 512)]

        def retention_layer(j, li, hin, hout):
            tb = carve_at(ARENA - TBL, ARENA, [("lg", [16], F32), ("nlg", [16], F32), ("bq", [16], F32), ("t1", [128], F32),
                        ("t2", [128], F32), ("t3", [128], F32), ("dmt", [8, 128], F32),
                        ("fq", [16, 128], BF16), ("fb", [16, 128], BF16), ("dkf", [8], F32), ("dkb", [8], F32),
                        ("cd", [16], F32), ("cdkf", [NCH, 8], F32), ("cdkb", [NCH, 8], F32), ("pj", [2], F32),
                        ("fq32", [128], F32)])
            TB = Buf(work)
            TB.begin()
            d0 = j * 16
            for q in (ACT, DVE):
                q.wait(dec.w, kp.w, iof.w, iop.w)
            a1 = ACT.mark(nc.scalar.activation(out=tb["lg"][:, :], in_=dec.t[:, d0:d0 + 16], func=AF.Exp, scale=-1.0))
            ACT.wait(a1)
            a2 = ACT.mark(nc.scalar.activation(out=tb["nlg"][:, :], in_=tb["lg"][:, :], func=AF.Ln, bias=1.0))
            DVE.wait(a2)
            v1 = DVE.mark(nc.vector.tensor_scalar(out=tb["lg"][:, :], in0=tb["nlg"][:, :], scalar1=-1.0,
                                                  scalar2=None, op0=ALU.mult))
            DVE.wait(v1)
            nc.vector.tensor_scalar(out=tb["pj"][:, 0:1], in0=iop.t[:, 0:1], scalar1=-1.0, scalar2=127.0,
                                    op0=ALU.mult, op1=ALU.add)
            nc.vector.tensor_copy(out=tb["pj"][:, 1:2], in_=iop.t[:, 0:1])
            nc.vector.tensor_scalar(out=tb["bq"][:, 0:8], in0=tb["lg"][:, 0:8], scalar1=LN16, scalar2=None, op0=ALU.add)
            nc.vector.tensor_scalar(out=tb["bq"][:, 8:16], in0=tb["lg"][:, 8:16], scalar1=128.0, scalar2=LN16,
                                    op0=ALU.mult, op1=ALU.add)
            v2 = DVE.mark(nc.vector.tensor_scalar(out=tb["cd"][:, :], in0=tb["lg"][:, :], scalar1=128.0, scalar2=None,
                                                  op0=ALU.mult))
            DVE.wait(v2)
            nc.vector.tensor_scalar(out=tb["dkf"][:, :], in0=tb["lg"][:, 0:8], scalar1=tb["pj"][:, 0:1], scalar2=None,
                                    op0=ALU.mult)
            nc.vector.tensor_scalar(out=tb["dkb"][:, :], in0=tb["lg"][:, 8:16], scalar1=tb["pj"][:, 1:2], scalar2=None,
                                    op0=ALU.mult)
            v3 = DVE.mark(nc.vector.tensor_scalar(out=tb["t3"][:, :], in0=iof.t[:, :], scalar1=iop.t[:, 0:1],
                                                  scalar2=None, op0=ALU.subtract))
            DVE.wait(v3)
            nc.vector.tensor_scalar(out=tb["t1"][:, :], in0=tb["t3"][:, :], scalar1=0.0, scalar2=None, op0=ALU.max)
            v4 = DVE.mark(nc.vector.tensor_scalar(out=tb["t2"][:, :], in0=tb["t3"][:, :], scalar1=-1.0, scalar2=0.0,
                                                  op0=ALU.mult, op1=ALU.max))
            ACT.wait(v4)
            e1 = ACT.mark(nc.scalar.activation(out=tb["cd"][:, :], in_=tb["cd"][:, :], func=AF.Exp))
            nc.scalar.activation(out=tb["dkf"][:, :], in_=tb["dkf"][:, :], func=AF.Exp)
            e2 = ACT.mark(nc.scalar.activation(out=tb["dkb"][:, :], in_=tb["dkb"][:, :], func=AF.Exp))
            for h in range(H):
                DVE.wait(v4, e2)
                nc.vector.tensor_scalar(out=tb["t3"][:, :], in0=tb["t1"][:, :], scalar1=tb["lg"][:, h:h + 1],
                                        scalar2=None, op0=ALU.mult)
                vv = DVE.mark(nc.vector.tensor_scalar(out=tb["fq32"][:, :], in0=tb["t2"][:, :],
                                                      scalar1=tb["lg"][:, 8 + h:9 + h], scalar2=None, op0=ALU.mult))
                DVE.wait(vv)
                vv = DVE.mark(nc.vector.tensor_tensor(out=tb["t3"][:, :], in0=tb["t3"][:, :], in1=tb["fq32"][:, :],
                                                      op=ALU.add))
                ACT.wait(vv)
                e2 = ACT.mark(nc.scalar.activation(out=tb["dmt"][:, h, :], in_=tb["t3"][:, :], func=AF.Exp,
                                                   bias=LN16 if False else 0.0))
                for dcc in range(2):
                    nc.scalar.activation(out=tb["fq"][:, 2 * h + dcc, :], in_=iof.t[:, :], func=AF.Exp,
                                         scale=tb["lg"][:, h:h + 1], bias=tb["bq"][:, h:h + 1])
                    e2 = ACT.mark(nc.scalar.activation(out=tb["fb"][:, 2 * h + dcc, :], in_=iof.t[:, :], func=AF.Exp,
                                                       scale=tb["nlg"][:, 8 + h:9 + h], bias=tb["bq"][:, 8 + h:9 + h]))
            DVE.wait(e1, e2)
            vv = DVE.mark(nc.vector.tensor_scalar(out=tb["dmt"][:, :, :], in0=tb["dmt"][:, :, :], scalar1=1.0 / 16.0,
                                                  scalar2=None, op0=ALU.mult))
            for c in range(NCH):
                nc.vector.tensor_scalar(out=tb["cdkf"][:, c, :], in0=tb["cd"][:, 0:8], scalar1=kp.t[:, c:c + 1],
                                        scalar2=None, op0=ALU.mult)
                vv = DVE.mark(nc.vector.tensor_scalar(out=tb["cdkb"][:, c, :], in0=tb["cd"][:, 8:16],
                                                      scalar1=kp.t[:, NCH + 1 + c:NCH + 2 + c], scalar2=None,
                                                      op0=ALU.mult))
            TBM = vv
            for q in (ACT, DVE, POOL, PE):
                q.wait(TBM, e2)
            def carve2(spec, full=False):
                return carve_at(0 if full else COMMON, ARENA - TBL, spec)

            ra = carve2([("qo", [4, 16, 128], BF16), ("ko", [4, 16, 128], BF16),
                         ("vt0", [4, 512], BF16),
                         ("kf0", [4, 512], BF16),
                         ("kb0", [4, 512], BF16),
                         ("orot", [4, 2, 2, 128], F32), ("ta", [512], F32), ("tb", [512], F32),
                         ("tc", [512], F32), ("td", [512], F32),
                         ("cf", [512], F32), ("sf", [512], F32), ("ct", [4, 128], F32), ("st", [4, 128], F32)])
            qo = Buf(ra["qo"]); ko = Buf(ra["ko"])
            vtr = Ring([Buf(ra["vt0"])])
            kfr = Ring([Buf(ra["kf0"])])
            kbr = Ring([Buf(ra["kb0"])])
            orot = Buf(ra["orot"])
            tmp = Buf(ra["ta"])
            cf = Buf(ra["cf"], newds()); sf = Buf(ra["sf"], newds())
            ct = Buf(ra["ct"], newds()); st = Buf(ra["st"], newds())
            sd = {k: newds() for k in ("q", "k", "v", "g", "kf", "kb")}
            WB = WB_IN[j]
            for (c0, N) in blocks:
                nchunk = N // 128
                cb = c0 // 128
                pump(1)
                front(hin, c0, N, "g_pre_mix", li)
                for b_, src_ in ((cf, COSF), (sf, SINF)):
                    b_.begin()
                    load(b_, b_.t[:, 0:N], src_[:, c0:c0 + N])
                for b_, src_ in ((ct, COST), (st, SINT)):
                    b_.begin()
                    load(b_, b_.t[:, 0:nchunk, :], src_[c0:c0 + N, :].rearrange("(c p) f -> p c f", p=128))

                def ev_v(gi, g, nch_, dst=Vd, ring=vtr, func=AF.Copy, key="v"):
                    vt = ring.next()
                    vt.begin()
                    ACT.wait(g.w, vt.prev)
                    m = ACT.mark(nc.scalar.activation(out=vt.t[:, 0:nch_, :], in_=g.t[:, 0:nch_, :], func=func))
                    g.read(m)
                    vt.wrote(m)
                    store(vt, dst[c0:c0 + N, gi * 512:(gi + 1) * 512].rearrange("(c p) f -> p c f", p=128),
                          vt.t[:, 0:nch_, :], sd[key])

                linear_tok(hn, nchunk, WB, 12288, [[(4096 + 512 * gi, 512)] for gi in range(8)], ev_v, ("in", j))
                linear_tok(hn, nchunk, WB, 12288, [[(8192 + 512 * gi, 512)] for gi in range(8)],
                           lambda gi, g, n_: ev_v(gi, g, n_, Gd, vtr, AF.Silu, "g"), ("in", j))

                def ev_ktok(gi, g, nch_):
                    gv = g.t.rearrange("p c (a b f) -> p c a b f", a=2, b=2)
                    orot.begin()
                    tmp.begin()
                    DVE.wait(g.w, orot.prev, tmp.prev, ct.w, st.w)
                    last = None
                    for hh in range(2):
                        x1 = gv[:, 0:nch_, hh, 0, :]
                        x2 = gv[:, 0:nch_, hh, 1, :]
                        cs = ct.t[:, 0:nch_, :]
                        sn = st.t[:, 0:nch_, :]
                        ta = ra["ta"].rearrange("p (c f) -> p c f", c=4)[:, 0:nch_, :]
                        tbb = ra["tb"].rearrange("p (c f) -> p c f", c=4)[:, 0:nch_, :]
                        tcc = ra["tc"].rearrange("p (c f) -> p c f", c=4)[:, 0:nch_, :]
                        tdd = ra["td"].rearrange("p (c f) -> p c f", c=4)[:, 0:nch_, :]
                        DVE.wait(last)
                        nc.vector.tensor_tensor(out=ta, in0=x1, in1=cs, op=ALU.mult)
                        nc.vector.tensor_tensor(out=tbb, in0=x2, in1=sn, op=ALU.mult)
                        nc.vector.tensor_tensor(out=tcc, in0=x2, in1=cs, op=ALU.mult)
                        mm = DVE.mark(nc.vector.tensor_tensor(out=tdd, in0=x1, in1=sn, op=ALU.mult))
                        DVE.wait(mm)
                        nc.vector.tensor_tensor(out=orot.t[:, 0:nch_, hh, 0, :], in0=ta, in1=tbb, op=ALU.subtract)
                        last = DVE.mark(nc.vector.tensor_tensor(out=orot.t[:, 0:nch_, hh, 1, :], in0=tcc, in1=tdd,
                                                                op=ALU.add))
                    g.read(last)
                    ct.read(last); st.read(last)
                    orot.wrote(last)
                    tmp.wrote(last)
                    kf = kfr.next(); kb = kbr.next()
                    kf.begin(); kb.begin()
                    POOL.wait(orot.w, kf.prev, kb.prev, TBM)
                    for hh in range(2):
                        h = 2 * gi + hh
                        src_ = orot.t[:, 0:nch_, hh, :, :]
                        nc.gpsimd.tensor_scalar(out=kf.t.rearrange("p c (a b f) -> p c a b f", a=2, b=2)[:, 0:nch_, hh, :, :],
                                                in0=src_, scalar1=tb["dkf"][:, h:h + 1], scalar2=None, op0=ALU.mult)
                        mk = POOL.mark(nc.gpsimd.tensor_scalar(
                            out=kb.t.rearrange("p c (a b f) -> p c a b f", a=2, b=2)[:, 0:nch_, hh, :, :],
                            in0=src_, scalar1=tb["dkb"][:, h:h + 1], scalar2=None, op0=ALU.mult))
                    orot.read(mk)
                    kf.wrote(mk); kb.wrote(mk)
                    store(kf, KFd[c0:c0 + N, gi * 512:(gi + 1) * 512].rearrange("(c p) f -> p c f", p=128),
                          kf.t[:, 0:nch_, :], sd["kf"])
                    store(kb, KBd[c0:c0 + N, gi * 512:(gi + 1) * 512].rearrange("(c p) f -> p c f", p=128),
                          kb.t[:, 0:nch_, :], sd["kb"])

                linear_tok(hn, nchunk, WB, 12288, [[(2048 + 512 * gi, 512)] for gi in range(4)], ev_ktok, ("in", j))

                def mk_ev_fm(ob):
                    def ev(gi, g, nm):
                        tmp.begin()
                        DVE.wait(g.w, tmp.prev, cf.w, sf.w, ob.prev)
                        last = None
                        for hh in range(2):
                            x1 = g.t[:, 2 * hh, 0:N]
                            x2 = g.t[:, 2 * hh + 1, 0:N]
                            DVE.wait(last)
                            nc.vector.tensor_tensor(out=ra["ta"][:, 0:N], in0=x1, in1=cf.t[:, 0:N], op=ALU.mult)
                            nc.vector.tensor_tensor(out=ra["tb"][:, 0:N], in0=x2, in1=sf.t[:, 0:N], op=ALU.mult)
                            nc.vector.tensor_tensor(out=ra["tc"][:, 0:N], in0=x2, in1=cf.t[:, 0:N], op=ALU.mult)
                            mm = DVE.mark(nc.vector.tensor_tensor(out=ra["td"][:, 0:N], in0=x1, in1=sf.t[:, 0:N],
                                                                  op=ALU.mult))
                            DVE.wait(mm)
                            kc1 = gi * 4 + 2 * hh
                            nc.vector.tensor_tensor(out=ob.t[:, 0:nchunk, kc1, :],
                                                    in0=ra["ta"][:, 0:N].rearrange("p (c f) -> p c f", f=128),
                                                    in1=ra["tb"][:, 0:N].rearrange("p (c f) -> p c f", f=128),
                                                    op=ALU.subtract)
                            last = DVE.mark(nc.vector.tensor_tensor(
                                out=ob.t[:, 0:nchunk, kc1 + 1, :],
                                in0=ra["tc"][:, 0:N].rearrange("p (c f) -> p c f", f=128),
                                in1=ra["td"][:, 0:N].rearrange("p (c f) -> p c f", f=128), op=ALU.add))
                        g.read(last)
                        tmp.wrote(last)
                        ob.wrote(last)
                    return ev

                qo.begin(); ko.begin()
                linear_fm(lambda kc: hn.t[:, kc, 0:N], hn, KC, WB, 12288,
                          [[(512 * gi, 512)] for gi in range(4)], N, mk_ev_fm(qo), ("in", j))
                linear_fm(lambda kc: hn.t[:, kc, 0:N], hn, KC, WB, 12288,
                          [[(2048 + 512 * gi, 512)] for gi in range(4)], N, mk_ev_fm(ko), ("in", j))
                cf.read(DVE.mark(nc.vector.engine_nop())) if False else None
                for b_ in (cf, sf):
                    b_.read((DVE, DVE.n))
                store(qo, QTd[cb:cb + nchunk].rearrange("c p (k f) -> p c k f", k=16), qo.t[:, 0:nchunk, :, :], sd["q"])
                store(ko, KTd[cb:cb + nchunk].rearrange("c p (k f) -> p c k f", k=16), ko.t[:, 0:nchunk, :, :], sd["k"])
            phase_sync_all()

            rb = carve2(full=True, spec=[("S", [16, 512], F32), ("so", [16, 512], BF16), ("kb0", [2048], BF16), ("kb1", [2048], BF16),
                         ("v0", [4096], BF16), ("v1", [4096], BF16)])
            S = Buf(rb["S"]); so = Buf(rb["so"])
            kbl = Ring([Buf(rb["kb0"], newds()), Buf(rb["kb1"], newds())])
            vl = Ring([Buf(rb["v0"], newds()), Buf(rb["v1"], newds())])
            so_sds = newds()
            S.begin()
            DVE.wait(S.prev)
            S.wrote(DVE.mark(nc.vector.memset(S.t[:, :, :], 0.0)))

            def state_update(S, kx, vx, cdk, c):
                for hg in range(4):
                    g = pg.next()
                    g.begin()
                    PE.wait(kx.w, vx.w, g.prev)
                    for hl in range(2):
                        h = 2 * hg + hl
                        for dcc in range(2):
                            ins = nc.tensor.matmul(g.t[:, 2 * hl + dcc, :],
                                                   lhsT=kx.t[:, h * 256 + dcc * 128:h * 256 + (dcc + 1) * 128],
                                                   rhs=vx.t[:, h * 512:(h + 1) * 512], start=True, stop=True)
                    pm = PE.mark(ins)
                    g.wrote(pm)
                    kx.read(pm); vx.read(pm)
                    DVE.wait(pm, S.w, S.r)
                    for hl in range(2):
                        h = 2 * hg + hl
                        ins = nc.vector.scalar_tensor_tensor(out=S.t[:, 2 * h:2 * h + 2, :], in0=S.t[:, 2 * h:2 * h + 2, :],
                                                             scalar=cdk[:, c, h:h + 1],
                                                             in1=g.t[:, 2 * hl:2 * hl + 2, :], op0=ALU.mult, op1=ALU.add)
                    m = DVE.mark(ins)
                    g.read(m)
                    S.wrote(m)

            for c in range(NCH - 1, -1, -1):
                kx = kbl.next(); vx = vl.next()
                kx.begin(); vx.begin()
                load(kx, kx.t[:, :], KBd[c * 128:(c + 1) * 128, :])
                load(vx, vx.t[:, :], Vd[c * 128:(c + 1) * 128, :])
                so.begin()
                ACT.wait(S.w, so.prev)
                for q4 in range(4):
                    ins = nc.scalar.activation(out=so.t[:, 4 * q4:4 * q4 + 4, :], in_=S.t[:, 4 * q4:4 * q4 + 4, :],
                                               func=AF.Copy, scale=kp.t[:, NCH + 1 + c:NCH + 2 + c])
                m = ACT.mark(ins)
                so.wrote(m)
                S.read(m)
                store(so, SBd[c], so.t.rearrange("p a b -> p (a b)"), so_sds)
                state_update(S, kx, vx, tb["cdkb"], c)
            phase_sync_all()

            rc = carve2(full=True, spec=[("S", [16, 512], F32), ("sfb", [16, 512], BF16), ("sbt", [16, 512], BF16),
                         ("qt0", [16, 128], BF16), ("qt1", [16, 128], BF16), ("kt0", [16, 128], BF16),
                         ("kt1", [16, 128], BF16), ("kf0", [2048], BF16), ("kf1", [2048], BF16),
                         ("v0", [4096], BF16), ("g0", [4096], BF16),
                         ("qf", [16, 128], BF16), ("qb", [16, 128], BF16), ("sc", [8, 128], BF16),
                         ("on", [4, 512], F32), ("gated", [4096], BF16), ("gT", [32, 128], BF16),
                         ("stt", [8, 6], F32), ("mv", [8, 2], F32), ("rs", [8], F32), ("nb", [8], F32)])
            S = Buf(rc["S"]); sfb = Buf(rc["sfb"]); sbt = Buf(rc["sbt"], newds())
            qtl = Ring([Buf(rc["qt0"], newds()), Buf(rc["qt1"], newds())])
            ktl = Ring([Buf(rc["kt0"], newds()), Buf(rc["kt1"], newds())])
            kfl = Ring([Buf(rc["kf0"], newds()), Buf(rc["kf1"], newds())])
            vl = Ring([Buf(rc["v0"], newds())])
            gl = Ring([Buf(rc["g0"], newds())])
            qf = Buf(rc["qf"]); qb = Buf(rc["qb"]); sc = Buf(rc["sc"]); on = Buf(rc["on"])
            gated = Buf(rc["gated"]); gT = Buf(rc["gT"]); stt = Buf(rc["stt"])
            gt_sds = newds()
            S.begin(); sfb.begin()
            DVE.wait(S.prev, sfb.prev)
            S.wrote(DVE.mark(nc.vector.memset(S.t[:, :, :], 0.0)))
            sfb.wrote(DVE.mark(nc.vector.memset(sfb.t[:, :, :], 0.0)))
            for c in range(NCH):
                qt = qtl.next(); kt = ktl.next(); kx = kfl.next(); vx = vl.next(); gx = gl.next()
                for b_, src_ in ((qt, QTd[c]), (kt, KTd[c])):
                    b_.begin()
                    load(b_, b_.t.rearrange("p a b -> p (a b)"), src_)
                kx.begin(); load(kx, kx.t[:, :], KFd[c * 128:(c + 1) * 128, :])
                vx.begin(); load(vx, vx.t[:, :], Vd[c * 128:(c + 1) * 128, :])
                gx.begin(); load(gx, gx.t[:, :], Gd[c * 128:(c + 1) * 128, :])
                sbt.begin(); load(sbt, sbt.t.rearrange("p a b -> p (a b)"), SBd[c])
                qf.begin(); qb.begin()
                POOL.wait(qt.w, qf.prev, qb.prev)
                nc.gpsimd.tensor_tensor(out=qf.t[:, :, :], in0=qt.t[:, :, :], in1=tb["fq"][:, :, :], op=ALU.mult)
                m = POOL.mark(nc.gpsimd.tensor_tensor(out=qb.t[:, :, :], in0=qt.t[:, :, :], in1=tb["fb"][:, :, :],
                                                      op=ALU.mult))
                qf.wrote(m); qb.wrote(m); qt.read(m)
                g = pg.next(); g.begin()
                PE.wait(qt.w, kt.w, g.prev)
                for h in range(H):
                    for dcc in range(2):
                        ins = nc.tensor.matmul(g.t[:, h // 4, (h % 4) * 128:(h % 4 + 1) * 128],
                                               lhsT=kt.t[:, 2 * h + dcc, :], rhs=qt.t[:, 2 * h + dcc, :],
                                               start=(dcc == 0), stop=(dcc == 1))
                pm = PE.mark(ins)
                g.wrote(pm); qt.read(pm); kt.read(pm)
                sc.begin()
                DVE.wait(pm, sc.prev)
                for half in range(2):
                    ins = nc.vector.tensor_tensor(out=sc.t[:, 4 * half:4 * half + 4, :],
                                                  in0=g.t[:, half, :].rearrange("p (a b) -> p a b", a=4),
                                                  in1=tb["dmt"][:, 4 * half:4 * half + 4, :], op=ALU.mult)
                m = DVE.mark(ins)
                g.read(m); sc.wrote(m)
                gated.begin()
                for hg in range(2):
                    g = pg.next(); g.begin()
                    PE.wait(sc.w, vx.w, qf.w, qb.w, sfb.w, sbt.w, g.prev)
                    for hl in range(4):
                        h = 4 * hg + hl
                        nc.tensor.matmul(g.t[:, hl, :], lhsT=sc.t[:, h, :], rhs=vx.t[:, h * 512:(h + 1) * 512],
                                         start=True, stop=False)
                        for dcc in range(2):
                            nc.tensor.matmul(g.t[:, hl, :], lhsT=qf.t[:, 2 * h + dcc, :], rhs=sfb.t[:, 2 * h + dcc, :],
                                             start=False, stop=False)
                        for dcc in range(2):
                            ins = nc.tensor.matmul(g.t[:, hl, :], lhsT=qb.t[:, 2 * h + dcc, :],
                                                   rhs=sbt.t[:, 2 * h + dcc, :], start=False, stop=(dcc == 1))
                    pm = PE.mark(ins)
                    g.wrote(pm)
                    for b_ in (sc, vx, qf, qb, sfb, sbt):
                        b_.read(pm)
                    stt.begin()
                    DVE.wait(pm, stt.prev)
                    for hl in range(4):
                        ins = nc.vector.bn_stats(out=rc["stt"][:, hl, :], in_=g.t[:, hl, :])
                    m = DVE.mark(ins)
                    DVE.wait(m)
                    for hl in range(4):
                        ins = nc.vector.bn_aggr(out=rc["mv"][:, hl, :], in_=rc["stt"][:, hl, :])
                    m = DVE.mark(ins)
                    DVE.wait(m)
                    ACT.wait(m)
                    m = ACT.mark(nc.scalar.activation(out=rc["rs"][:, 0:4], in_=rc["mv"][:, 0:4, 1], func=AF.Sqrt,
                                                      bias=EPS))
                    DVE.wait(m)
                    nc.vector.reciprocal(out=rc["rs"][:, 0:4], in_=rc["rs"][:, 0:4])
                    m = DVE.mark(nc.vector.tensor_copy(out=rc["rs"][:, 4:8], in_=rc["mv"][:, 0:4, 0]))
                    DVE.wait(m)
                    m = DVE.mark(nc.vector.scalar_tensor_tensor(out=rc["nb"][:, 0:4], in0=rc["rs"][:, 4:8], scalar=-1.0,
                                                                in1=rc["rs"][:, 0:4], op0=ALU.mult, op1=ALU.mult))
                    stt.wrote(m)
                    on.begin()
                    ACT.wait(m, on.prev)
                    for hl in range(4):
                        ins = nc.scalar.activation(out=on.t[:, hl, :], in_=g.t[:, hl, :], func=AF.Identity,
                                                   scale=rc["rs"][:, hl:hl + 1], bias=rc["nb"][:, hl:hl + 1])
                    m = ACT.mark(ins)
                    g.read(m); stt.read(m); on.wrote(m)
                    POOL.wait(m, gx.w, gated.prev)
                    m = POOL.mark(nc.gpsimd.tensor_tensor(
                        out=gated.t[:, hg * 2048:(hg + 1) * 2048].rearrange("p (a b) -> p a b", a=4),
                        in0=on.t[:, :, :], in1=gx.t[:, hg * 2048:(hg + 1) * 2048].rearrange("p (a b) -> p a b", a=4),
                        op=ALU.mult))
                    on.read(m); gx.read(m); gated.wrote(m)
                state_update(S, kx, vx, tb["cdkf"], c)
                sfb.begin()
                ACT.wait(S.w, sfb.prev)
                for q4 in range(4):
                    ins = nc.scalar.activation(out=sfb.t[:, 4 * q4:4 * q4 + 4, :], in_=S.t[:, 4 * q4:4 * q4 + 4, :],
                                               func=AF.Copy, scale=kp.t[:, c + 1:c + 2])
                m = ACT.mark(ins)
                sfb.wrote(m); S.read(m)
                g = pg.next(); g.begin()
                gb = g.t.rearrange("p a b -> p (a b)").bitcast(BF16).rearrange("p (a b) -> p a b", a=32)
                PE.wait(gated.w, g.prev, ident.w)
                for ec in range(32):
                    ins = nc.tensor.transpose(out=gb[:, ec, 0:128], in_=gated.t[:, ec * 128:(ec + 1) * 128],
                                              identity=ident.t[:, :])
                pm = PE.mark(ins)
                g.wrote(pm); gated.read(pm)
                gT.begin()
                ACT.wait(pm, gT.prev)
                DVE.wait(pm, gT.prev)
                m1 = ACT.mark(nc.scalar.activation(out=gT.t[:, 0:16, :], in_=gb[:, 0:16, 0:128], func=AF.Copy))
                m2 = DVE.mark(nc.vector.tensor_copy(out=gT.t[:, 16:32, :], in_=gb[:, 16:32, 0:128]))
                g.read(m1); g.read(m2); gT.wrote(m1); gT.wrote(m2)
                store(gT, GTd[c], gT.t.rearrange("p a b -> p (a b)"), gt_sds)
            phase_sync_all()

            rd = carve2([("gt", [4, 32, 128], BF16)])
            gtb = Buf(rd["gt"], newds())
            for (c0, N) in blocks:
                nchunk = N // 128
                cb = c0 // 128
                pump(1)
                gtb.begin()
                for ci in range(nchunk):
                    load(gtb, gtb.t[:, ci, :, :].rearrange("p e f -> p (e f)"), GTd[cb + ci])
                xm.begin()

                def ev_o(gi, g, nm):
                    ACT.wait(g.w, xm.prev)
                    for m_ in range(nm):
                        ins = nc.scalar.activation(out=xm.t[:, gi * 4 + m_, 0:N], in_=g.t[:, m_, 0:N], func=AF.Copy)
                    mk = ACT.mark(ins)
                    g.read(mk); xm.wrote(mk)

                linear_fm(lambda kc: gtb.t[:, 0:nchunk, kc, :], gtb, 32, WB_OUT[j], D,
                          [[(512 * gi, 512)] for gi in range(4)], N, ev_o, ("out", j))
                back(N, c0, "g_post_mix", li, hin, hout, False)
            phase_sync_all()

        def conformer_layer(j, li, hin, hout):
            HALO = 15
            cw = carve([("u", [KC, 512], BF16), ("sig", [4, 512], F32), ("dg0", [CK, 128], BF16), ("dg1", [CK, 128], BF16)])
            ub = Buf(cw["u"]); sig = Buf(cw["sig"])
            u_sds = newds()
            dgr = Ring([Buf(cw["dg0"]), Buf(cw["dg1"])])
            for b_ in dgr.b:
                b_.sds = newds()
            wc0 = pc[("w_dw", j)]
            for kc in range(KC):
                t_ = dgr.next()
                t_.begin()
                DVE.wait(t_.prev, ident.w, prm.w)
                for k in range(CK):
                    ins = nc.vector.tensor_scalar(out=t_.t[:, k, :], in0=ident.t[:, :],
                                                  scalar1=prm.t[:, wc0 + kc * CK + k:wc0 + kc * CK + k + 1],
                                                  scalar2=None, op0=ALU.mult)
                t_.wrote(DVE.mark(ins))
                store(t_, DG[j][:, kc * CK * 128:(kc + 1) * CK * 128], t_.t.rearrange("p a b -> p (a b)"), t_.sds)
            for (c0, N) in blocks:
                pump(1)
                front(hin, c0, N, "g_pre_mix", li)
                ub.begin()
                gate_g = {}

                def ev_glu(gi, g, nm):
                    sig.begin()
                    ACT.wait(g.w, sig.prev)
                    for m_ in range(2):
                        kc = gi * 2 + m_
                        ins = nc.scalar.activation(out=sig.t[:, m_, 0:N], in_=g.t[:, 2 + m_, 0:N], func=AF.Sigmoid,
                                                   bias=P("b_pw1", j, KC + kc))
                    mk = ACT.mark(ins)
                    sig.wrote(mk)
                    DVE.wait(mk, ub.prev)
                    for m_ in range(2):
                        kc = gi * 2 + m_
                        ins = nc.vector.scalar_tensor_tensor(out=ub.t[:, kc, 0:N], in0=g.t[:, m_, 0:N],
                                                             scalar=P("b_pw1", j, kc), in1=sig.t[:, m_, 0:N],
                                                             op0=ALU.add, op1=ALU.mult)
                    mk2 = DVE.mark(ins)
                    for si, scn in enumerate(SPECIAL):
                        lo = scn * 128 - c0
                        if 0 <= lo < N:
                            DVE.wait(mk2, msk.w)
                            mk2 = DVE.mark(nc.vector.tensor_tensor(
                                out=ub.t[:, gi * 2:gi * 2 + 2, lo:lo + 128], in0=ub.t[:, gi * 2:gi * 2 + 2, lo:lo + 128],
                                in1=msk.t[:, si * 128:(si + 1) * 128].rearrange("p (a f) -> p a f", a=1).broadcast_to([128, 2, 128])
                                if False else msk.t[:, si * 128:(si + 1) * 128], op=ALU.mult)) if False else mk2
                            for m_ in range(2):
                                kc = gi * 2 + m_
                                mk2 = DVE.mark(nc.vector.tensor_tensor(out=ub.t[:, kc, lo:lo + 128],
                                                                       in0=ub.t[:, kc, lo:lo + 128],
                                                                       in1=msk.t[:, si * 128:(si + 1) * 128], op=ALU.mult))
                                DVE.wait(mk2)
                    g.read(mk2); sig.read(mk2); ub.wrote(mk2)

                linear_fm(lambda kc: hn.t[:, kc, 0:N], hn, KC, WB_PW1[j], 2 * D,
                          [[(256 * gi, 256), (D + 256 * gi, 256)] for gi in range(8)], N, ev_glu, ("pw1", j))
                store(ub, UT.rearrange("(kc p) t -> p kc t", p=128)[:, :, c0:c0 + N], ub.t[:, :, 0:N], u_sds)
            phase_sync_all()

            W_ = 512 + 2 * HALO
            cw = carve([("uh", [KC, W_], BF16), ("mu", [512], F32), ("ex2", [512], F32), ("xh", [KC, 512], BF16),
                        ("t", [512], F32)])
            uh = Buf(cw["uh"], newds()); mu = Buf(cw["mu"]); xh = Buf(cw["xh"]); tt = Buf(cw["t"])
            y = xm
            UTv = UT.rearrange("(kc p) t -> p kc t", p=128)
            for (c0, N) in blocks:
                pump(1)
                lo = max(c0 - HALO, 0)
                hi = min(c0 + N + HALO, T)
                uh.begin()
                o0 = lo - (c0 - HALO)
                if o0 > 0:
                    DVE.wait(uh.prev)
                    uh.wrote(DVE.mark(nc.vector.memset(uh.t[:, :, 0:o0], 0.0)))
                if hi < c0 + N + HALO:
                    DVE.wait(uh.prev)
                    uh.wrote(DVE.mark(nc.vector.memset(uh.t[:, :, o0 + hi - lo:N + 2 * HALO], 0.0)))
                load(uh, uh.t[:, :, o0:o0 + hi - lo], UTv[:, :, lo:hi])
                y.begin()
                for g4 in range(4):
                    g = pg.next()
                    g.begin()
                    for half in range(2):
                        kc0 = g4 * 4 + half * 2
                        sl = wring.next()
                        sl.begin()
                        slf = sl.t.rearrange("p a b -> p (a b)")
                        load(sl, slf[:, 0:2 * CK * 128], DG[j][:, kc0 * CK * 128:(kc0 + 2) * CK * 128])
                        PE.wait(sl.w, uh.w)
                        if half == 0:
                            PE.wait(g.prev)
                        for kk in range(2):
                            kc = kc0 + kk
                            for k in range(CK):
                                ins = nc.tensor.matmul(g.t[:, half * 2 + kk, 0:N],
                                                       lhsT=slf[:, (kk * CK + k) * 128:(kk * CK + k + 1) * 128],
                                                       rhs=uh.t[:, kc, k:k + N], start=(k == 0), stop=(k == CK - 1))
                        pm = PE.mark(ins)
                        sl.read(pm)
                    g.wrote(pm)
                    uh.read(pm)
                    ACT.wait(pm, y.prev, prm.w)
                    for m_ in range(4):
                        kc = g4 * 4 + m_
                        ins = nc.scalar.activation(out=y.t[:, kc, 0:N], in_=g.t[:, m_, 0:N], func=AF.Identity,
                                                   bias=P("b_dw", j, kc))
                    mk = ACT.mark(ins)
                    g.read(mk)
                    y.wrote(mk)
                hn.begin()
                xh.begin()
                ACT.wait(y.w, hn.prev, xh.prev)
                for kc in range(KC):
                    nc.scalar.activation(out=hn.t[:, kc, 0:N], in_=y.t[:, kc, 0:N], func=AF.Square)
                    ins = nc.scalar.activation(out=xh.t[:, kc, 0:N], in_=y.t[:, kc, 0:N], func=AF.Copy)
                mk = ACT.mark(ins)
                hn.wrote(mk); xh.wrote(mk)
                g = pg.next(); g.begin()
                PE.wait(mk, g.prev)
                for kc in range(KC):
                    nc.tensor.matmul(g.t[:, 0, 0:N], lhsT=ones.t[:], rhs=xh.t[:, kc, 0:N], start=(kc == 0),
                                     stop=(kc == KC - 1))
                for kc in range(KC):
                    ins = nc.tensor.matmul(g.t[:, 1, 0:N], lhsT=ones.t[:], rhs=hn.t[:, kc, 0:N], start=(kc == 0),
                                           stop=(kc == KC - 1))
                pm = PE.mark(ins)
                hn.read(pm); xh.read(pm); g.wrote(pm)
                mu.begin(); rstd.begin(); tt.begin()
                DVE.wait(pm, mu.prev, rstd.prev, tt.prev)
                nc.vector.tensor_scalar(out=mu.t[:, 0:N], in0=g.t[:, 0, 0:N], scalar1=1.0 / D, scalar2=None, op0=ALU.mult)
                m1 = DVE.mark(nc.vector.tensor_scalar(out=cw["ex2"][:, 0:N], in0=g.t[:, 1, 0:N], scalar1=1.0 / D,
                                                      scalar2=EPS, op0=ALU.mult, op1=ALU.add))
                DVE.wait(m1)
                m1 = DVE.mark(nc.vector.tensor_tensor(out=tt.t[:, 0:N], in0=mu.t[:, 0:N], in1=mu.t[:, 0:N], op=ALU.mult))
                DVE.wait(m1)
                m1 = DVE.mark(nc.vector.tensor_tensor(out=rstd.t[:, 0:N], in0=cw["ex2"][:, 0:N], in1=tt.t[:, 0:N],
                                                      op=ALU.subtract))
                ACT.wait(m1)
                m1 = ACT.mark(nc.scalar.activation(out=rstd.t[:, 0:N], in_=rstd.t[:, 0:N], func=AF.Sqrt))
                DVE.wait(m1)
                m1 = DVE.mark(nc.vector.reciprocal(out=rstd.t[:, 0:N], in_=rstd.t[:, 0:N]))
                g.read(m1); mu.wrote(m1); rstd.wrote(m1)
                xh.begin()
                lastd = m1
                for kc in range(KC):
                    DVE.wait(lastd, y.w)
                    ma = DVE.mark(nc.vector.tensor_tensor(out=y.t[:, kc, 0:N], in0=y.t[:, kc, 0:N], in1=mu.t[:, 0:N],
                                                          op=ALU.subtract))
                    DVE.wait(ma)
                    lastd = DVE.mark(nc.vector.tensor_tensor(out=y.t[:, kc, 0:N], in0=y.t[:, kc, 0:N],
                                                             in1=rstd.t[:, 0:N], op=ALU.mult))
                    ACT.wait(lastd, xh.prev)
                    mk = ACT.mark(nc.scalar.activation(out=xh.t[:, kc, 0:N], in_=y.t[:, kc, 0:N], func=AF.Silu,
                                                       scale=P("ln_g", j, kc), bias=P("ln_b", j, kc)))
                xh.wrote(mk); y.read(mk); mu.read(lastd); rstd.read(lastd)
                xm.begin()

                def ev_p2(gi, g, nm):
                    ACT.wait(g.w, xm.prev)
                    for m_ in range(nm):
                        kc = gi * 4 + m_
                        ins = nc.scalar.activation(out=xm.t[:, kc, 0:N], in_=g.t[:, m_, 0:N], func=AF.Identity,
                                                   bias=P("b_pw2", j, kc))
                    mk_ = ACT.mark(ins)
                    g.read(mk_); xm.wrote(mk_)

                linear_fm(lambda kc: xh.t[:, kc, 0:N], xh, KC, WB_PW2[j], D,
                          [[(512 * gi, 512)] for gi in range(4)], N, ev_p2, ("pw2", j))
                back(N, c0, "g_post_mix", li, hin, hout, True)
            phase_sync_all()

        def ffn_layer(li, hin, hout):
            fw = carve_at(COMMON, ARENA, [("act", [FC, 512], BF16), ("a0", [2, 514], F32), ("a1", [2, 514], F32),
                                          ("y", [2, 512], F32), ("hn2", [KC, 512], BF16), ("rstf", [512], F32)])
            act = Buf(fw["act"])
            ar = Ring([Buf(fw["a0"]), Buf(fw["a1"])])
            yb = Buf(fw["y"])
            hns = [hn, Buf(fw["hn2"])]
            rstf = Buf(fw["rstf"])
            fblocks = [(s_, min(s_ + 510, T)) for s_ in range(0, T, 510)]
            NB = len(fblocks)
            hs = hin.rearrange("(kc p) t -> p kc t", p=128)

            def geom(bi):
                s0, e0 = fblocks[bi]
                lo = max(s0 - 1, 0)
                hi = min(e0 + 1, T)
                return dict(s0=s0, e0=e0, lo=lo, hi=hi, N=hi - lo, NO=e0 - s0, o0=lo - (s0 - 1))

            fst = {}

            def front_A(bi):
                ge = geom(bi)
                H_ = hns[bi % 2]
                H_.begin()
                last = None
                for kc in range(KC):
                    hb = hres.next()
                    hb.begin()
                    load(hb, hb.t[:, 0:ge["N"]], hs[:, kc, ge["lo"]:ge["hi"]])
                    ACT.wait(hb.w, H_.prev)
                    last = ACT.mark(nc.scalar.activation(out=H_.t[:, kc, 0:ge["N"]], in_=hb.t[:, 0:ge["N"]],
                                                         func=AF.Square))
                    hb.read(last)
                H_.wrote(last)

            def front_B(bi):
                ge = geom(bi)
                H_ = hns[bi % 2]
                g = pg.next()
                g.begin()
                PE.wait(H_.w, g.prev, ones.w)
                for kc in range(KC):
                    ins = nc.tensor.matmul(g.t[:, 0, 0:ge["N"]], lhsT=ones.t[:], rhs=H_.t[:, kc, 0:ge["N"]],
                                           start=(kc == 0), stop=(kc == KC - 1))
                pm = PE.mark(ins)
                H_.read(pm)
                g.wrote(pm)
                rstf.begin()
                ACT.wait(pm, rstf.prev)
                m0 = ACT.mark(nc.scalar.activation(out=rstf.t[:, 0:ge["N"]], in_=g.t[:, 0, 0:ge["N"]], func=AF.Sqrt,
                                                   scale=1.0 / D, bias=EPS))
                g.read(m0)
                DVE.wait(m0)
                m1 = DVE.mark(nc.vector.reciprocal(out=rstf.t[:, 0:ge["N"]], in_=rstf.t[:, 0:ge["N"]]))
                rstf.wrote(m1)

            def front_C(bi):
                ge = geom(bi)
                H_ = hns[bi % 2]
                H_.begin()
                last = None
                for kc in range(KC):
                    hb = hres.next()
                    hb.begin()
                    load(hb, hb.t[:, 0:ge["N"]], hs[:, kc, ge["lo"]:ge["hi"]])
                    DVE.wait(hb.w, H_.prev, rstf.w, prm.w)
                    last = DVE.mark(nc.vector.scalar_tensor_tensor(out=H_.t[:, kc, 0:ge["N"]], in0=hb.t[:, 0:ge["N"]],
                                                                   scalar=P("g_pre_ffn", li, kc), in1=rstf.t[:, 0:ge["N"]],
                                                                   op0=ALU.mult, op1=ALU.mult))
                    hb.read(last)
                H_.wrote(last)
                rstf.read(last)

            def back_A(bi):
                ge = geom(bi)
                S_ = hns[bi % 2]
                S_.begin()
                ACT.wait(xm.w, S_.prev)
                for kc in range(KC):
                    ins = nc.scalar.activation(out=S_.t[:, kc, 0:ge["NO"]], in_=xm.t[:, kc, 0:ge["NO"]], func=AF.Square)
                m = ACT.mark(ins)
                S_.wrote(m)
                xm.read(m)

            def back_B(bi):
                ge = geom(bi)
                S_ = hns[bi % 2]
                g = pg.next()
                g.begin()
                PE.wait(S_.w, g.prev)
                for kc in range(KC):
                    ins = nc.tensor.matmul(g.t[:, 0, 0:ge["NO"]], lhsT=ones.t[:], rhs=S_.t[:, kc, 0:ge["NO"]],
                                           start=(kc == 0), stop=(kc == KC - 1))
                pm = PE.mark(ins)
                S_.read(pm)
                g.wrote(pm)
                rstd.begin()
                ACT.wait(pm, rstd.prev)
                m0 = ACT.mark(nc.scalar.activation(out=rstd.t[:, 0:ge["NO"]], in_=g.t[:, 0, 0:ge["NO"]], func=AF.Sqrt,
                                                   scale=1.0 / D, bias=EPS))
                g.read(m0)
                DVE.wait(m0)
                m1 = DVE.mark(nc.vector.reciprocal(out=rstd.t[:, 0:ge["NO"]], in_=rstd.t[:, 0:ge["NO"]]))
                rstd.wrote(m1)

            def back_C(bi):
                ge = geom(bi)
                NO = ge["NO"]
                last = None
                for kc in range(KC):
                    hb = hres.next()
                    hb.begin()
                    load(hb, hb.t[:, 0:NO], hs[:, kc, ge["s0"]:ge["e0"]])
                    DVE.wait(rstd.w, hb.w, xm.w, last)
                    ma = DVE.mark(nc.vector.scalar_tensor_tensor(out=xm.t[:, kc, 0:NO], in0=xm.t[:, kc, 0:NO],
                                                                 scalar=P("g_post_ffn", li, kc), in1=rstd.t[:, 0:NO],
                                                                 op0=ALU.mult, op1=ALU.mult))
                    DVE.wait(ma)
                    last = DVE.mark(nc.vector.tensor_tensor(out=xm.t[:, kc, 0:NO], in0=xm.t[:, kc, 0:NO],
                                                            in1=hb.t[:, 0:NO], op=ALU.add))
                    hb.read(last)
                xm.wrote(last)
                rstd.read(last)
                store(xm, hout.rearrange("(kc p) t -> p kc t", p=128)[:, :, ge["s0"]:ge["e0"]], xm.t[:, :, 0:NO], xm_sds)

            groups = []
            for gi in range((FC + 1) // 2):
                nj = min(2, FC - 2 * gi)
                groups.append([(256 * gi, 128 * nj), (FH + 256 * gi, 128 * nj)])

            front_A(0); front_B(0); front_C(0)
            for bi in range(NB):
                pump(1)
                ge = geom(bi)
                N = ge["N"]; NO = ge["NO"]; o0 = ge["o0"]; s0 = ge["s0"]; lo = ge["lo"]
                H_ = hns[bi % 2]
                act.begin()

                def ev_up(gi, g, nm, N=N, NO=NO, o0=o0, s0=s0, lo=lo):
                    nj = nm // 2
                    ab = ar.next()
                    ab.begin()
                    ACT.wait(g.w, ab.prev)
                    if o0 > 0:
                        nc.scalar.activation(out=ab.t[:, 0:nj, 0:1], in_=ab.t[:, 0:nj, 0:1], func=AF.Copy, scale=0.0)
                    if o0 + N < NO + 2:
                        nc.scalar.activation(out=ab.t[:, 0:nj, o0 + N:NO + 2], in_=ab.t[:, 0:nj, o0 + N:NO + 2],
                                             func=AF.Copy, scale=0.0)
                    mk = ACT.mark(nc.scalar.activation(out=ab.t[:, 0:nj, o0:o0 + N], in_=g.t[:, 0:nj, 0:N], func=AF.Copy))
                    ab.wrote(mk)
                    yb.begin()
                    DVE.wait(mk, yb.prev, act.prev, prm.w)
                    wl = pc[("ffn_w_dw", li)]
                    last = None
                    for jj in range(nj):
                        jg = gi * 2 + jj
                        DVE.wait(last)
                        m0 = DVE.mark(nc.vector.tensor_scalar(out=yb.t[:, jj, 0:NO], in0=ab.t[:, jj, 0:NO],
                                                              scalar1=prm.t[:, wl + jg * 3:wl + jg * 3 + 1],
                                                              scalar2=P("ffn_b_dw", li, jg), op0=ALU.mult, op1=ALU.add))
                        DVE.wait(m0)
                        m0 = DVE.mark(nc.vector.scalar_tensor_tensor(out=yb.t[:, jj, 0:NO], in0=ab.t[:, jj, 1:NO + 1],
                                                                     scalar=prm.t[:, wl + jg * 3 + 1:wl + jg * 3 + 2],
                                                                     in1=yb.t[:, jj, 0:NO], op0=ALU.mult, op1=ALU.add))
                        DVE.wait(m0)
                        last = DVE.mark(nc.vector.scalar_tensor_tensor(out=yb.t[:, jj, 0:NO], in0=ab.t[:, jj, 2:NO + 2],
                                                                       scalar=prm.t[:, wl + jg * 3 + 2:wl + jg * 3 + 3],
                                                                       in1=yb.t[:, jj, 0:NO], op0=ALU.mult, op1=ALU.add))
                    ab.read(last)
                    ACT.wait(last)
                    m3 = ACT.mark(nc.scalar.activation(out=yb.t[:, 0:nj, 0:NO], in_=yb.t[:, 0:nj, 0:NO],
                                                       func=AF.Gelu_apprx_tanh))
                    DVE.wait(m3)
                    vo = s0 - lo
                    m4 = DVE.mark(nc.vector.tensor_tensor(out=act.t[:, gi * 2:gi * 2 + nj, 0:NO], in0=yb.t[:, 0:nj, 0:NO],
                                                          in1=g.t[:, nj:2 * nj, vo:vo + NO], op=ALU.mult))
                    g.read(m4); yb.wrote(m4); act.wrote(m4)

                def hook(gi, bi=bi):
                    if bi > 0:
                        if gi == 0:
                            back_A(bi - 1)
                        elif gi == 3:
                            back_B(bi - 1)
                        elif gi == 4:
                            back_C(bi - 1)
                    if bi + 1 < NB:
                        if gi == 9:
                            front_A(bi + 1)
                        elif gi == 13:
                            front_B(bi + 1)
                        elif gi == 14:
                            front_C(bi + 1)

                linear_fm(lambda kc, H_=H_, N=N: H_.t[:, kc, 0:N], H_, KC, WB_UP[li], 2 * FH, groups, N, ev_up,
                          ("up", li), hook=hook)
                xm.begin()

                def ev_dn(gi, g, nm, NO=NO):
                    ACT.wait(g.w, xm.prev)
                    for m_ in range(nm):
                        ins = nc.scalar.activation(out=xm.t[:, gi * 4 + m_, 0:NO], in_=g.t[:, m_, 0:NO], func=AF.Copy)
                    mk_ = ACT.mark(ins)
                    g.read(mk_); xm.wrote(mk_)

                linear_fm(lambda kc, NO=NO: act.t[:, kc, 0:NO], act, FC, WB_DN[li], D,
                          [[(512 * gi, 512)] for gi in range(4)], NO, ev_dn, ("dn", li))
            back_A(NB - 1); back_B(NB - 1); back_C(NB - 1)
            phase_sync_all()

        for q in (ACT, DVE, POOL, PE):
            q.wait(prm.w, dec.w, kp.w, msk.w, ident.w, ones.w)
        cur = XT
        for li in range(DEPTH):
            j = li // 2
            mid = HB
            if li % 2 == 0:
                retention_layer(j, li, cur, mid)
            else:
                conformer_layer(j, li, cur, mid)
            nxt = YT if li == DEPTH - 1 else HA
            ffn_layer(li, mid, nxt)
            cur = nxt
        pump(10 ** 6)
        for q in (POOL,):
            q.wait(store_marks)
    return nc, pc, NP, SPECIAL, NCH, T


def _fm(vec):
    return np.ascontiguousarray(vec.reshape(-1, 128).T)


def make_core_inputs(seqs, meta, SEGC, SPECIAL, NCH, T):
    xt = np.zeros((T, D), np.float32)
    keep = np.ones(NCH + 1, np.float32)
    valid = np.zeros(T, np.float32)
    pos = np.zeros(T, np.float64)
    places = []
    c = 0
    for s in seqs:
        L = s.shape[0]
        ncs = L // 128
        keep[c] = 0.0
        r0 = c * 128 + 128 - NMETA
        xt[r0:r0 + NMETA] = meta
        xt[r0 + NMETA:r0 + NMETA + L] = s
        valid[r0:r0 + NMETA + L] = 1.0
        pos[c * 128:(c + 1 + ncs) * 128] = np.arange((1 + ncs) * 128)
        places.append((r0 + NMETA, L))
        c += 1 + ncs
    while c < NCH:
        keep[c] = 0.0
        c += 1
    keep[NCH] = 0.0
    keepb = np.concatenate([keep[1:NCH + 1], [0.0]]).astype(np.float32)
    kpv = np.concatenate([keep, keepb]).astype(np.float32)
    kp = np.ascontiguousarray(np.broadcast_to(kpv[None, :], (128, kpv.size))).astype(np.float32)
    mskv = np.concatenate([valid[sc * 128:(sc + 1) * 128] for sc in SPECIAL])
    msk = np.ascontiguousarray(np.broadcast_to(mskv[None, :], (128, mskv.size))).astype(np.float32)
    inv = 10000.0 ** (-np.arange(128, dtype=np.float32) / np.float32(128))
    ang = pos.astype(np.float32)[:, None] * inv[None, :].astype(np.float32)
    cost = np.cos(ang.astype(np.float64)).astype(np.float32)
    sint = np.sin(ang.astype(np.float64)).astype(np.float32)
    return dict(xt=np.ascontiguousarray(xt.T), kp=kp, msk=msk, cost=cost, sint=sint,
                cosf=np.ascontiguousarray(cost.T), sinf=np.ascontiguousarray(sint.T)), places


def pack_params(inp, pc, NP, DEPTH):
    prm = np.zeros((128, NP), np.float32)
    for (nm, idx), o in pc.items():
        if nm == "g_pre_mix":
            a = _fm(inp["norm_pre_mix"][idx])
        elif nm == "g_post_mix":
            a = _fm(inp["norm_post_mix"][idx])
        elif nm == "g_pre_ffn":
            a = _fm(inp["norm_pre_ffn"][idx])
        elif nm == "g_post_ffn":
            a = _fm(inp["norm_post_ffn"][idx])
        elif nm == "ffn_w_dw":
            w = inp["ffn_w_dw"][idx]
            a = np.ascontiguousarray(w.T.reshape(FC, 128, 3).transpose(1, 0, 2).reshape(128, FC * 3))
        elif nm == "ffn_b_dw":
            a = _fm(inp["ffn_b_dw"][idx])
        elif nm == "b_pw1":
            a = _fm(inp["conv_b_pw1"][idx])
        elif nm == "w_dw":
            w = inp["conv_w_dw"][idx]
            a = np.ascontiguousarray(w.T.reshape(KC, 128, CK).transpose(1, 0, 2).reshape(128, KC * CK))
        elif nm == "b_dw":
            a = _fm(inp["conv_b_dw"][idx])
        elif nm == "ln_g":
            a = _fm(inp["conv_ln_g"][idx])
        elif nm == "ln_b":
            a = _fm(inp["conv_ln_b"][idx])
        elif nm == "b_pw2":
            a = _fm(inp["conv_b_pw2"][idx])
        prm[:, o:o + a.shape[1]] = a
    return prm


_CACHE = {}


def run_model(inp, core_seqs, SEGC, DEPTH):
    key = (SEGC, DEPTH)
    if key not in _CACHE:
        _CACHE[key] = build_program(SEGC, DEPTH)
    nc, pc, NP, SPECIAL, NCH, T = _CACHE[key]
    NRET = (DEPTH + 1) // 2
    NCONV = DEPTH // 2
    prm = pack_params(inp, pc, NP, DEPTH)
    decv = np.zeros((NRET * 16,), np.float32)
    for j in range(NRET):
        decv[j * 16:j * 16 + 8] = inp["ret_decay_fwd"][j]
        decv[j * 16 + 8:j * 16 + 16] = inp["ret_decay_bwd"][j]
    dec = np.ascontiguousarray(np.broadcast_to(decv[None, :], (128, decv.size))).astype(np.float32)

    def flat(a, n):
        a = np.ascontiguousarray(a[:n]) if n > 0 else np.zeros((1,) + a.shape[1:], np.float32)
        return a.reshape(-1, 2048)

    shared = dict(prm=prm, dec=dec,
                  ret_w_in=flat(inp["ret_w_in"], NRET), ret_w_out=flat(inp["ret_w_out"], NRET),
                  conv_w_pw1=flat(inp["conv_w_pw1"], NCONV), conv_w_pw2=flat(inp["conv_w_pw2"], NCONV),
                  ffn_w_up=flat(inp["ffn_w_up"], DEPTH), ffn_w_down=flat(inp["ffn_w_down"], DEPTH))
    in_maps = []
    places = []
    for seqs in core_seqs:
        d, pl = make_core_inputs(seqs, inp["meta_tokens"], SEGC, SPECIAL, NCH, T)
        d.update(shared)
        in_maps.append(d)
        places.append(pl)
    res = run_bass_kernel_spmd(nc, in_maps, core_ids=list(range(len(core_seqs))))
    outs = []
    for ci, pl in enumerate(places):
        yt = res.results[ci]["yt"]
        y = yt.T
        outs.append([np.ascontiguousarray(y[r0:r0 + L]) for (r0, L) in pl])
    return outs


def kernel(**inp):
    xp = np.asarray(inp["x_prompt"], np.float32)
    xs = np.asarray(inp["x_sample"], np.float32)
    SEGC = xp.shape[1] // 128
    DEPTH = 4
    core_seqs = []
    for c in range(4):
        core_seqs.append([xs[c], xp[c]])
    for c in range(4):
        core_seqs.append([xp[4 + 3 * c + k] for k in range(3)])
    outs = run_model(inp, core_seqs, SEGC, DEPTH)
    yp = np.zeros_like(xp)
    ys = np.zeros_like(xs)
    for c in range(4):
        ys[c] = outs[c][0]
        yp[c] = outs[c][1]
    for c in range(4):
        for k in range(3):
            yp[4 + 3 * c + k] = outs[4 + c][k]
    return (yp, ys)
```

```python
import numpy as np
from contextlib import ExitStack
import concourse.bass as bass
import concourse.mybir as mybir
from concourse.bass_utils import run_bass_kernel_spmd

F32 = mybir.dt.float32
BF16 = mybir.dt.bfloat16
I32 = mybir.dt.int32
AF = mybir.ActivationFunctionType
ALU = mybir.AluOpType

D = 2048
KC = 16
H = 8
HV = 4096
FH = 5504
FC = 43
CK = 31
NMETA = 16
EPS = 1e-6
LN16 = float(np.log(1.0 / 16.0))


_UID = [0]


def _next_uid():
    _UID[0] += 1
    return _UID[0]


class Q:
    def __init__(self, nc, eng, name, es, step=1):
        self.uid = _next_uid()
        self.e = eng
        self.sem = es.enter_context(nc.semaphore(name))
        self.n = 0
        self.step = step
        self.seen = {}

    def mark(self, ins):
        ins.then_inc(self.sem, self.step)
        self.n += self.step
        return (self, self.n)

    def wait(self, *marks):
        for m in marks:
            if m is None:
                continue
            if isinstance(m, dict):
                self.wait(*m.values())
                continue
            src, n = m
            if self.seen.get(src.uid, 0) >= n:
                continue
            self.e.wait_ge(src.sem, n)
            self.seen[src.uid] = n


class DS:
    def __init__(self, nc, name, es):
        self.uid = _next_uid()
        self.sem = es.enter_context(nc.semaphore(name))
        self.n = 0

    def mark(self, ins):
        ins.then_inc(self.sem, 16)
        self.n += 16
        return (self, self.n)


def _merge(d, m):
    if m is None:
        return
    src, n = m
    k = src.uid
    if k not in d or d[k][1] < n:
        d[k] = (src, n)


class Buf:
    def __init__(self, t, ds=None):
        self.t = t
        self.w = {}
        self.r = {}
        self.prev = {}
        self.ds = ds

    def begin(self):
        self.prev = {}
        for m in list(self.w.values()) + list(self.r.values()):
            _merge(self.prev, m)
        self.w = {}
        self.r = {}

    def wrote(self, m):
        _merge(self.w, m)

    def read(self, m):
        _merge(self.r, m)


class PGroup:
    def __init__(self, t):
        self.t = t
        self.h = [Buf(t[:, 0:2, :]), Buf(t[:, 2:4, :])]

    def begin(self):
        for h in self.h:
            h.begin()

    def _m(self, attr):
        d = {}
        for h in self.h:
            for m in getattr(h, attr).values():
                _merge(d, m)
        return d

    @property
    def prev(self):
        return self._m("prev")

    @property
    def w(self):
        return self._m("w")

    def wrote(self, m):
        for h in self.h:
            h.wrote(m)

    def read(self, m):
        for h in self.h:
            h.read(m)


class Ring:
    def __init__(self, bufs):
        self.b = bufs
        self.i = 0

    def next(self):
        b = self.b[self.i % len(self.b)]
        self.i += 1
        return b


def build_program(SEGC, DEPTH):
    NCH = 3 * (SEGC + 1)
    T = NCH * 128
    NRET = (DEPTH + 1) // 2
    NCONV = DEPTH // 2
    SPECIAL = sorted({0, 1 + SEGC, 2 + 2 * SEGC, 1 + 2 * SEGC, 2 + 3 * SEGC})
    pc = {}
    off = 0
    for i in range(DEPTH):
        for nm, w in (("g_pre_mix", KC), ("g_post_mix", KC), ("g_pre_ffn", KC), ("g_post_ffn", KC),
                      ("ffn_w_dw", FC * 3), ("ffn_b_dw", FC)):
            pc[(nm, i)] = off
            off += w
    for j in range(NCONV):
        for nm, w in (("b_pw1", 2 * KC), ("w_dw", KC * CK), ("b_dw", KC), ("ln_g", KC), ("ln_b", KC), ("b_pw2", KC)):
            pc[(nm, j)] = off
            off += w
    NP = off

    nc = bass.Bass("TRN2", target_bir_lowering=False)
    dt = nc.dram_tensor

    def din(name, shape, dtp=F32):
        return dt(name, shape, dtp, kind="ExternalInput").ap()

    def dsc(name, shape, dtp):
        return dt(name, shape, dtp, kind="Internal").ap()

    XT = din("xt", [D, T])
    PRM = din("prm", [128, NP])
    DEC = din("dec", [128, NRET * 16])
    KPD = din("kp", [128, 2 * NCH + 2])
    MSKD = din("msk", [128, len(SPECIAL) * 128])
    COSF = din("cosf", [128, T])
    SINF = din("sinf", [128, T])
    COST = din("cost", [T, 128])
    SINT = din("sint", [T, 128])
    W_IN = din("ret_w_in", [NRET * D * 12288 // 2048, 2048])
    W_OUT = din("ret_w_out", [NRET * HV * D // 2048, 2048])
    W_PW1 = din("conv_w_pw1", [max(NCONV, 1) * D * 2 * D // 2048, 2048])
    W_PW2 = din("conv_w_pw2", [max(NCONV, 1) * D * D // 2048, 2048])
    W_UP = din("ffn_w_up", [DEPTH * D * 2 * FH // 2048, 2048])
    W_DN = din("ffn_w_down", [DEPTH * FH * D // 2048, 2048])
    YT = dt("yt", [D, T], F32, kind="ExternalOutput").ap()

    WB_IN = [dsc(f"wb_in{j}", [D * 12288 // 2048, 2048], BF16) for j in range(NRET)]
    WB_OUT = [dsc(f"wb_out{j}", [HV * D // 2048, 2048], BF16) for j in range(NRET)]
    WB_PW1 = [dsc(f"wb_pw1{j}", [D * 2 * D // 2048, 2048], BF16) for j in range(NCONV)]
    WB_PW2 = [dsc(f"wb_pw2{j}", [D * D // 2048, 2048], BF16) for j in range(NCONV)]
    WB_UP = [dsc(f"wb_up{i}", [D * 2 * FH // 2048, 2048], BF16) for i in range(DEPTH)]
    WB_DN = [dsc(f"wb_dn{i}", [FH * D // 2048, 2048], BF16) for i in range(DEPTH)]
    HA = dsc("ha", [D, T], F32)
    HB = dsc("hb", [D, T], F32)
    UT = dsc("ut", [D, T], BF16)
    DG = [dsc(f"dg{j}", [128, KC * CK * 128], BF16) for j in range(NCONV)]
    QTd = dsc("qtd", [NCH, 128, D], BF16)
    KTd = dsc("ktd", [NCH, 128, D], BF16)
    KFd = dsc("kfd", [T, D], BF16)
    KBd = dsc("kbd", [T, D], BF16)
    Vd = dsc("vd", [T, HV], BF16)
    Gd = dsc("gd", [T, HV], BF16)
    SBd = dsc("sbd", [NCH, 128, 16 * 512], BF16)
    GTd = dsc("gtd", [NCH, 128, HV], BF16)

    es = ExitStack()
    with es:
        def sb(name, shape, dtp):
            return es.enter_context(nc.sbuf_tensor("sb_" + name, shape, dtp))

        PE = Q(nc, nc.tensor, "s_pe", es)
        ACT = Q(nc, nc.scalar, "s_act", es)
        DVE = Q(nc, nc.vector, "s_dve", es)
        POOL = Q(nc, nc.gpsimd, "s_pool", es)
        SP = Q(nc, nc.sync, "s_sp", es)
        dcount = [0]

        def newds():
            dcount[0] += 1
            return DS(nc, f"ds{dcount[0]}", es)

        def load(buf, out_ap, in_ap, extra=()):
            SP.wait(buf.prev, *extra)
            m = buf.ds.mark(nc.sync.dma_start(out=out_ap, in_=in_ap))
            buf.wrote(m)
            return m

        store_marks = {}

        def store(buf, out_ap, in_ap, sds):
            POOL.wait(buf.w)
            m = sds.mark(nc.gpsimd.dma_start(out=out_ap, in_=in_ap))
            buf.read(m)
            _merge(store_marks, m)
            return m

        def phase_barrier():
            SP.wait(store_marks)

        cast_q = []
        wready = {}

        def plan_cast(key, src, row0, nrows, dst):
            ds_ = newds()
            n = 0
            for r0 in range(0, nrows, 2048):
                rn = min(2048, nrows - r0)
                cast_q.append((ds_, dst[r0:r0 + rn, :], src[row0 + r0:row0 + r0 + rn, :]))
                n += 16
            wready[key] = (ds_, n)

        for i in range(DEPTH):
            j = i // 2
            if i % 2 == 0:
                plan_cast(("in", j), W_IN, j * (D * 12288 // 2048), D * 12288 // 2048, WB_IN[j])
                plan_cast(("out", j), W_OUT, j * (HV * D // 2048), HV * D // 2048, WB_OUT[j])
            else:
                plan_cast(("pw1", j), W_PW1, j * (D * 2 * D // 2048), D * 2 * D // 2048, WB_PW1[j])
                plan_cast(("pw2", j), W_PW2, j * (D * D // 2048), D * D // 2048, WB_PW2[j])
            plan_cast(("up", i), W_UP, i * (D * 2 * FH // 2048), D * 2 * FH // 2048, WB_UP[i])
            plan_cast(("dn", i), W_DN, i * (FH * D // 2048), FH * D // 2048, WB_DN[i])
        cast_pos = [0]

        def pump(n):
            for _ in range(n):
                if cast_pos[0] >= len(cast_q):
                    return
                ds_, o, i_ = cast_q[cast_pos[0]]
                cast_pos[0] += 1
                ds_.mark(nc.gpsimd.dma_start(out=o, in_=i_))

        def wwait(key):
            ds_, n = wready[key]
            while ds_.n < n:
                pump(1)
            SP.wait((ds_, n))

        prm = Buf(sb("prm", [128, NP], F32), newds())
        dec = Buf(sb("dec", [128, NRET * 16], F32), newds())
        kp = Buf(sb("kp", [128, 2 * NCH + 2], F32), newds())
        msk = Buf(sb("msk", [128, len(SPECIAL) * 128], F32), newds())
        ones = Buf(sb("ones", [128, 128], BF16))
        ident = Buf(sb("ident", [128, 128], BF16))
        iof = Buf(sb("iof", [128, 128], F32))
        iop = Buf(sb("iop", [128, 1], F32))
        ioi = Buf(sb("ioi", [128, 128], I32))
        ipi = Buf(sb("ipi", [128, 1], I32))
        load(prm, prm.t[:], PRM)
        load(dec, dec.t[:], DEC)
        load(kp, kp.t[:], KPD)
        load(msk, msk.t[:], MSKD)
        ones.wrote(DVE.mark(nc.vector.memset(ones.t[:], 1.0)))
        ioi.wrote(POOL.mark(nc.gpsimd.iota(ioi.t[:], pattern=[[1, 128]], base=0, channel_multiplier=0)))
        ipi.wrote(POOL.mark(nc.gpsimd.iota(ipi.t[:], pattern=[[0, 1]], base=0, channel_multiplier=1)))
        DVE.wait(ioi.w, ipi.w)
        iof.wrote(DVE.mark(nc.vector.tensor_copy(out=iof.t[:], in_=ioi.t[:])))
        iop.wrote(DVE.mark(nc.vector.tensor_copy(out=iop.t[:], in_=ipi.t[:])))
        DVE.wait(iof.w, iop.w)
        ident.wrote(DVE.mark(nc.vector.tensor_scalar(out=ident.t[:], in0=iof.t[:], scalar1=iop.t[:, 0:1],
                                                     scalar2=None, op0=ALU.is_equal)))

        def P(nm, idx, col=0, n=1):
            o = pc[(nm, idx)] + col
            return prm.t[:, o:o + n]

        pg = Ring([PGroup(es.enter_context(nc.psum_tensor(f"pg{i}", [128, 4, 512], F32))) for i in range(2)])
        ARENA = 194 * 1024
        COMMON = 104 * 1024
        TBL = 20 * 1024
        work = sb("arena", [128, ARENA // 4], F32)

        def carve_at(base, limit, spec):
            res = {}
            o = base // 4
            for nm, shp, dtp in spec:
                nel = int(np.prod(shp))
                nbytes = nel * (4 if dtp in (F32, I32) else 2)
                nw = (nbytes + 3) // 4
                ap = work[:, o:o + nw]
                if dtp != F32:
                    ap = ap.bitcast(dtp)[:, 0:nel]
                names = "abcd"[:len(shp)]
                if len(shp) > 1:
                    kw = {names[i]: shp[i] for i in range(len(shp) - 1)}
                    ap = ap.rearrange("p (" + " ".join(names) + ") -> p " + " ".join(names), **kw)
                res[nm] = ap
                o += nw
            assert o * 4 <= limit, (o * 4, limit)
            return res

        cm = carve_at(0, COMMON, [("xm", [KC, 512], F32), ("hn", [KC, 512], BF16), ("w0", [16, 512], BF16),
                                  ("w1", [16, 512], BF16), ("w2", [16, 512], BF16), ("rstd", [512], F32),
                                  ("h0", [512], F32), ("h1", [512], F32), ("h2", [512], F32)])
        wring = Ring([Buf(cm[f"w{i}"], newds()) for i in range(3)])
        xm = Buf(cm["xm"], newds())
        hn = Buf(cm["hn"])
        rstd = Buf(cm["rstd"])
        hres = Ring([Buf(cm[f"h{i}"], newds()) for i in range(3)])
        xm_sds = newds()
        hresf_ds = [newds() for _ in range(7)]

        def carve(spec):
            return carve_at(COMMON, ARENA - TBL, spec)

        work_guard = Buf(work)

        def phase_sync_all():
            marks = []
            for q in (PE, ACT, DVE, POOL):
                if q.n > 0:
                    q.e.wait_ge(q.sem, q.n)
                marks.append(q.mark(q.e.nop(nofuse=True)))
            for q in (PE, ACT, DVE, POOL, SP):
                q.wait(*marks)
                q.wait(store_marks)

        def front(src, c0, N, gname, li):
            xm.begin()
            load(xm, xm.t[:, :, 0:N], src.rearrange("(kc p) t -> p kc t", p=128)[:, :, c0:c0 + N])
            hn.begin()
            ACT.wait(xm.w, hn.prev)
            for kc in range(KC):
                ins = nc.scalar.activation(out=hn.t[:, kc, 0:N], in_=xm.t[:, kc, 0:N], func=AF.Square)
            m = ACT.mark(ins)
            hn.wrote(m)
            g = pg.next()
            g.begin()
            PE.wait(hn.w, g.prev, ones.w)
            for kc in range(KC):
                ins = nc.tensor.matmul(g.t[:, 0, 0:N], lhsT=ones.t[:], rhs=hn.t[:, kc, 0:N],
                                       start=(kc == 0), stop=(kc == KC - 1))
            pm = PE.mark(ins)
            hn.read(pm)
            rstd.begin()
            ACT.wait(pm, rstd.prev)
            m0 = ACT.mark(nc.scalar.activation(out=rstd.t[:, 0:N], in_=g.t[:, 0, 0:N], func=AF.Sqrt, scale=1.0 / D,
                                               bias=EPS))
            DVE.wait(m0)
            m1 = DVE.mark(nc.vector.reciprocal(out=rstd.t[:, 0:N], in_=rstd.t[:, 0:N]))
            g.read(m0)
            g.read(m1)
            rstd.wrote(m1)
            hn.begin()
            DVE.wait(m1, hn.prev, prm.w)
            for kc in range(KC):
                ins = nc.vector.scalar_tensor_tensor(out=hn.t[:, kc, 0:N], in0=xm.t[:, kc, 0:N],
                                                     scalar=P(gname, li, kc), in1=rstd.t[:, 0:N],
                                                     op0=ALU.mult, op1=ALU.mult)
            m2 = DVE.mark(ins)
            hn.wrote(m2)
            xm.read(m2)
            rstd.read(m2)

        def back(N, c0, gname, li, hsrc, dst, special_mask):
            hn.begin()
            ACT.wait(xm.w, hn.prev)
            for kc in range(KC):
                ins = nc.scalar.activation(out=hn.t[:, kc, 0:N], in_=xm.t[:, kc, 0:N], func=AF.Square)
            m = ACT.mark(ins)
            hn.wrote(m)
            xm.read(m)
            g = pg.next()
            g.begin()
            PE.wait(hn.w, g.prev)
            for kc in range(KC):
                ins = nc.tensor.matmul(g.t[:, 0, 0:N], lhsT=ones.t[:], rhs=hn.t[:, kc, 0:N],
                                       start=(kc == 0), stop=(kc == KC - 1))
            pm = PE.mark(ins)
            hn.read(pm)
            rstd.begin()
            ACT.wait(pm, rstd.prev)
            m0 = ACT.mark(nc.scalar.activation(out=rstd.t[:, 0:N], in_=g.t[:, 0, 0:N], func=AF.Sqrt, scale=1.0 / D,
                                               bias=EPS))
            DVE.wait(m0)
            m1 = DVE.mark(nc.vector.reciprocal(out=rstd.t[:, 0:N], in_=rstd.t[:, 0:N]))
            g.read(m0)
            g.read(m1)
            rstd.wrote(m1)
            hs = hsrc.rearrange("(kc p) t -> p kc t", p=128)
            last = None
            for kc in range(KC):
                hb = hres.next()
                hb.begin()
                load(hb, hb.t[:, 0:N], hs[:, kc, c0:c0 + N])
                DVE.wait(m1, hb.w, xm.w, last)
                ma = DVE.mark(nc.vector.scalar_tensor_tensor(out=xm.t[:, kc, 0:N], in0=xm.t[:, kc, 0:N],
                                                             scalar=P(gname, li, kc), in1=rstd.t[:, 0:N],
                                                             op0=ALU.mult, op1=ALU.mult))
                DVE.wait(ma)
                last = DVE.mark(nc.vector.tensor_tensor(out=xm.t[:, kc, 0:N], in0=xm.t[:, kc, 0:N],
                                                        in1=hb.t[:, 0:N], op=ALU.add))
                hb.read(last)
                if special_mask:
                    for si, sc in enumerate(SPECIAL):
                        lo = sc * 128 - c0
                        if 0 <= lo < N:
                            DVE.wait(last, msk.w)
                            last = DVE.mark(nc.vector.tensor_tensor(
                                out=xm.t[:, kc, lo:lo + 128], in0=xm.t[:, kc, lo:lo + 128],
                                in1=msk.t[:, si * 128:(si + 1) * 128], op=ALU.mult))
            xm.wrote(last)
            rstd.read(last)
            store(xm, dst.rearrange("(kc p) t -> p kc t", p=128)[:, :, c0:c0 + N], xm.t[:, :, 0:N], xm_sds)

        def load_slab(WB, ncolsW, k0, kn, pieces, wkey):
            W2 = WB.rearrange("a b -> (a b)").rearrange("(k n) -> k n", n=ncolsW)
            sl = wring.next()
            sl.begin()
            o = 0
            for (cc, ncol) in pieces:
                load(sl, sl.t[:, 0:kn, o:o + ncol],
                     W2[k0 * 128:(k0 + kn) * 128, cc:cc + ncol].rearrange("(kc p) n -> p kc n", p=128))
                o += ncol
            return sl

        def linear_fm(xrhs, xbuf, KCin, WB, ncolsW, groups, N, evac, wkey, hook=None):
            wwait(wkey)
            for gi, pieces in enumerate(groups):
                if hook is not None:
                    hook(gi)
                ncols = sum(p[1] for p in pieces)
                nm = ncols // 128
                g = pg.next()
                g.begin()
                first = True
                for k0 in range(0, KCin, 16):
                    kn = min(16, KCin - k0)
                    sl = load_slab(WB, ncolsW, k0, kn, pieces, wkey)
                    PE.wait(sl.w, xbuf.w)
                    for m in range(nm):
                        if first and m == 0:
                            PE.wait(g.h[0].prev)
                        if first and (m == 2 or (m == 0 and nm <= 2 and False)):
                            PE.wait(g.h[1].prev)
                        for kc in range(kn):
                            ins = nc.tensor.matmul(g.t[:, m, 0:N], lhsT=sl.t[:, kc, m * 128:(m + 1) * 128],
                                                   rhs=xrhs(k0 + kc), start=(k0 + kc == 0),
                                                   stop=(k0 + kc == KCin - 1))
                    pm = PE.mark(ins)
                    sl.read(pm)
                    first = False
                xbuf.read(pm)
                g.wrote(pm)
                evac(gi, g, nm)

        def linear_tok(xbuf, nchunk, WB, ncolsW, groups, evac, wkey):
            wwait(wkey)
            for gi, pieces in enumerate(groups):
                g = pg.next()
                g.begin()
                sl = load_slab(WB, ncolsW, 0, KC, pieces, wkey)
                PE.wait(sl.w, xbuf.w, g.prev)
                for ci in range(nchunk):
                    for kc in range(KC):
                        ins = nc.tensor.matmul(g.t[:, ci, :], lhsT=xbuf.t[:, kc, ci * 128:(ci + 1) * 128],
                                               rhs=sl.t[:, kc, :], start=(kc == 0), stop=(kc == KC - 1))
                pm = PE.mark(ins)
                sl.read(pm)
                xbuf.read(pm)
                g.wrote(pm)
                evac(gi, g, nchunk)

        blocks = [(c0, min(512, T - c0)) for c0 in range(0, T, 512)]

        def retention_layer(j, li, hin, hout):
            tb = carve_at(ARENA - TBL, ARENA, [("lg", [16], F32), ("nlg", [16], F32), ("bq", [16], F32), ("t1", [128], F32),
                        ("t2", [128], F32), ("t3", [128], F32), ("dmt", [8, 128], F32),
                        ("fq", [16, 128], BF16), ("fb", [16, 128], BF16), ("dkf", [8], F32), ("dkb", [8], F32),
                        ("cd", [16], F32), ("cdkf", [NCH, 8], F32), ("cdkb", [NCH, 8], F32), ("pj", [2], F32),
                        ("fq32", [128], F32)])
            TB = Buf(work)
            TB.begin()
            d0 = j * 16
            for q in (ACT, DVE):
                q.wait(dec.w, kp.w, iof.w, iop.w)
            a1 = ACT.mark(nc.scalar.activation(out=tb["lg"][:, :], in_=dec.t[:, d0:d0 + 16], func=AF.Exp, scale=-1.0))
            ACT.wait(a1)
            a2 = ACT.mark(nc.scalar.activation(out=tb["nlg"][:, :], in_=tb["lg"][:, :], func=AF.Ln, bias=1.0))
            DVE.wait(a2)
            v1 = DVE.mark(nc.vector.tensor_scalar(out=tb["lg"][:, :], in0=tb["nlg"][:, :], scalar1=-1.0,
                                                  scalar2=None, op0=ALU.mult))
            DVE.wait(v1)
            nc.vector.tensor_scalar(out=tb["pj"][:, 0:1], in0=iop.t[:, 0:1], scalar1=-1.0, scalar2=127.0,
                                    op0=ALU.mult, op1=ALU.add)
            nc.vector.tensor_copy(out=tb["pj"][:, 1:2], in_=iop.t[:, 0:1])
            nc.vector.tensor_scalar(out=tb["bq"][:, 0:8], in0=tb["lg"][:, 0:8], scalar1=LN16, scalar2=None, op0=ALU.add)
            nc.vector.tensor_scalar(out=tb["bq"][:, 8:16], in0=tb["lg"][:, 8:16], scalar1=128.0, scalar2=LN16,
                                    op0=ALU.mult, op1=ALU.add)
            v2 = DVE.mark(nc.vector.tensor_scalar(out=tb["cd"][:, :], in0=tb["lg"][:, :], scalar1=128.0, scalar2=None,
                                                  op0=ALU.mult))
            DVE.wait(v2)
            nc.vector.tensor_scalar(out=tb["dkf"][:, :], in0=tb["lg"][:, 0:8], scalar1=tb["pj"][:, 0:1], scalar2=None,
                                    op0=ALU.mult)
            nc.vector.tensor_scalar(out=tb["dkb"][:, :], in0=tb["lg"][:, 8:16], scalar1=tb["pj"][:, 1:2], scalar2=None,
                                    op0=ALU.mult)
            v3 = DVE.mark(nc.vector.tensor_scalar(out=tb["t3"][:, :], in0=iof.t[:, :], scalar1=iop.t[:, 0:1],
                                                  scalar2=None, op0=ALU.subtract))
            DVE.wait(v3)
            nc.vector.tensor_scalar(out=tb["t1"][:, :], in0=tb["t3"][:, :], scalar1=0.0, scalar2=None, op0=ALU.max)
            v4 = DVE.mark(nc.vector.tensor_scalar(out=tb["t2"][:, :], in0=tb["t3"][:, :], scalar1=-1.0, scalar2=0.0,
                                                  op0=ALU.mult, op1=ALU.max))
            ACT.wait(v4)
            e1 = ACT.mark(nc.scalar.activation(out=tb["cd"][:, :], in_=tb["cd"][:, :], func=AF.Exp))
            nc.scalar.activation(out=tb["dkf"][:, :], in_=tb["dkf"][:, :], func=AF.Exp)
            e2 = ACT.mark(nc.scalar.activation(out=tb["dkb"][:, :], in_=tb["dkb"][:, :], func=AF.Exp))
            for h in range(H):
                DVE.wait(v4, e2)
                nc.vector.tensor_scalar(out=tb["t3"][:, :], in0=tb["t1"][:, :], scalar1=tb["lg"][:, h:h + 1],
                                        scalar2=None, op0=ALU.mult)
                vv = DVE.mark(nc.vector.tensor_scalar(out=tb["fq32"][:, :], in0=tb["t2"][:, :],
                                                      scalar1=tb["lg"][:, 8 + h:9 + h], scalar2=None, op0=ALU.mult))
                DVE.wait(vv)
                vv = DVE.mark(nc.vector.tensor_tensor(out=tb["t3"][:, :], in0=tb["t3"][:, :], in1=tb["fq32"][:, :],
                                                      op=ALU.add))
                ACT.wait(vv)
                e2 = ACT.mark(nc.scalar.activation(out=tb["dmt"][:, h, :], in_=tb["t3"][:, :], func=AF.Exp,
                                                   bias=LN16 if False else 0.0))
                for dcc in range(2):
                    nc.scalar.activation(out=tb["fq"][:, 2 * h + dcc, :], in_=iof.t[:, :], func=AF.Exp,
                                         scale=tb["lg"][:, h:h + 1], bias=tb["bq"][:, h:h + 1])
                    e2 = ACT.mark(nc.scalar.activation(out=tb["fb"][:, 2 * h + dcc, :], in_=iof.t[:, :], func=AF.Exp,
                                                       scale=tb["nlg"][:, 8 + h:9 + h], bias=tb["bq"][:, 8 + h:9 + h]))
            DVE.wait(e1, e2)
            vv = DVE.mark(nc.vector.tensor_scalar(out=tb["dmt"][:, :, :], in0=tb["dmt"][:, :, :], scalar1=1.0 / 16.0,
                                                  scalar2=None, op0=ALU.mult))
            for c in range(NCH):
                nc.vector.tensor_scalar(out=tb["cdkf"][:, c, :], in0=tb["cd"][:, 0:8], scalar1=kp.t[:, c:c + 1],
                                        scalar2=None, op0=ALU.mult)
                vv = DVE.mark(nc.vector.tensor_scalar(out=tb["cdkb"][:, c, :], in0=tb["cd"][:, 8:16],
                                                      scalar1=kp.t[:, NCH + 1 + c:NCH + 2 + c], scalar2=None,
                                                      op0=ALU.mult))
            TBM = vv
            for q in (ACT, DVE, POOL, PE):
                q.wait(TBM, e2)
            def carve2(spec, full=False):
                return carve_at(0 if full else COMMON, ARENA - TBL, spec)

            ra = carve2([("qo", [4, 16, 128], BF16), ("ko", [4, 16, 128], BF16),
                         ("vt0", [4, 512], BF16),
                         ("kf0", [4, 512], BF16),
                         ("kb0", [4, 512], BF16),
                         ("orot", [4, 2, 2, 128], F32), ("ta", [512], F32), ("tb", [512], F32),
                         ("tc", [512], F32), ("td", [512], F32),
                         ("cf", [512], F32), ("sf", [512], F32), ("ct", [4, 128], F32), ("st", [4, 128], F32)])
            qo = Buf(ra["qo"]); ko = Buf(ra["ko"])
            vtr = Ring([Buf(ra["vt0"])])
            kfr = Ring([Buf(ra["kf0"])])
            kbr = Ring([Buf(ra["kb0"])])
            orot = Buf(ra["orot"])
            tmp = Buf(ra["ta"])
            cf = Buf(ra["cf"], newds()); sf = Buf(ra["sf"], newds())
            ct = Buf(ra["ct"], newds()); st = Buf(ra["st"], newds())
            sd = {k: newds() for k in ("q", "k", "v", "g", "kf", "kb")}
            WB = WB_IN[j]
            for (c0, N) in blocks:
                nchunk = N // 128
                cb = c0 // 128
                pump(1)
                front(hin, c0, N, "g_pre_mix", li)
                for b_, src_ in ((cf, COSF), (sf, SINF)):
                    b_.begin()
                    load(b_, b_.t[:, 0:N], src_[:, c0:c0 + N])
                for b_, src_ in ((ct, COST), (st, SINT)):
                    b_.begin()
                    load(b_, b_.t[:, 0:nchunk, :], src_[c0:c0 + N, :].rearrange("(c p) f -> p c f", p=128))

                def ev_v(gi, g, nch_, dst=Vd, ring=vtr, func=AF.Copy, key="v"):
                    vt = ring.next()
                    vt.begin()
                    ACT.wait(g.w, vt.prev)
                    m = ACT.mark(nc.scalar.activation(out=vt.t[:, 0:nch_, :], in_=g.t[:, 0:nch_, :], func=func))
                    g.read(m)
                    vt.wrote(m)
                    store(vt, dst[c0:c0 + N, gi * 512:(gi + 1) * 512].rearrange("(c p) f -> p c f", p=128),
                          vt.t[:, 0:nch_, :], sd[key])

                linear_tok(hn, nchunk, WB, 12288, [[(4096 + 512 * gi, 512)] for gi in range(8)], ev_v, ("in", j))
                linear_tok(hn, nchunk, WB, 12288, [[(8192 + 512 * gi, 512)] for gi in range(8)],
                           lambda gi, g, n_: ev_v(gi, g, n_, Gd, vtr, AF.Silu, "g"), ("in", j))

                def ev_ktok(gi, g, nch_):
                    gv = g.t.rearrange("p c (a b f) -> p c a b f", a=2, b=2)
                    orot.begin()
                    tmp.begin()
                    DVE.wait(g.w, orot.prev, tmp.prev, ct.w, st.w)
                    last = None
                    for hh in range(2):
                        x1 = gv[:, 0:nch_, hh, 0, :]
                        x2 = gv[:, 0:nch_, hh, 1, :]
                        cs = ct.t[:, 0:nch_, :]
                        sn = st.t[:, 0:nch_, :]
                        ta = ra["ta"].rearrange("p (c f) -> p c f", c=4)[:, 0:nch_, :]
                        tbb = ra["tb"].rearrange("p (c f) -> p c f", c=4)[:, 0:nch_, :]
                        tcc = ra["tc"].rearrange("p (c f) -> p c f", c=4)[:, 0:nch_, :]
                        tdd = ra["td"].rearrange("p (c f) -> p c f", c=4)[:, 0:nch_, :]
                        DVE.wait(last)
                        nc.vector.tensor_tensor(out=ta, in0=x1, in1=cs, op=ALU.mult)
                        nc.vector.tensor_tensor(out=tbb, in0=x2, in1=sn, op=ALU.mult)
                        nc.vector.tensor_tensor(out=tcc, in0=x2, in1=cs, op=ALU.mult)
                        mm = DVE.mark(nc.vector.tensor_tensor(out=tdd, in0=x1, in1=sn, op=ALU.mult))
                        DVE.wait(mm)
                        nc.vector.tensor_tensor(out=orot.t[:, 0:nch_, hh, 0, :], in0=ta, in1=tbb, op=ALU.subtract)
                        last = DVE.mark(nc.vector.tensor_tensor(out=orot.t[:, 0:nch_, hh, 1, :], in0=tcc, in1=tdd,
                                                                op=ALU.add))
                    g.read(last)
                    ct.read(last); st.read(last)
                    orot.wrote(last)
                    tmp.wrote(last)
                    kf = kfr.next(); kb = kbr.next()
                    kf.begin(); kb.begin()
                    POOL.wait(orot.w, kf.prev, kb.prev, TBM)
                    for hh in range(2):
                        h = 2 * gi + hh
                        src_ = orot.t[:, 0:nch_, hh, :, :]
                        nc.gpsimd.tensor_scalar(out=kf.t.rearrange("p c (a b f) -> p c a b f", a=2, b=2)[:, 0:nch_, hh, :, :],
                                                in0=src_, scalar1=tb["dkf"][:, h:h + 1], scalar2=None, op0=ALU.mult)
                        mk = POOL.mark(nc.gpsimd.tensor_scalar(
                            out=kb.t.rearrange("p c (a b f) -> p c a b f", a=2, b=2)[:, 0:nch_, hh, :, :],
                            in0=src_, scalar1=tb["dkb"][:, h:h + 1], scalar2=None, op0=ALU.mult))
                    orot.read(mk)
                    kf.wrote(mk); kb.wrote(mk)
                    store(kf, KFd[c0:c0 + N, gi * 512:(gi + 1) * 512].rearrange("(c p) f -> p c f", p=128),
                          kf.t[:, 0:nch_, :], sd["kf"])
                    store(kb, KBd[c0:c0 + N, gi * 512:(gi + 1) * 512].rearrange("(c p) f -> p c f", p=128),
                          kb.t[:, 0:nch_, :], sd["kb"])

                linear_tok(hn, nchunk, WB, 12288, [[(2048 + 512 * gi, 512)] for gi in range(4)], ev_ktok, ("in", j))

                def mk_ev_fm(ob):
                    def ev(gi, g, nm):
                        xcv = ra["orot"].rearrange("p c a b f -> p c (a b f)")
                        orot.begin()
                        ACT.wait(g.w, orot.prev)
                        mkc = ACT.mark(nc.scalar.activation(out=xcv[:, 0:4, 0:N], in_=g.t[:, 0:4, 0:N], func=AF.Copy))
                        g.read(mkc)
                        orot.wrote(mkc)
                        tmp.begin()
                        DVE.wait(mkc, tmp.prev, cf.w, sf.w, ob.prev)
                        last = None
                        for hh in range(2):
                            x1 = xcv[:, 2 * hh, 0:N]
                            x2 = xcv[:, 2 * hh + 1, 0:N]
                            DVE.wait(last)
                            nc.vector.tensor_tensor(out=ra["ta"][:, 0:N], in0=x1, in1=cf.t[:, 0:N], op=ALU.mult)
                            nc.vector.tensor_tensor(out=ra["tb"][:, 0:N], in0=x2, in1=sf.t[:, 0:N], op=ALU.mult)
                            nc.vector.tensor_tensor(out=ra["tc"][:, 0:N], in0=x2, in1=cf.t[:, 0:N], op=ALU.mult)
                            mm = DVE.mark(nc.vector.tensor_tensor(out=ra["td"][:, 0:N], in0=x1, in1=sf.t[:, 0:N],
                                                                  op=ALU.mult))
                            DVE.wait(mm)
                            kc1 = gi * 4 + 2 * hh
                            nc.vector.tensor_tensor(out=ob.t[:, 0:nchunk, kc1, :],
                                                    in0=ra["ta"][:, 0:N].rearrange("p (c f) -> p c f", f=128),
                                                    in1=ra["tb"][:, 0:N].rearrange("p (c f) -> p c f", f=128),
                                                    op=ALU.subtract)
                            last = DVE.mark(nc.vector.tensor_tensor(
                                out=ob.t[:, 0:nchunk, kc1 + 1, :],
                                in0=ra["tc"][:, 0:N].rearrange("p (c f) -> p c f", f=128),
                                in1=ra["td"][:, 0:N].rearrange("p (c f) -> p c f", f=128), op=ALU.add))
                        orot.read(last)
                        tmp.wrote(last)
                        ob.wrote(last)
                    return ev

                qo.begin(); ko.begin()
                linear_fm(lambda kc: hn.t[:, kc, 0:N], hn, KC, WB, 12288,
                          [[(512 * gi, 512)] for gi in range(4)], N, mk_ev_fm(qo), ("in", j))
                linear_fm(lambda kc: hn.t[:, kc, 0:N], hn, KC, WB, 12288,
                          [[(2048 + 512 * gi, 512)] for gi in range(4)], N, mk_ev_fm(ko), ("in", j))
                cf.read(DVE.mark(nc.vector.engine_nop())) if False else None
                for b_ in (cf, sf):
                    b_.read((DVE, DVE.n))
                store(qo, QTd[cb:cb + nchunk].rearrange("c p (k f) -> p c k f", k=16), qo.t[:, 0:nchunk, :, :], sd["q"])
                store(ko, KTd[cb:cb + nchunk].rearrange("c p (k f) -> p c k f", k=16), ko.t[:, 0:nchunk, :, :], sd["k"])
            phase_sync_all()

            rb = carve2(full=True, spec=[("S", [16, 512], F32), ("so", [16, 512], BF16), ("kb0", [2048], BF16), ("kb1", [2048], BF16),
                         ("v0", [4096], BF16), ("v1", [4096], BF16)])
            S = Buf(rb["S"]); so = Buf(rb["so"])
            kbl = Ring([Buf(rb["kb0"], newds()), Buf(rb["kb1"], newds())])
            vl = Ring([Buf(rb["v0"], newds()), Buf(rb["v1"], newds())])
            so_sds = newds()
            S.begin()
            DVE.wait(S.prev)
            S.wrote(DVE.mark(nc.vector.memset(S.t[:, :, :], 0.0)))

            def state_update(S, kx, vx, cdk, c):
                for hg in range(4):
                    g = pg.next()
                    g.begin()
                    PE.wait(kx.w, vx.w, g.prev)
                    for hl in range(2):
                        h = 2 * hg + hl
                        for dcc in range(2):
                            ins = nc.tensor.matmul(g.t[:, 2 * hl + dcc, :],
                                                   lhsT=kx.t[:, h * 256 + dcc * 128:h * 256 + (dcc + 1) * 128],
                                                   rhs=vx.t[:, h * 512:(h + 1) * 512], start=True, stop=True)
                    pm = PE.mark(ins)
                    g.wrote(pm)
                    kx.read(pm); vx.read(pm)
                    DVE.wait(pm, S.w, S.r)
                    for hl in range(2):
                        h = 2 * hg + hl
                        ins = nc.vector.scalar_tensor_tensor(out=S.t[:, 2 * h:2 * h + 2, :], in0=S.t[:, 2 * h:2 * h + 2, :],
                                                             scalar=cdk[:, c, h:h + 1],
                                                             in1=g.t[:, 2 * hl:2 * hl + 2, :], op0=ALU.mult, op1=ALU.add)
                    m = DVE.mark(ins)
                    g.read(m)
                    S.wrote(m)

            for c in range(NCH - 1, -1, -1):
                kx = kbl.next(); vx = vl.next()
                kx.begin(); vx.begin()
                load(kx, kx.t[:, :], KBd[c * 128:(c + 1) * 128, :])
                load(vx, vx.t[:, :], Vd[c * 128:(c + 1) * 128, :])
                so.begin()
                ACT.wait(S.w, so.prev)
                for q4 in range(4):
                    ins = nc.scalar.activation(out=so.t[:, 4 * q4:4 * q4 + 4, :], in_=S.t[:, 4 * q4:4 * q4 + 4, :],
                                               func=AF.Copy, scale=kp.t[:, NCH + 1 + c:NCH + 2 + c])
                m = ACT.mark(ins)
                so.wrote(m)
                S.read(m)
                store(so, SBd[c], so.t.rearrange("p a b -> p (a b)"), so_sds)
                state_update(S, kx, vx, tb["cdkb"], c)
            phase_sync_all()

            rc = carve2(full=True, spec=[("S", [16, 512], F32), ("sfb", [16, 512], BF16), ("sbt", [16, 512], BF16),
                         ("qt0", [16, 128], BF16), ("qt1", [16, 128], BF16), ("kt0", [16, 128], BF16),
                         ("kt1", [16, 128], BF16), ("kf0", [2048], BF16), ("kf1", [2048], BF16),
                         ("v0", [4096], BF16), ("g0", [4096], BF16),
                         ("qf", [16, 128], BF16), ("qb", [16, 128], BF16), ("sc", [8, 128], BF16),
                         ("on", [4, 512], F32), ("gated", [4096], BF16), ("gT", [32, 128], BF16),
                         ("stt", [8, 6], F32), ("mv", [8, 2], F32), ("rs", [8], F32), ("nb", [8], F32)])
            S = Buf(rc["S"]); sfb = Buf(rc["sfb"]); sbt = Buf(rc["sbt"], newds())
            qtl = Ring([Buf(rc["qt0"], newds()), Buf(rc["qt1"], newds())])
            ktl = Ring([Buf(rc["kt0"], newds()), Buf(rc["kt1"], newds())])
            kfl = Ring([Buf(rc["kf0"], newds()), Buf(rc["kf1"], newds())])
            vl = Ring([Buf(rc["v0"], newds())])
            gl = Ring([Buf(rc["g0"], newds())])
            qf = Buf(rc["qf"]); qb = Buf(rc["qb"]); sc = Buf(rc["sc"]); on = Buf(rc["on"])
            gated = Buf(rc["gated"]); gT = Buf(rc["gT"]); stt = Buf(rc["stt"])
            gt_sds = newds()
            S.begin(); sfb.begin()
            DVE.wait(S.prev, sfb.prev)
            S.wrote(DVE.mark(nc.vector.memset(S.t[:, :, :], 0.0)))
            sfb.wrote(DVE.mark(nc.vector.memset(sfb.t[:, :, :], 0.0)))
            for c in range(NCH):
                qt = qtl.next(); kt = ktl.next(); kx = kfl.next(); vx = vl.next(); gx = gl.next()
                for b_, src_ in ((qt, QTd[c]), (kt, KTd[c])):
                    b_.begin()
                    load(b_, b_.t.rearrange("p a b -> p (a b)"), src_)
                kx.begin(); load(kx, kx.t[:, :], KFd[c * 128:(c + 1) * 128, :])
                vx.begin(); load(vx, vx.t[:, :], Vd[c * 128:(c + 1) * 128, :])
                gx.begin(); load(gx, gx.t[:, :], Gd[c * 128:(c + 1) * 128, :])
                sbt.begin(); load(sbt, sbt.t.rearrange("p a b -> p (a b)"), SBd[c])
                qf.begin(); qb.begin()
                POOL.wait(qt.w, qf.prev, qb.prev)
                nc.gpsimd.tensor_tensor(out=qf.t[:, :, :], in0=qt.t[:, :, :], in1=tb["fq"][:, :, :], op=ALU.mult)
                m = POOL.mark(nc.gpsimd.tensor_tensor(out=qb.t[:, :, :], in0=qt.t[:, :, :], in1=tb["fb"][:, :, :],
                                                      op=ALU.mult))
                qf.wrote(m); qb.wrote(m); qt.read(m)
                g = pg.next(); g.begin()
                PE.wait(qt.w, kt.w, g.prev)
                for h in range(H):
                    for dcc in range(2):
                        ins = nc.tensor.matmul(g.t[:, h // 4, (h % 4) * 128:(h % 4 + 1) * 128],
                                               lhsT=kt.t[:, 2 * h + dcc, :], rhs=qt.t[:, 2 * h + dcc, :],
                                               start=(dcc == 0), stop=(dcc == 1))
                pm = PE.mark(ins)
                g.wrote(pm); qt.read(pm); kt.read(pm)
                sc.begin()
                DVE.wait(pm, sc.prev)
                for half in range(2):
                    ins = nc.vector.tensor_tensor(out=sc.t[:, 4 * half:4 * half + 4, :],
                                                  in0=g.t[:, half, :].rearrange("p (a b) -> p a b", a=4),
                                                  in1=tb["dmt"][:, 4 * half:4 * half + 4, :], op=ALU.mult)
                m = DVE.mark(ins)
                g.read(m); sc.wrote(m)
                gated.begin()
                for hg in range(2):
                    g = pg.next(); g.begin()
                    PE.wait(sc.w, vx.w, qf.w, qb.w, sfb.w, sbt.w, g.prev)
                    for hl in range(4):
                        h = 4 * hg + hl
                        nc.tensor.matmul(g.t[:, hl, :], lhsT=sc.t[:, h, :], rhs=vx.t[:, h * 512:(h + 1) * 512],
                                         start=True, stop=False)
                        for dcc in range(2):
                            nc.tensor.matmul(g.t[:, hl, :], lhsT=qf.t[:, 2 * h + dcc, :], rhs=sfb.t[:, 2 * h + dcc, :],
                                             start=False, stop=False)
                        for dcc in range(2):
                            ins = nc.tensor.matmul(g.t[:, hl, :], lhsT=qb.t[:, 2 * h + dcc, :],
                                                   rhs=sbt.t[:, 2 * h + dcc, :], start=False, stop=(dcc == 1))
                    pm = PE.mark(ins)
                    g.wrote(pm)
                    for b_ in (sc, vx, qf, qb, sfb, sbt):
                        b_.read(pm)
                    stt.begin()
                    DVE.wait(pm, stt.prev)
                    for hl in range(4):
                        ins = nc.vector.bn_stats(out=rc["stt"][:, hl, :], in_=g.t[:, hl, :])
                    m = DVE.mark(ins)
                    DVE.wait(m)
                    for hl in range(4):
                        ins = nc.vector.bn_aggr(out=rc["mv"][:, hl, :], in_=rc["stt"][:, hl, :])
                    m = DVE.mark(ins)
                    DVE.wait(m)
                    ACT.wait(m)
                    m = ACT.mark(nc.scalar.activation(out=rc["rs"][:, 0:4], in_=rc["mv"][:, 0:4, 1], func=AF.Sqrt,
                                                      bias=EPS))
                    DVE.wait(m)
                    nc.vector.reciprocal(out=rc["rs"][:, 0:4], in_=rc["rs"][:, 0:4])
                    m = DVE.mark(nc.vector.tensor_copy(out=rc["rs"][:, 4:8], in_=rc["mv"][:, 0:4, 0]))
                    DVE.wait(m)
                    m = DVE.mark(nc.vector.scalar_tensor_tensor(out=rc["nb"][:, 0:4], in0=rc["rs"][:, 4:8], scalar=-1.0,
                                                                in1=rc["rs"][:, 0:4], op0=ALU.mult, op1=ALU.mult))
                    stt.wrote(m)
                    on.begin()
                    ACT.wait(m, on.prev)
                    for hl in range(4):
                        ins = nc.scalar.activation(out=on.t[:, hl, :], in_=g.t[:, hl, :], func=AF.Identity,
                                                   scale=rc["rs"][:, hl:hl + 1], bias=rc["nb"][:, hl:hl + 1])
                    m = ACT.mark(ins)
                    g.read(m); stt.read(m); on.wrote(m)
                    POOL.wait(m, gx.w, gated.prev)
                    m = POOL.mark(nc.gpsimd.tensor_tensor(
                        out=gated.t[:, hg * 2048:(hg + 1) * 2048].rearrange("p (a b) -> p a b", a=4),
                        in0=on.t[:, :, :], in1=gx.t[:, hg * 2048:(hg + 1) * 2048].rearrange("p (a b) -> p a b", a=4),
                        op=ALU.mult))
                    on.read(m); gx.read(m); gated.wrote(m)
                state_update(S, kx, vx, tb["cdkf"], c)
                sfb.begin()
                ACT.wait(S.w, sfb.prev)
                for q4 in range(4):
                    ins = nc.scalar.activation(out=sfb.t[:, 4 * q4:4 * q4 + 4, :], in_=S.t[:, 4 * q4:4 * q4 + 4, :],
                                               func=AF.Copy, scale=kp.t[:, c + 1:c + 2])
                m = ACT.mark(ins)
                sfb.wrote(m); S.read(m)
                g = pg.next(); g.begin()
                gb = g.t.rearrange("p a b -> p (a b)").bitcast(BF16).rearrange("p (a b) -> p a b", a=32)
                PE.wait(gated.w, g.prev, ident.w)
                for ec in range(32):
                    ins = nc.tensor.transpose(out=gb[:, ec, 0:128], in_=gated.t[:, ec * 128:(ec + 1) * 128],
                                              identity=ident.t[:, :])
                pm = PE.mark(ins)
                g.wrote(pm); gated.read(pm)
                gT.begin()
                ACT.wait(pm, gT.prev)
                DVE.wait(pm, gT.prev)
                m1 = ACT.mark(nc.scalar.activation(out=gT.t[:, 0:16, :], in_=gb[:, 0:16, 0:128], func=AF.Copy))
                m2 = DVE.mark(nc.vector.tensor_copy(out=gT.t[:, 16:32, :], in_=gb[:, 16:32, 0:128]))
                g.read(m1); g.read(m2); gT.wrote(m1); gT.wrote(m2)
                store(gT, GTd[c], gT.t.rearrange("p a b -> p (a b)"), gt_sds)
            phase_sync_all()

            rd = carve2([("gt", [4, 32, 128], BF16)])
            gtb = Buf(rd["gt"], newds())
            for (c0, N) in blocks:
                nchunk = N // 128
                cb = c0 // 128
                pump(1)
                gtb.begin()
                for ci in range(nchunk):
                    load(gtb, gtb.t[:, ci, :, :].rearrange("p e f -> p (e f)"), GTd[cb + ci])
                xm.begin()

                def ev_o(gi, g, nm):
                    ACT.wait(g.w, xm.prev)
                    for m_ in range(nm):
                        ins = nc.scalar.activation(out=xm.t[:, gi * 4 + m_, 0:N], in_=g.t[:, m_, 0:N], func=AF.Copy)
                    mk = ACT.mark(ins)
                    g.read(mk); xm.wrote(mk)

                linear_fm(lambda kc: gtb.t[:, 0:nchunk, kc, :], gtb, 32, WB_OUT[j], D,
                          [[(512 * gi, 512)] for gi in range(4)], N, ev_o, ("out", j))
                back(N, c0, "g_post_mix", li, hin, hout, False)
            phase_sync_all()

        def conformer_layer(j, li, hin, hout):
            HALO = 15
            cw = carve([("u", [KC, 512], BF16), ("sig", [4, 512], F32), ("dg0", [CK, 128], BF16), ("dg1", [CK, 128], BF16)])
            ub = Buf(cw["u"]); sig = Buf(cw["sig"])
            u_sds = newds()
            dgr = Ring([Buf(cw["dg0"]), Buf(cw["dg1"])])
            for b_ in dgr.b:
                b_.sds = newds()
            wc0 = pc[("w_dw", j)]
            for kc in range(KC):
                t_ = dgr.next()
                t_.begin()
                DVE.wait(t_.prev, ident.w, prm.w)
                for k in range(CK):
                    ins = nc.vector.tensor_scalar(out=t_.t[:, k, :], in0=ident.t[:, :],
                                                  scalar1=prm.t[:, wc0 + kc * CK + k:wc0 + kc * CK + k + 1],
                                                  scalar2=None, op0=ALU.mult)
                t_.wrote(DVE.mark(ins))
                store(t_, DG[j][:, kc * CK * 128:(kc + 1) * CK * 128], t_.t.rearrange("p a b -> p (a b)"), t_.sds)
            for (c0, N) in blocks:
                pump(1)
                front(hin, c0, N, "g_pre_mix", li)
                ub.begin()
                gate_g = {}

                def ev_glu(gi, g, nm):
                    sig.begin()
                    ACT.wait(g.w, sig.prev)
                    for m_ in range(2):
                        kc = gi * 2 + m_
                        ins = nc.scalar.activation(out=sig.t[:, m_, 0:N], in_=g.t[:, 2 + m_, 0:N], func=AF.Sigmoid,
                                                   bias=P("b_pw1", j, KC + kc))
                    mk = ACT.mark(ins)
                    sig.wrote(mk)
                    DVE.wait(mk, ub.prev)
                    for m_ in range(2):
                        kc = gi * 2 + m_
                        ins = nc.vector.scalar_tensor_tensor(out=ub.t[:, kc, 0:N], in0=g.t[:, m_, 0:N],
                                                             scalar=P("b_pw1", j, kc), in1=sig.t[:, m_, 0:N],
                                                             op0=ALU.add, op1=ALU.mult)
                    mk2 = DVE.mark(ins)
                    for si, scn in enumerate(SPECIAL):
                        lo = scn * 128 - c0
                        if 0 <= lo < N:
                            DVE.wait(mk2, msk.w)
                            mk2 = DVE.mark(nc.vector.tensor_tensor(
                                out=ub.t[:, gi * 2:gi * 2 + 2, lo:lo + 128], in0=ub.t[:, gi * 2:gi * 2 + 2, lo:lo + 128],
                                in1=msk.t[:, si * 128:(si + 1) * 128].rearrange("p (a f) -> p a f", a=1).broadcast_to([128, 2, 128])
                                if False else msk.t[:, si * 128:(si + 1) * 128], op=ALU.mult)) if False else mk2
                            for m_ in range(2):
                                kc = gi * 2 + m_
                                mk2 = DVE.mark(nc.vector.tensor_tensor(out=ub.t[:, kc, lo:lo + 128],
                                                                       in0=ub.t[:, kc, lo:lo + 128],
                                                                       in1=msk.t[:, si * 128:(si + 1) * 128], op=ALU.mult))
                                DVE.wait(mk2)
                    g.read(mk2); sig.read(mk2); ub.wrote(mk2)

                linear_fm(lambda kc: hn.t[:, kc, 0:N], hn, KC, WB_PW1[j], 2 * D,
                          [[(256 * gi, 256), (D + 256 * gi, 256)] for gi in range(8)], N, ev_glu, ("pw1", j))
                store(ub, UT.rearrange("(kc p) t -> p kc t", p=128)[:, :, c0:c0 + N], ub.t[:, :, 0:N], u_sds)
            phase_sync_all()

            W_ = 512 + 2 * HALO
            cw = carve([("uh", [KC, W_], BF16), ("mu", [512], F32), ("ex2", [512], F32), ("xh", [KC, 512], BF16),
                        ("t", [512], F32)])
            uh = Buf(cw["uh"], newds()); mu = Buf(cw["mu"]); xh = Buf(cw["xh"]); tt = Buf(cw["t"])
            y = xm
            UTv = UT.rearrange("(kc p) t -> p kc t", p=128)
            for (c0, N) in blocks:
                pump(1)
                lo = max(c0 - HALO, 0)
                hi = min(c0 + N + HALO, T)
                uh.begin()
                o0 = lo - (c0 - HALO)
                if o0 > 0:
                    DVE.wait(uh.prev)
                    uh.wrote(DVE.mark(nc.vector.memset(uh.t[:, :, 0:o0], 0.0)))
                if hi < c0 + N + HALO:
                    DVE.wait(uh.prev)
                    uh.wrote(DVE.mark(nc.vector.memset(uh.t[:, :, o0 + hi - lo:N + 2 * HALO], 0.0)))
                load(uh, uh.t[:, :, o0:o0 + hi - lo], UTv[:, :, lo:hi])
                y.begin()
                for g4 in range(4):
                    g = pg.next()
                    g.begin()
                    for half in range(2):
                        kc0 = g4 * 4 + half * 2
                        sl = wring.next()
                        sl.begin()
                        slf = sl.t.rearrange("p a b -> p (a b)")
                        load(sl, slf[:, 0:2 * CK * 128], DG[j][:, kc0 * CK * 128:(kc0 + 2) * CK * 128])
                        PE.wait(sl.w, uh.w)
                        if half == 0:
                            PE.wait(g.prev)
                        for kk in range(2):
                            kc = kc0 + kk
                            for k in range(CK):
                                ins = nc.tensor.matmul(g.t[:, half * 2 + kk, 0:N],
                                                       lhsT=slf[:, (kk * CK + k) * 128:(kk * CK + k + 1) * 128],
                                                       rhs=uh.t[:, kc, k:k + N], start=(k == 0), stop=(k == CK - 1))
                        pm = PE.mark(ins)
                        sl.read(pm)
                    g.wrote(pm)
                    uh.read(pm)
                    ACT.wait(pm, y.prev, prm.w)
                    for m_ in range(4):
                        kc = g4 * 4 + m_
                        ins = nc.scalar.activation(out=y.t[:, kc, 0:N], in_=g.t[:, m_, 0:N], func=AF.Identity,
                                                   bias=P("b_dw", j, kc))
                    mk = ACT.mark(ins)
                    g.read(mk)
                    y.wrote(mk)
                hn.begin()
                xh.begin()
                ACT.wait(y.w, hn.prev, xh.prev)
                for kc in range(KC):
                    nc.scalar.activation(out=hn.t[:, kc, 0:N], in_=y.t[:, kc, 0:N], func=AF.Square)
                    ins = nc.scalar.activation(out=xh.t[:, kc, 0:N], in_=y.t[:, kc, 0:N], func=AF.Copy)
                mk = ACT.mark(ins)
                hn.wrote(mk); xh.wrote(mk)
                g = pg.next(); g.begin()
                PE.wait(mk, g.prev)
                for kc in range(KC):
                    nc.tensor.matmul(g.t[:, 0, 0:N], lhsT=ones.t[:], rhs=xh.t[:, kc, 0:N], start=(kc == 0),
                                     stop=(kc == KC - 1))
                for kc in range(KC):
                    ins = nc.tensor.matmul(g.t[:, 1, 0:N], lhsT=ones.t[:], rhs=hn.t[:, kc, 0:N], start=(kc == 0),
                                           stop=(kc == KC - 1))
                pm = PE.mark(ins)
                hn.read(pm); xh.read(pm); g.wrote(pm)
                mu.begin(); rstd.begin(); tt.begin()
                DVE.wait(pm, mu.prev, rstd.prev, tt.prev)
                nc.vector.tensor_scalar(out=mu.t[:, 0:N], in0=g.t[:, 0, 0:N], scalar1=1.0 / D, scalar2=None, op0=ALU.mult)
                m1 = DVE.mark(nc.vector.tensor_scalar(out=cw["ex2"][:, 0:N], in0=g.t[:, 1, 0:N], scalar1=1.0 / D,
                                                      scalar2=EPS, op0=ALU.mult, op1=ALU.add))
                DVE.wait(m1)
                m1 = DVE.mark(nc.vector.tensor_tensor(out=tt.t[:, 0:N], in0=mu.t[:, 0:N], in1=mu.t[:, 0:N], op=ALU.mult))
                DVE.wait(m1)
                m1 = DVE.mark(nc.vector.tensor_tensor(out=rstd.t[:, 0:N], in0=cw["ex2"][:, 0:N], in1=tt.t[:, 0:N],
                                                      op=ALU.subtract))
                ACT.wait(m1)
                m1 = ACT.mark(nc.scalar.activation(out=rstd.t[:, 0:N], in_=rstd.t[:, 0:N], func=AF.Sqrt))
                DVE.wait(m1)
                m1 = DVE.mark(nc.vector.reciprocal(out=rstd.t[:, 0:N], in_=rstd.t[:, 0:N]))
                g.read(m1); mu.wrote(m1); rstd.wrote(m1)
                xh.begin()
                lastd = m1
                for kc in range(KC):
                    DVE.wait(lastd, y.w)
                    ma = DVE.mark(nc.vector.tensor_tensor(out=y.t[:, kc, 0:N], in0=y.t[:, kc, 0:N], in1=mu.t[:, 0:N],
                                                          op=ALU.subtract))
                    DVE.wait(ma)
                    lastd = DVE.mark(nc.vector.tensor_tensor(out=y.t[:, kc, 0:N], in0=y.t[:, kc, 0:N],
                                                             in1=rstd.t[:, 0:N], op=ALU.mult))
                    ACT.wait(lastd, xh.prev)
                    mk = ACT.mark(nc.scalar.activation(out=xh.t[:, kc, 0:N], in_=y.t[:, kc, 0:N], func=AF.Silu,
                                                       scale=P("ln_g", j, kc), bias=P("ln_b", j, kc)))
                xh.wrote(mk); y.read(mk); mu.read(lastd); rstd.read(lastd)
                xm.begin()

                def ev_p2(gi, g, nm):
                    ACT.wait(g.w, xm.prev)
                    for m_ in range(nm):
                        kc = gi * 4 + m_
                        ins = nc.scalar.activation(out=xm.t[:, kc, 0:N], in_=g.t[:, m_, 0:N], func=AF.Identity,
                                                   bias=P("b_pw2", j, kc))
                    mk_ = ACT.mark(ins)
                    g.read(mk_); xm.wrote(mk_)

                linear_fm(lambda kc: xh.t[:, kc, 0:N], xh, KC, WB_PW2[j], D,
                          [[(512 * gi, 512)] for gi in range(4)], N, ev_p2, ("pw2", j))
                back(N, c0, "g_post_mix", li, hin, hout, True)
            phase_sync_all()

        def ffn_layer(li, hin, hout):
            fw = carve_at(COMMON, ARENA, [("act", [FC, 512], BF16), ("a0", [2, 514], F32), ("a1", [2, 514], F32),
                                          ("y", [2, 512], F32), ("hn2", [KC, 512], BF16), ("rstf", [512], F32)]
                          + [(f"hf{i}", [512], F32) for i in range(7)])
            hres = Ring([Buf(fw[f"hf{i}"], hresf_ds[i]) for i in range(7)])
            act = Buf(fw["act"])
            ar = Ring([Buf(fw["a0"]), Buf(fw["a1"])])
            yb = Buf(fw["y"])
            hns = [hn, Buf(fw["hn2"])]
            rstf = Buf(fw["rstf"])
            fblocks = [(s_, min(s_ + 510, T)) for s_ in range(0, T, 510)]
            NB = len(fblocks)
            hs = hin.rearrange("(kc p) t -> p kc t", p=128)

            def geom(bi):
                s0, e0 = fblocks[bi]
                lo = max(s0 - 1, 0)
                hi = min(e0 + 1, T)
                return dict(s0=s0, e0=e0, lo=lo, hi=hi, N=hi - lo, NO=e0 - s0, o0=lo - (s0 - 1))

            fst = {}

            def front_A(bi, kcs):
                ge = geom(bi)
                H_ = hns[bi % 2]
                if kcs[0] == 0:
                    H_.begin()
                    fst["fa"] = None
                last = fst["fa"]
                for kc in kcs:
                    hb = hres.next()
                    hb.begin()
                    load(hb, hb.t[:, 0:ge["N"]], hs[:, kc, ge["lo"]:ge["hi"]])
                    ACT.wait(hb.w, H_.prev)
                    last = ACT.mark(nc.scalar.activation(out=H_.t[:, kc, 0:ge["N"]], in_=hb.t[:, 0:ge["N"]],
                                                         func=AF.Square))
                    hb.read(last)
                fst["fa"] = last
                if kcs[-1] == KC - 1:
                    H_.wrote(last)

            def front_B(bi):
                ge = geom(bi)
                H_ = hns[bi % 2]
                g = pg.next()
                g.begin()
                PE.wait(H_.w, g.prev, ones.w)
                for kc in range(KC):
                    ins = nc.tensor.matmul(g.t[:, 0, 0:ge["N"]], lhsT=ones.t[:], rhs=H_.t[:, kc, 0:ge["N"]],
                                           start=(kc == 0), stop=(kc == KC - 1))
                pm = PE.mark(ins)
                H_.read(pm)
                g.wrote(pm)
                rstf.begin()
                ACT.wait(pm, rstf.prev)
                m0 = ACT.mark(nc.scalar.activation(out=rstf.t[:, 0:ge["N"]], in_=g.t[:, 0, 0:ge["N"]], func=AF.Sqrt,
                                                   scale=1.0 / D, bias=EPS))
                g.read(m0)
                DVE.wait(m0)
                m1 = DVE.mark(nc.vector.reciprocal(out=rstf.t[:, 0:ge["N"]], in_=rstf.t[:, 0:ge["N"]]))
                rstf.wrote(m1)

            def front_C(bi, kcs):
                ge = geom(bi)
                H_ = hns[bi % 2]
                if kcs[0] == 0:
                    H_.begin()
                    fst["fc"] = None
                last = fst["fc"]
                for kc in kcs:
                    hb = hres.next()
                    hb.begin()
                    load(hb, hb.t[:, 0:ge["N"]], hs[:, kc, ge["lo"]:ge["hi"]])
                    DVE.wait(hb.w, H_.prev, rstf.w, prm.w)
                    last = DVE.mark(nc.vector.scalar_tensor_tensor(out=H_.t[:, kc, 0:ge["N"]], in0=hb.t[:, 0:ge["N"]],
                                                                   scalar=P("g_pre_ffn", li, kc), in1=rstf.t[:, 0:ge["N"]],
                                                                   op0=ALU.mult, op1=ALU.mult))
                    hb.read(last)
                fst["fc"] = last
                if kcs[-1] == KC - 1:
                    H_.wrote(last)
                    rstf.read(last)

            def back_A(bi):
                ge = geom(bi)
                S_ = hns[bi % 2]
                S_.begin()
                ACT.wait(xm.w, S_.prev)
                for kc in range(KC):
                    ins = nc.scalar.activation(out=S_.t[:, kc, 0:ge["NO"]], in_=xm.t[:, kc, 0:ge["NO"]], func=AF.Square)
                m = ACT.mark(ins)
                S_.wrote(m)
                xm.read(m)

            def back_B(bi):
                ge = geom(bi)
                S_ = hns[bi % 2]
                g = pg.next()
                g.begin()
                PE.wait(S_.w, g.prev)
                for kc in range(KC):
                    ins = nc.tensor.matmul(g.t[:, 0, 0:ge["NO"]], lhsT=ones.t[:], rhs=S_.t[:, kc, 0:ge["NO"]],
                                           start=(kc == 0), stop=(kc == KC - 1))
                pm = PE.mark(ins)
                S_.read(pm)
                g.wrote(pm)
                rstd.begin()
                ACT.wait(pm, rstd.prev)
                m0 = ACT.mark(nc.scalar.activation(out=rstd.t[:, 0:ge["NO"]], in_=g.t[:, 0, 0:ge["NO"]], func=AF.Sqrt,
                                                   scale=1.0 / D, bias=EPS))
                g.read(m0)
                DVE.wait(m0)
                m1 = DVE.mark(nc.vector.reciprocal(out=rstd.t[:, 0:ge["NO"]], in_=rstd.t[:, 0:ge["NO"]]))
                rstd.wrote(m1)

            def back_C(bi, kcs):
                ge = geom(bi)
                NO = ge["NO"]
                if kcs[0] == 0:
                    fst["bc"] = None
                last = fst["bc"]
                for kc in kcs:
                    hb = hres.next()
                    hb.begin()
                    load(hb, hb.t[:, 0:NO], hs[:, kc, ge["s0"]:ge["e0"]])
                    DVE.wait(rstd.w, hb.w, xm.w, last)
                    ma = DVE.mark(nc.vector.scalar_tensor_tensor(out=xm.t[:, kc, 0:NO], in0=xm.t[:, kc, 0:NO],
                                                                 scalar=P("g_post_ffn", li, kc), in1=rstd.t[:, 0:NO],
                                                                 op0=ALU.mult, op1=ALU.mult))
                    DVE.wait(ma)
                    last = DVE.mark(nc.vector.tensor_tensor(out=xm.t[:, kc, 0:NO], in0=xm.t[:, kc, 0:NO],
                                                            in1=hb.t[:, 0:NO], op=ALU.add))
                    hb.read(last)
                fst["bc"] = last
                if kcs[-1] == KC - 1:
                    xm.wrote(last)
                    rstd.read(last)
                    store(xm, hout.rearrange("(kc p) t -> p kc t", p=128)[:, :, ge["s0"]:ge["e0"]], xm.t[:, :, 0:NO],
                          xm_sds)

            groups = []
            for gi in range((FC + 1) // 2):
                nj = min(2, FC - 2 * gi)
                groups.append([(256 * gi, 128 * nj), (FH + 256 * gi, 128 * nj)])

            front_A(0, list(range(KC))); front_B(0); front_C(0, list(range(KC)))
            for bi in range(NB):
                pump(1)
                ge = geom(bi)
                N = ge["N"]; NO = ge["NO"]; o0 = ge["o0"]; s0 = ge["s0"]; lo = ge["lo"]
                H_ = hns[bi % 2]
                act.begin()

                def ev_up(gi, g, nm, N=N, NO=NO, o0=o0, s0=s0, lo=lo):
                    nj = nm // 2
                    ab = ar.next()
                    ab.begin()
                    ACT.wait(g.w, ab.prev)
                    if o0 > 0:
                        nc.scalar.activation(out=ab.t[:, 0:nj, 0:1], in_=ab.t[:, 0:nj, 0:1], func=AF.Copy, scale=0.0)
                    if o0 + N < NO + 2:
                        nc.scalar.activation(out=ab.t[:, 0:nj, o0 + N:NO + 2], in_=ab.t[:, 0:nj, o0 + N:NO + 2],
                                             func=AF.Copy, scale=0.0)
                    mk = ACT.mark(nc.scalar.activation(out=ab.t[:, 0:nj, o0:o0 + N], in_=g.t[:, 0:nj, 0:N], func=AF.Copy))
                    ab.wrote(mk)
                    if nj == 2:
                        g.h[0].read(mk)
                    yb.begin()
                    DVE.wait(mk, yb.prev, act.prev, prm.w)
                    wl = pc[("ffn_w_dw", li)]
                    ms = [None] * nj
                    for jj in range(nj):
                        jg = gi * 2 + jj
                        ms[jj] = DVE.mark(nc.vector.tensor_scalar(out=yb.t[:, jj, 0:NO], in0=ab.t[:, jj, 0:NO],
                                                                  scalar1=prm.t[:, wl + jg * 3:wl + jg * 3 + 1],
                                                                  scalar2=P("ffn_b_dw", li, jg), op0=ALU.mult, op1=ALU.add))
                    for tap in (1, 2):
                        for jj in range(nj):
                            jg = gi * 2 + jj
                            DVE.wait(ms[jj])
                            ms[jj] = DVE.mark(nc.vector.scalar_tensor_tensor(
                                out=yb.t[:, jj, 0:NO], in0=ab.t[:, jj, tap:NO + tap],
                                scalar=prm.t[:, wl + jg * 3 + tap:wl + jg * 3 + tap + 1],
                                in1=yb.t[:, jj, 0:NO], op0=ALU.mult, op1=ALU.add))
                    last = ms[nj - 1]
                    ab.read(last)
                    ACT.wait(*ms)
                    m3 = ACT.mark(nc.scalar.activation(out=yb.t[:, 0:nj, 0:NO], in_=yb.t[:, 0:nj, 0:NO],
                                                       func=AF.Gelu_apprx_tanh))
                    DVE.wait(m3)
                    vo = s0 - lo
                    m4 = DVE.mark(nc.vector.tensor_tensor(out=act.t[:, gi * 2:gi * 2 + nj, 0:NO], in0=yb.t[:, 0:nj, 0:NO],
                                                          in1=g.t[:, nj:2 * nj, vo:vo + NO], op=ALU.mult))
                    g.read(m4); yb.wrote(m4); act.wrote(m4)

                def hook(gi, bi=bi):
                    if bi > 0:
                        if gi == 0:
                            back_A(bi - 1)
                        elif gi == 2:
                            back_B(bi - 1)
                        elif 3 <= gi < 11:
                            back_C(bi - 1, [2 * (gi - 3), 2 * (gi - 3) + 1])
                    if bi + 1 < NB:
                        if 11 <= gi < 15:
                            front_A(bi + 1, list(range(4 * (gi - 11), 4 * (gi - 11) + 4)))
                        elif gi == 16:
                            front_B(bi + 1)
                        elif 17 <= gi < 21:
                            front_C(bi + 1, list(range(4 * (gi - 17), 4 * (gi - 17) + 4)))

                linear_fm(lambda kc, H_=H_, N=N: H_.t[:, kc, 0:N], H_, KC, WB_UP[li], 2 * FH, groups, N, ev_up,
                          ("up", li), hook=hook)
                xm.begin()

                def ev_dn(gi, g, nm, NO=NO):
                    ACT.wait(g.w, xm.prev)
                    for m_ in range(nm):
                        ins = nc.scalar.activation(out=xm.t[:, gi * 4 + m_, 0:NO], in_=g.t[:, m_, 0:NO], func=AF.Copy)
                    mk_ = ACT.mark(ins)
                    g.read(mk_); xm.wrote(mk_)

                linear_fm(lambda kc, NO=NO: act.t[:, kc, 0:NO], act, FC, WB_DN[li], D,
                          [[(512 * gi, 512)] for gi in range(4)], NO, ev_dn, ("dn", li))
            back_A(NB - 1); back_B(NB - 1); back_C(NB - 1, list(range(KC)))
            phase_sync_all()

        for q in (ACT, DVE, POOL, PE):
            q.wait(prm.w, dec.w, kp.w, msk.w, ident.w, ones.w)
        cur = XT
        for li in range(DEPTH):
            j = li // 2
            mid = HB
            if li % 2 == 0:
                retention_layer(j, li, cur, mid)
            else:
                conformer_layer(j, li, cur, mid)
            nxt = YT if li == DEPTH - 1 else HA
            ffn_layer(li, mid, nxt)
            cur = nxt
        pump(10 ** 6)
        for q in (POOL,):
            q.wait(store_marks)
    return nc, pc, NP, SPECIAL, NCH, T


def _fm(vec):
    return np.ascontiguousarray(vec.reshape(-1, 128).T)


def make_core_inputs(seqs, meta, SEGC, SPECIAL, NCH, T):
    xt = np.zeros((T, D), np.float32)
    keep = np.ones(NCH + 1, np.float32)
    valid = np.zeros(T, np.float32)
    pos = np.zeros(T, np.float64)
    places = []
    c = 0
    for s in seqs:
        L = s.shape[0]
        ncs = L // 128
        keep[c] = 0.0
        r0 = c * 128 + 128 - NMETA
        xt[r0:r0 + NMETA] = meta
        xt[r0 + NMETA:r0 + NMETA + L] = s
        valid[r0:r0 + NMETA + L] = 1.0
        pos[c * 128:(c + 1 + ncs) * 128] = np.arange((1 + ncs) * 128)
        places.append((r0 + NMETA, L))
        c += 1 + ncs
    while c < NCH:
        keep[c] = 0.0
        c += 1
    keep[NCH] = 0.0
    keepb = np.concatenate([keep[1:NCH + 1], [0.0]]).astype(np.float32)
    kpv = np.concatenate([keep, keepb]).astype(np.float32)
    kp = np.ascontiguousarray(np.broadcast_to(kpv[None, :], (128, kpv.size))).astype(np.float32)
    mskv = np.concatenate([valid[sc * 128:(sc + 1) * 128] for sc in SPECIAL])
    msk = np.ascontiguousarray(np.broadcast_to(mskv[None, :], (128, mskv.size))).astype(np.float32)
    inv = 10000.0 ** (-np.arange(128, dtype=np.float32) / np.float32(128))
    ang = pos.astype(np.float32)[:, None] * inv[None, :].astype(np.float32)
    cost = np.cos(ang.astype(np.float64)).astype(np.float32)
    sint = np.sin(ang.astype(np.float64)).astype(np.float32)
    return dict(xt=np.ascontiguousarray(xt.T), kp=kp, msk=msk, cost=cost, sint=sint,
                cosf=np.ascontiguousarray(cost.T), sinf=np.ascontiguousarray(sint.T)), places


def pack_params(inp, pc, NP, DEPTH):
    prm = np.zeros((128, NP), np.float32)
    for (nm, idx), o in pc.items():
        if nm == "g_pre_mix":
            a = _fm(inp["norm_pre_mix"][idx])
        elif nm == "g_post_mix":
            a = _fm(inp["norm_post_mix"][idx])
        elif nm == "g_pre_ffn":
            a = _fm(inp["norm_pre_ffn"][idx])
        elif nm == "g_post_ffn":
            a = _fm(inp["norm_post_ffn"][idx])
        elif nm == "ffn_w_dw":
            w = inp["ffn_w_dw"][idx]
            a = np.ascontiguousarray(w.T.reshape(FC, 128, 3).transpose(1, 0, 2).reshape(128, FC * 3))
        elif nm == "ffn_b_dw":
            a = _fm(inp["ffn_b_dw"][idx])
        elif nm == "b_pw1":
            a = _fm(inp["conv_b_pw1"][idx])
        elif nm == "w_dw":
            w = inp["conv_w_dw"][idx]
            a = np.ascontiguousarray(w.T.reshape(KC, 128, CK).transpose(1, 0, 2).reshape(128, KC * CK))
        elif nm == "b_dw":
            a = _fm(inp["conv_b_dw"][idx])
        elif nm == "ln_g":
            a = _fm(inp["conv_ln_g"][idx])
        elif nm == "ln_b":
            a = _fm(inp["conv_ln_b"][idx])
        elif nm == "b_pw2":
            a = _fm(inp["conv_b_pw2"][idx])
        prm[:, o:o + a.shape[1]] = a
    return prm


_CACHE = {}


def run_model(inp, core_seqs, SEGC, DEPTH):
    key = (SEGC, DEPTH)
    if key not in _CACHE:
        _CACHE[key] = build_program(SEGC, DEPTH)
    nc, pc, NP, SPECIAL, NCH, T = _CACHE[key]
    NRET = (DEPTH + 1) // 2
    NCONV = DEPTH // 2
    prm = pack_params(inp, pc, NP, DEPTH)
    decv = np.zeros((NRET * 16,), np.float32)
    for j in range(NRET):
        decv[j * 16:j * 16 + 8] = inp["ret_decay_fwd"][j]
        decv[j * 16 + 8:j * 16 + 16] = inp["ret_decay_bwd"][j]
    dec = np.ascontiguousarray(np.broadcast_to(decv[None, :], (128, decv.size))).astype(np.float32)

    def flat(a, n):
        a = np.ascontiguousarray(a[:n]) if n > 0 else np.zeros((1,) + a.shape[1:], np.float32)
        return a.reshape(-1, 2048)

    shared = dict(prm=prm, dec=dec,
                  ret_w_in=flat(inp["ret_w_in"], NRET), ret_w_out=flat(inp["ret_w_out"], NRET),
                  conv_w_pw1=flat(inp["conv_w_pw1"], NCONV), conv_w_pw2=flat(inp["conv_w_pw2"], NCONV),
                  ffn_w_up=flat(inp["ffn_w_up"], DEPTH), ffn_w_down=flat(inp["ffn_w_down"], DEPTH))
    in_maps = []
    places = []
    for seqs in core_seqs:
        d, pl = make_core_inputs(seqs, inp["meta_tokens"], SEGC, SPECIAL, NCH, T)
        d.update(shared)
        in_maps.append(d)
        places.append(pl)
    res = run_bass_kernel_spmd(nc, in_maps, core_ids=list(range(len(core_seqs))))
    outs = []
    for ci, pl in enumerate(places):
        yt = res.results[ci]["yt"]
        y = yt.T
        outs.append([np.ascontiguousarray(y[r0:r0 + L]) for (r0, L) in pl])
    return outs


def kernel(**inp):
    xp = np.asarray(inp["x_prompt"], np.float32)
    xs = np.asarray(inp["x_sample"], np.float32)
    SEGC = xp.shape[1] // 128
    DEPTH = 4
    core_seqs = []
    for c in range(4):
        core_seqs.append([xs[c], xp[c]])
    for c in range(4):
        core_seqs.append([xp[4 + 3 * c + k] for k in range(3)])
    outs = run_model(inp, core_seqs, SEGC, DEPTH)
    yp = np.zeros_like(xp)
    ys = np.zeros_like(xs)
    for c in range(4):
        ys[c] = outs[c][0]
        yp[c] = outs[c][1]
    for c in range(4):
        for k in range(3):
            yp[4 + 3 * c + k] = outs[4 + c][k]
    return (yp, ys)
```

```python
import numpy as np
from contextlib import ExitStack
import concourse.bass as bass
import concourse.mybir as mybir
from concourse.bass_utils import run_bass_kernel_spmd

F32 = mybir.dt.float32
BF16 = mybir.dt.bfloat16
I32 = mybir.dt.int32
AF = mybir.ActivationFunctionType
ALU = mybir.AluOpType

D = 2048
KC = 16
H = 8
HV = 4096
FH = 5504
FC = 43
CK = 31
NMETA = 16
EPS = 1e-6
LN16 = float(np.log(1.0 / 16.0))


_UID = [0]


def _next_uid():
    _UID[0] += 1
    return _UID[0]


class Q:
    def __init__(self, nc, eng, name, es, step=1):
        self.uid = _next_uid()
        self.e = eng
        self.sem = es.enter_context(nc.semaphore(name))
        self.n = 0
        self.step = step
        self.seen = {}

    def mark(self, ins):
        ins.then_inc(self.sem, self.step)
        self.n += self.step
        return (self, self.n)

    def wait(self, *marks):
        for m in marks:
            if m is None:
                continue
            if isinstance(m, dict):
                self.wait(*m.values())
                continue
            src, n = m
            if self.seen.get(src.uid, 0) >= n:
                continue
            self.e.wait_ge(src.sem, n)
            self.seen[src.uid] = n


class DS:
    def __init__(self, nc, name, es):
        self.uid = _next_uid()
        self.sem = es.enter_context(nc.semaphore(name))
        self.n = 0

    def mark(self, ins):
        ins.then_inc(self.sem, 16)
        self.n += 16
        return (self, self.n)


def _merge(d, m):
    if m is None:
        return
    src, n = m
    k = src.uid
    if k not in d or d[k][1] < n:
        d[k] = (src, n)


class Buf:
    def __init__(self, t, ds=None):
        self.t = t
        self.w = {}
        self.r = {}
        self.prev = {}
        self.ds = ds

    def begin(self):
        self.prev = {}
        for m in list(self.w.values()) + list(self.r.values()):
            _merge(self.prev, m)
        self.w = {}
        self.r = {}

    def wrote(self, m):
        _merge(self.w, m)

    def read(self, m):
        _merge(self.r, m)


class PGroup:
    def __init__(self, t):
        self.t = t
        self.h = [Buf(t[:, 0:2, :]), Buf(t[:, 2:4, :])]

    def begin(self):
        for h in self.h:
            h.begin()

    def _m(self, attr):
        d = {}
        for h in self.h:
            for m in getattr(h, attr).values():
                _merge(d, m)
        return d

    @property
    def prev(self):
        return self._m("prev")

    @property
    def w(self):
        return self._m("w")

    def wrote(self, m):
        for h in self.h:
            h.wrote(m)

    def read(self, m):
        for h in self.h:
            h.read(m)


class Ring:
    def __init__(self, bufs):
        self.b = bufs
        self.i = 0

    def next(self):
        b = self.b[self.i % len(self.b)]
        self.i += 1
        return b


def build_program(SEGC, DEPTH):
    NCH = 3 * (SEGC + 1)
    T = NCH * 128
    NRET = (DEPTH + 1) // 2
    NCONV = DEPTH // 2
    SPECIAL = sorted({0, 1 + SEGC, 2 + 2 * SEGC, 1 + 2 * SEGC, 2 + 3 * SEGC})
    pc = {}
    off = 0
    for i in range(DEPTH):
        for nm, w in (("g_pre_mix", KC), ("g_post_mix", KC), ("g_pre_ffn", KC), ("g_post_ffn", KC),
                      ("ffn_w_dw", FC * 3), ("ffn_b_dw", FC)):
            pc[(nm, i)] = off
            off += w
    for j in range(NCONV):
        for nm, w in (("b_pw1", 2 * KC), ("w_dw", KC * CK), ("b_dw", KC), ("ln_g", KC), ("ln_b", KC), ("b_pw2", KC)):
            pc[(nm, j)] = off
            off += w
    NP = off

    nc = bass.Bass("TRN2", target_bir_lowering=False)
    dt = nc.dram_tensor

    def din(name, shape, dtp=F32):
        return dt(name, shape, dtp, kind="ExternalInput").ap()

    def dsc(name, shape, dtp):
        return dt(name, shape, dtp, kind="Internal").ap()

    XT = din("xt", [D, T])
    PRM = din("prm", [128, NP])
    DEC = din("dec", [128, NRET * 16])
    KPD = din("kp", [128, 2 * NCH + 2])
    MSKD = din("msk", [128, len(SPECIAL) * 128])
    COSF = din("cosf", [128, T])
    SINF = din("sinf", [128, T])
    COST = din("cost", [T, 128])
    SINT = din("sint", [T, 128])
    W_IN = din("ret_w_in", [NRET * D * 12288 // 2048, 2048])
    W_OUT = din("ret_w_out", [NRET * HV * D // 2048, 2048])
    W_PW1 = din("conv_w_pw1", [max(NCONV, 1) * D * 2 * D // 2048, 2048])
    W_PW2 = din("conv_w_pw2", [max(NCONV, 1) * D * D // 2048, 2048])
    W_UP = din("ffn_w_up", [DEPTH * D * 2 * FH // 2048, 2048])
    W_DN = din("ffn_w_down", [DEPTH * FH * D // 2048, 2048])
    YT = dt("yt", [D, T], F32, kind="ExternalOutput").ap()

    WB_IN = [dsc(f"wb_in{j}", [D * 12288 // 2048, 2048], BF16) for j in range(NRET)]
    WB_OUT = [dsc(f"wb_out{j}", [HV * D // 2048, 2048], BF16) for j in range(NRET)]
    WB_PW1 = [dsc(f"wb_pw1{j}", [D * 2 * D // 2048, 2048], BF16) for j in range(NCONV)]
    WB_PW2 = [dsc(f"wb_pw2{j}", [D * D // 2048, 2048], BF16) for j in range(NCONV)]
    WB_UP = [dsc(f"wb_up{i}", [D * 2 * FH // 2048, 2048], BF16) for i in range(DEPTH)]
    WB_DN = [dsc(f"wb_dn{i}", [FH * D // 2048, 2048], BF16) for i in range(DEPTH)]
    HA = dsc("ha", [D, T], F32)
    HB = dsc("hb", [D, T], F32)
    UT = dsc("ut", [D, T], BF16)
    DG = [dsc(f"dg{j}", [128, KC * CK * 128], BF16) for j in range(NCONV)]
    QTd = dsc("qtd", [NCH, 128, D], BF16)
    KTd = dsc("ktd", [NCH, 128, D], BF16)
    KFd = dsc("kfd", [T, D], BF16)
    KBd = dsc("kbd", [T, D], BF16)
    Vd = dsc("vd", [T, HV], BF16)
    Gd = dsc("gd", [T, HV], BF16)
    SBd = dsc("sbd", [NCH, 128, 16 * 512], BF16)
    GTd = dsc("gtd", [NCH, 128, HV], BF16)

    es = ExitStack()
    with es:
        def sb(name, shape, dtp):
            return es.enter_context(nc.sbuf_tensor("sb_" + name, shape, dtp))

        PE = Q(nc, nc.tensor, "s_pe", es)
        ACT = Q(nc, nc.scalar, "s_act", es)
        DVE = Q(nc, nc.vector, "s_dve", es)
        POOL = Q(nc, nc.gpsimd, "s_pool", es)
        SP = Q(nc, nc.sync, "s_sp", es)
        dcount = [0]

        def newds():
            dcount[0] += 1
            return DS(nc, f"ds{dcount[0]}", es)

        def load(buf, out_ap, in_ap, extra=()):
            SP.wait(buf.prev, *extra)
            m = buf.ds.mark(nc.sync.dma_start(out=out_ap, in_=in_ap))
            buf.wrote(m)
            return m

        store_marks = {}

        def store(buf, out_ap, in_ap, sds):
            POOL.wait(buf.w)
            m = sds.mark(nc.gpsimd.dma_start(out=out_ap, in_=in_ap))
            buf.read(m)
            _merge(store_marks, m)
            return m

        def phase_barrier():
            SP.wait(store_marks)

        cast_q = []
        wready = {}

        def plan_cast(key, src, row0, nrows, dst):
            ds_ = newds()
            n = 0
            for r0 in range(0, nrows, 2048):
                rn = min(2048, nrows - r0)
                cast_q.append((ds_, dst[r0:r0 + rn, :], src[row0 + r0:row0 + r0 + rn, :]))
                n += 16
            wready[key] = (ds_, n)

        for i in range(DEPTH):
            j = i // 2
            if i % 2 == 0:
                plan_cast(("in", j), W_IN, j * (D * 12288 // 2048), D * 12288 // 2048, WB_IN[j])
                plan_cast(("out", j), W_OUT, j * (HV * D // 2048), HV * D // 2048, WB_OUT[j])
            else:
                plan_cast(("pw1", j), W_PW1, j * (D * 2 * D // 2048), D * 2 * D // 2048, WB_PW1[j])
                plan_cast(("pw2", j), W_PW2, j * (D * D // 2048), D * D // 2048, WB_PW2[j])
            plan_cast(("up", i), W_UP, i * (D * 2 * FH // 2048), D * 2 * FH // 2048, WB_UP[i])
            plan_cast(("dn", i), W_DN, i * (FH * D // 2048), FH * D // 2048, WB_DN[i])
        cast_pos = [0]

        def pump(n):
            for _ in range(n):
                if cast_pos[0] >= len(cast_q):
                    return
                ds_, o, i_ = cast_q[cast_pos[0]]
                cast_pos[0] += 1
                ds_.mark(nc.gpsimd.dma_start(out=o, in_=i_))

        def wwait(key):
            ds_, n = wready[key]
            while ds_.n < n:
                pump(1)
            SP.wait((ds_, n))

        prm = Buf(sb("prm", [128, NP], F32), newds())
        dec = Buf(sb("dec", [128, NRET * 16], F32), newds())
        kp = Buf(sb("kp", [128, 2 * NCH + 2], F32), newds())
        msk = Buf(sb("msk", [128, len(SPECIAL) * 128], F32), newds())
        ones = Buf(sb("ones", [128, 128], BF16))
        ident = Buf(sb("ident", [128, 128], BF16))
        iof = Buf(sb("iof", [128, 128], F32))
        iop = Buf(sb("iop", [128, 1], F32))
        ioi = Buf(sb("ioi", [128, 128], I32))
        ipi = Buf(sb("ipi", [128, 1], I32))
        load(prm, prm.t[:], PRM)
        load(dec, dec.t[:], DEC)
        load(kp, kp.t[:], KPD)
        load(msk, msk.t[:], MSKD)
        ones.wrote(DVE.mark(nc.vector.memset(ones.t[:], 1.0)))
        ioi.wrote(POOL.mark(nc.gpsimd.iota(ioi.t[:], pattern=[[1, 128]], base=0, channel_multiplier=0)))
        ipi.wrote(POOL.mark(nc.gpsimd.iota(ipi.t[:], pattern=[[0, 1]], base=0, channel_multiplier=1)))
        DVE.wait(ioi.w, ipi.w)
        iof.wrote(DVE.mark(nc.vector.tensor_copy(out=iof.t[:], in_=ioi.t[:])))
        iop.wrote(DVE.mark(nc.vector.tensor_copy(out=iop.t[:], in_=ipi.t[:])))
        DVE.wait(iof.w, iop.w)
        ident.wrote(DVE.mark(nc.vector.tensor_scalar(out=ident.t[:], in0=iof.t[:], scalar1=iop.t[:, 0:1],
                                                     scalar2=None, op0=ALU.is_equal)))

        def P(nm, idx, col=0, n=1):
            o = pc[(nm, idx)] + col
            return prm.t[:, o:o + n]

        pg = Ring([PGroup(es.enter_context(nc.psum_tensor(f"pg{i}", [128, 4, 512], F32))) for i in range(2)])
        ARENA = 194 * 1024
        COMMON = 104 * 1024
        TBL = 20 * 1024
        work = sb("arena", [128, ARENA // 4], F32)

        def carve_at(base, limit, spec):
            res = {}
            o = base // 4
            for nm, shp, dtp in spec:
                nel = int(np.prod(shp))
                nbytes = nel * (4 if dtp in (F32, I32) else 2)
                nw = (nbytes + 3) // 4
                ap = work[:, o:o + nw]
                if dtp != F32:
                    ap = ap.bitcast(dtp)[:, 0:nel]
                names = "abcd"[:len(shp)]
                if len(shp) > 1:
                    kw = {names[i]: shp[i] for i in range(len(shp) - 1)}
                    ap = ap.rearrange("p (" + " ".join(names) + ") -> p " + " ".join(names), **kw)
                res[nm] = ap
                o += nw
            assert o * 4 <= limit, (o * 4, limit)
            return res

        cm = carve_at(0, COMMON, [("xm", [KC, 512], F32), ("hn", [KC, 512], BF16), ("w0", [16, 512], BF16),
                                  ("w1", [16, 512], BF16), ("w2", [16, 512], BF16), ("rstd", [512], F32),
                                  ("h0", [512], F32), ("h1", [512], F32), ("h2", [512], F32)])
        wring = Ring([Buf(cm[f"w{i}"], newds()) for i in range(3)])
        xm = Buf(cm["xm"], newds())
        hn = Buf(cm["hn"])
        rstd = Buf(cm["rstd"])
        hres = Ring([Buf(cm[f"h{i}"], newds()) for i in range(3)])
        xm_sds = newds()
        hresf_ds = [newds() for _ in range(7)]

        def carve(spec):
            return carve_at(COMMON, ARENA - TBL, spec)

        work_guard = Buf(work)

        def phase_sync_all():
            marks = []
            for q in (PE, ACT, DVE, POOL):
                if q.n > 0:
                    q.e.wait_ge(q.sem, q.n)
                marks.append(q.mark(q.e.nop(nofuse=True)))
            for q in (PE, ACT, DVE, POOL, SP):
                q.wait(*marks)
                q.wait(store_marks)

        def front(src, c0, N, gname, li):
            xm.begin()
            load(xm, xm.t[:, :, 0:N], src.rearrange("(kc p) t -> p kc t", p=128)[:, :, c0:c0 + N])
            hn.begin()
            ACT.wait(xm.w, hn.prev)
            for kc in range(KC):
                ins = nc.scalar.activation(out=hn.t[:, kc, 0:N], in_=xm.t[:, kc, 0:N], func=AF.Square)
            m = ACT.mark(ins)
            hn.wrote(m)
            g = pg.next()
            g.begin()
            PE.wait(hn.w, g.prev, ones.w)
            for kc in range(KC):
                ins = nc.tensor.matmul(g.t[:, 0, 0:N], lhsT=ones.t[:], rhs=hn.t[:, kc, 0:N],
                                       start=(kc == 0), stop=(kc == KC - 1))
            pm = PE.mark(ins)
            hn.read(pm)
            rstd.begin()
            ACT.wait(pm, rstd.prev)
            m0 = ACT.mark(nc.scalar.activation(out=rstd.t[:, 0:N], in_=g.t[:, 0, 0:N], func=AF.Sqrt, scale=1.0 / D,
                                               bias=EPS))
            DVE.wait(m0)
            m1 = DVE.mark(nc.vector.reciprocal(out=rstd.t[:, 0:N], in_=rstd.t[:, 0:N]))
            g.read(m0)
            g.read(m1)
            rstd.wrote(m1)
            hn.begin()
            DVE.wait(m1, hn.prev, prm.w)
            for kc in range(KC):
                ins = nc.vector.scalar_tensor_tensor(out=hn.t[:, kc, 0:N], in0=xm.t[:, kc, 0:N],
                                                     scalar=P(gname, li, kc), in1=rstd.t[:, 0:N],
                                                     op0=ALU.mult, op1=ALU.mult)
            m2 = DVE.mark(ins)
            hn.wrote(m2)
            xm.read(m2)
            rstd.read(m2)

        def back(N, c0, gname, li, hsrc, dst, special_mask):
            hn.begin()
            ACT.wait(xm.w, hn.prev)
            for kc in range(KC):
                ins = nc.scalar.activation(out=hn.t[:, kc, 0:N], in_=xm.t[:, kc, 0:N], func=AF.Square)
            m = ACT.mark(ins)
            hn.wrote(m)
            xm.read(m)
            g = pg.next()
            g.begin()
            PE.wait(hn.w, g.prev)
            for kc in range(KC):
                ins = nc.tensor.matmul(g.t[:, 0, 0:N], lhsT=ones.t[:], rhs=hn.t[:, kc, 0:N],
                                       start=(kc == 0), stop=(kc == KC - 1))
            pm = PE.mark(ins)
            hn.read(pm)
            rstd.begin()
            ACT.wait(pm, rstd.prev)
            m0 = ACT.mark(nc.scalar.activation(out=rstd.t[:, 0:N], in_=g.t[:, 0, 0:N], func=AF.Sqrt, scale=1.0 / D,
                                               bias=EPS))
            DVE.wait(m0)
            m1 = DVE.mark(nc.vector.reciprocal(out=rstd.t[:, 0:N], in_=rstd.t[:, 0:N]))
            g.read(m0)
            g.read(m1)
            rstd.wrote(m1)
            hs = hsrc.rearrange("(kc p) t -> p kc t", p=128)
            last = None
            for kc in range(KC):
                hb = hres.next()
                hb.begin()
                load(hb, hb.t[:, 0:N], hs[:, kc, c0:c0 + N])
                DVE.wait(m1, hb.w, xm.w, last)
                ma = DVE.mark(nc.vector.scalar_tensor_tensor(out=xm.t[:, kc, 0:N], in0=xm.t[:, kc, 0:N],
                                                             scalar=P(gname, li, kc), in1=rstd.t[:, 0:N],
                                                             op0=ALU.mult, op1=ALU.mult))
                DVE.wait(ma)
                last = DVE.mark(nc.vector.tensor_tensor(out=xm.t[:, kc, 0:N], in0=xm.t[:, kc, 0:N],
                                                        in1=hb.t[:, 0:N], op=ALU.add))
                hb.read(last)
                if special_mask:
                    for si, sc in enumerate(SPECIAL):
                        lo = sc * 128 - c0
                        if 0 <= lo < N:
                            DVE.wait(last, msk.w)
                            last = DVE.mark(nc.vector.tensor_tensor(
                                out=xm.t[:, kc, lo:lo + 128], in0=xm.t[:, kc, lo:lo + 128],
                                in1=msk.t[:, si * 128:(si + 1) * 128], op=ALU.mult))
            xm.wrote(last)
            rstd.read(last)
            store(xm, dst.rearrange("(kc p) t -> p kc t", p=128)[:, :, c0:c0 + N], xm.t[:, :, 0:N], xm_sds)

        def load_slab(WB, ncolsW, k0, kn, pieces, wkey):
            W2 = WB.rearrange("a b -> (a b)").rearrange("(k n) -> k n", n=ncolsW)
            sl = wring.next()
            sl.begin()
            o = 0
            for (cc, ncol) in pieces:
                load(sl, sl.t[:, 0:kn, o:o + ncol],
                     W2[k0 * 128:(k0 + kn) * 128, cc:cc + ncol].rearrange("(kc p) n -> p kc n", p=128))
                o += ncol
            return sl

        def linear_fm(xrhs, xbuf, KCin, WB, ncolsW, groups, N, evac, wkey, hook=None):
            wwait(wkey)
            for gi, pieces in enumerate(groups):
                if hook is not None:
                    hook(gi)
                ncols = sum(p[1] for p in pieces)
                nm = ncols // 128
                g = pg.next()
                g.begin()
                first = True
                for k0 in range(0, KCin, 16):
                    kn = min(16, KCin - k0)
                    sl = load_slab(WB, ncolsW, k0, kn, pieces, wkey)
                    PE.wait(sl.w, xbuf.w)
                    for m in range(nm):
                        if first and m == 0:
                            PE.wait(g.h[0].prev)
                        if first and (m == 2 or (m == 0 and nm <= 2 and False)):
                            PE.wait(g.h[1].prev)
                        for kc in range(kn):
                            ins = nc.tensor.matmul(g.t[:, m, 0:N], lhsT=sl.t[:, kc, m * 128:(m + 1) * 128],
                                                   rhs=xrhs(k0 + kc), start=(k0 + kc == 0),
                                                   stop=(k0 + kc == KCin - 1))
                    pm = PE.mark(ins)
                    sl.read(pm)
                    first = False
                xbuf.read(pm)
                g.wrote(pm)
                evac(gi, g, nm)

        def linear_tok(xbuf, nchunk, WB, ncolsW, groups, evac, wkey):
            wwait(wkey)
            for gi, pieces in enumerate(groups):
                g = pg.next()
                g.begin()
                sl = load_slab(WB, ncolsW, 0, KC, pieces, wkey)
                PE.wait(sl.w, xbuf.w, g.prev)
                for ci in range(nchunk):
                    for kc in range(KC):
                        ins = nc.tensor.matmul(g.t[:, ci, :], lhsT=xbuf.t[:, kc, ci * 128:(ci + 1) * 128],
                                               rhs=sl.t[:, kc, :], start=(kc == 0), stop=(kc == KC - 1))
                pm = PE.mark(ins)
                sl.read(pm)
                xbuf.read(pm)
                g.wrote(pm)
                evac(gi, g, nchunk)

        blocks = [(c0, min(512, T - c0)) for c0 in range(0, T, 512)]

        def retention_layer(j, li, hin, hout):
            tb = carve_at(ARENA - TBL, ARENA, [("lg", [16], F32), ("nlg", [16], F32), ("bq", [16], F32), ("t1", [128], F32),
                        ("t2", [128], F32), ("t3", [128], F32), ("dmt", [8, 128], F32),
                        ("fq", [16, 128], BF16), ("fb", [16, 128], BF16), ("dkf", [8], F32), ("dkb", [8], F32),
                        ("cd", [16], F32), ("cdkf", [NCH, 8], F32), ("cdkb", [NCH, 8], F32), ("pj", [2], F32),
                        ("fq32", [128], F32)])
            TB = Buf(work)
            TB.begin()
            d0 = j * 16
            for q in (ACT, DVE):
                q.wait(dec.w, kp.w, iof.w, iop.w)
            a1 = ACT.mark(nc.scalar.activation(out=tb["lg"][:, :], in_=dec.t[:, d0:d0 + 16], func=AF.Exp, scale=-1.0))
            ACT.wait(a1)
            a2 = ACT.mark(nc.scalar.activation(out=tb["nlg"][:, :], in_=tb["lg"][:, :], func=AF.Ln, bias=1.0))
            DVE.wait(a2)
            v1 = DVE.mark(nc.vector.tensor_scalar(out=tb["lg"][:, :], in0=tb["nlg"][:, :], scalar1=-1.0,
                                                  scalar2=None, op0=ALU.mult))
            DVE.wait(v1)
            nc.vector.tensor_scalar(out=tb["pj"][:, 0:1], in0=iop.t[:, 0:1], scalar1=-1.0, scalar2=127.0,
                                    op0=ALU.mult, op1=ALU.add)
            nc.vector.tensor_copy(out=tb["pj"][:, 1:2], in_=iop.t[:, 0:1])
            nc.vector.tensor_scalar(out=tb["bq"][:, 0:8], in0=tb["lg"][:, 0:8], scalar1=LN16, scalar2=None, op0=ALU.add)
            nc.vector.tensor_scalar(out=tb["bq"][:, 8:16], in0=tb["lg"][:, 8:16], scalar1=128.0, scalar2=LN16,
                                    op0=ALU.mult, op1=ALU.add)
            v2 = DVE.mark(nc.vector.tensor_scalar(out=tb["cd"][:, :], in0=tb["lg"][:, :], scalar1=128.0, scalar2=None,
                                                  op0=ALU.mult))
            DVE.wait(v2)
            nc.vector.tensor_scalar(out=tb["dkf"][:, :], in0=tb["lg"][:, 0:8], scalar1=tb["pj"][:, 0:1], scalar2=None,
                                    op0=ALU.mult)
            nc.vector.tensor_scalar(out=tb["dkb"][:, :], in0=tb["lg"][:, 8:16], scalar1=tb["pj"][:, 1:2], scalar2=None,
                                    op0=ALU.mult)
            v3 = DVE.mark(nc.vector.tensor_scalar(out=tb["t3"][:, :], in0=iof.t[:, :], scalar1=iop.t[:, 0:1],
                                                  scalar2=None, op0=ALU.subtract))
            DVE.wait(v3)
            nc.vector.tensor_scalar(out=tb["t1"][:, :], in0=tb["t3"][:, :], scalar1=0.0, scalar2=None, op0=ALU.max)
            v4 = DVE.mark(nc.vector.tensor_scalar(out=tb["t2"][:, :], in0=tb["t3"][:, :], scalar1=-1.0, scalar2=0.0,
                                                  op0=ALU.mult, op1=ALU.max))
            ACT.wait(v4)
            e1 = ACT.mark(nc.scalar.activation(out=tb["cd"][:, :], in_=tb["cd"][:, :], func=AF.Exp))
            nc.scalar.activation(out=tb["dkf"][:, :], in_=tb["dkf"][:, :], func=AF.Exp)
            e2 = ACT.mark(nc.scalar.activation(out=tb["dkb"][:, :], in_=tb["dkb"][:, :], func=AF.Exp))
            for h in range(H):
                DVE.wait(v4, e2)
                nc.vector.tensor_scalar(out=tb["t3"][:, :], in0=tb["t1"][:, :], scalar1=tb["lg"][:, h:h + 1],
                                        scalar2=None, op0=ALU.mult)
                vv = DVE.mark(nc.vector.tensor_scalar(out=tb["fq32"][:, :], in0=tb["t2"][:, :],
                                                      scalar1=tb["lg"][:, 8 + h:9 + h], scalar2=None, op0=ALU.mult))
                DVE.wait(vv)
                vv = DVE.mark(nc.vector.tensor_tensor(out=tb["t3"][:, :], in0=tb["t3"][:, :], in1=tb["fq32"][:, :],
                                                      op=ALU.add))
                ACT.wait(vv)
                e2 = ACT.mark(nc.scalar.activation(out=tb["dmt"][:, h, :], in_=tb["t3"][:, :], func=AF.Exp,
                                                   bias=LN16 if False else 0.0))
                for dcc in range(2):
                    nc.scalar.activation(out=tb["fq"][:, 2 * h + dcc, :], in_=iof.t[:, :], func=AF.Exp,
                                         scale=tb["lg"][:, h:h + 1], bias=tb["bq"][:, h:h + 1])
                    e2 = ACT.mark(nc.scalar.activation(out=tb["fb"][:, 2 * h + dcc, :], in_=iof.t[:, :], func=AF.Exp,
                                                       scale=tb["nlg"][:, 8 + h:9 + h], bias=tb["bq"][:, 8 + h:9 + h]))
            DVE.wait(e1, e2)
            vv = DVE.mark(nc.vector.tensor_scalar(out=tb["dmt"][:, :, :], in0=tb["dmt"][:, :, :], scalar1=1.0 / 16.0,
                                                  scalar2=None, op0=ALU.mult))
            for c in range(NCH):
                nc.vector.tensor_scalar(out=tb["cdkf"][:, c, :], in0=tb["cd"][:, 0:8], scalar1=kp.t[:, c:c + 1],
                                        scalar2=None, op0=ALU.mult)
                vv = DVE.mark(nc.vector.tensor_scalar(out=tb["cdkb"][:, c, :], in0=tb["cd"][:, 8:16],
                                                      scalar1=kp.t[:, NCH + 1 + c:NCH + 2 + c], scalar2=None,
                                                      op0=ALU.mult))
            TBM = vv
            for q in (ACT, DVE, POOL, PE):
                q.wait(TBM, e2)
            def carve2(spec, full=False):
                return carve_at(0 if full else COMMON, ARENA - TBL, spec)

            ra = carve2([("qo", [4, 16, 128], BF16), ("ko", [4, 16, 128], BF16),
                         ("vt0", [4, 512], BF16),
                         ("kf0", [4, 512], BF16),
                         ("kb0", [4, 512], BF16),
                         ("orot", [4, 2, 2, 128], F32), ("ta", [512], F32), ("tb", [512], F32),
                         ("tc", [512], F32), ("td", [512], F32),
                         ("cf", [512], F32), ("sf", [512], F32), ("ct", [4, 128], F32), ("st", [4, 128], F32)])
            qo = Buf(ra["qo"]); ko = Buf(ra["ko"])
            vtr = Ring([Buf(ra["vt0"])])
            kfr = Ring([Buf(ra["kf0"])])
            kbr = Ring([Buf(ra["kb0"])])
            orot = Buf(ra["orot"])
            tmp = Buf(ra["ta"])
            cf = Buf(ra["cf"], newds()); sf = Buf(ra["sf"], newds())
            ct = Buf(ra["ct"], newds()); st = Buf(ra["st"], newds())
            sd = {k: newds() for k in ("q", "k", "v", "g", "kf", "kb")}
            WB = WB_IN[j]
            for (c0, N) in blocks:
                nchunk = N // 128
                cb = c0 // 128
                pump(1)
                front(hin, c0, N, "g_pre_mix", li)
                for b_, src_ in ((cf, COSF), (sf, SINF)):
                    b_.begin()
                    load(b_, b_.t[:, 0:N], src_[:, c0:c0 + N])
                for b_, src_ in ((ct, COST), (st, SINT)):
                    b_.begin()
                    load(b_, b_.t[:, 0:nchunk, :], src_[c0:c0 + N, :].rearrange("(c p) f -> p c f", p=128))

                def ev_v(gi, g, nch_, dst=Vd, ring=vtr, func=AF.Copy, key="v"):
                    vt = ring.next()
                    vt.begin()
                    ACT.wait(g.w, vt.prev)
                    m = ACT.mark(nc.scalar.activation(out=vt.t[:, 0:nch_, :], in_=g.t[:, 0:nch_, :], func=func))
                    g.read(m)
                    vt.wrote(m)
                    store(vt, dst[c0:c0 + N, gi * 512:(gi + 1) * 512].rearrange("(c p) f -> p c f", p=128),
                          vt.t[:, 0:nch_, :], sd[key])

                linear_tok(hn, nchunk, WB, 12288, [[(4096 + 512 * gi, 512)] for gi in range(8)], ev_v, ("in", j))
                linear_tok(hn, nchunk, WB, 12288, [[(8192 + 512 * gi, 512)] for gi in range(8)],
                           lambda gi, g, n_: ev_v(gi, g, n_, Gd, vtr, AF.Silu, "g"), ("in", j))

                def ev_ktok(gi, g, nch_):
                    gv = g.t.rearrange("p c (a b f) -> p c a b f", a=2, b=2)
                    orot.begin()
                    tmp.begin()
                    DVE.wait(g.w, orot.prev, tmp.prev, ct.w, st.w)
                    last = None
                    for hh in range(2):
                        x1 = gv[:, 0:nch_, hh, 0, :]
                        x2 = gv[:, 0:nch_, hh, 1, :]
                        cs = ct.t[:, 0:nch_, :]
                        sn = st.t[:, 0:nch_, :]
                        ta = ra["ta"].rearrange("p (c f) -> p c f", c=4)[:, 0:nch_, :]
                        tbb = ra["tb"].rearrange("p (c f) -> p c f", c=4)[:, 0:nch_, :]
                        tcc = ra["tc"].rearrange("p (c f) -> p c f", c=4)[:, 0:nch_, :]
                        tdd = ra["td"].rearrange("p (c f) -> p c f", c=4)[:, 0:nch_, :]
                        DVE.wait(last)
                        nc.vector.tensor_tensor(out=ta, in0=x1, in1=cs, op=ALU.mult)
                        nc.vector.tensor_tensor(out=tbb, in0=x2, in1=sn, op=ALU.mult)
                        nc.vector.tensor_tensor(out=tcc, in0=x2, in1=cs, op=ALU.mult)
                        mm = DVE.mark(nc.vector.tensor_tensor(out=tdd, in0=x1, in1=sn, op=ALU.mult))
                        DVE.wait(mm)
                        nc.vector.tensor_tensor(out=orot.t[:, 0:nch_, hh, 0, :], in0=ta, in1=tbb, op=ALU.subtract)
                        last = DVE.mark(nc.vector.tensor_tensor(out=orot.t[:, 0:nch_, hh, 1, :], in0=tcc, in1=tdd,
                                                                op=ALU.add))
                    g.read(last)
                    ct.read(last); st.read(last)
                    orot.wrote(last)
                    tmp.wrote(last)
                    kf = kfr.next(); kb = kbr.next()
                    kf.begin(); kb.begin()
                    ACT.wait(orot.w, kf.prev, kb.prev, TBM)
                    for hh in range(2):
                        h = 2 * gi + hh
                        src_ = orot.t[:, 0:nch_, hh, :, :]
                        nc.scalar.activation(out=kf.t.rearrange("p c (a b f) -> p c a b f", a=2, b=2)[:, 0:nch_, hh, :, :],
                                             in_=src_, func=AF.Copy, scale=tb["dkf"][:, h:h + 1])
                        mk = ACT.mark(nc.scalar.activation(
                            out=kb.t.rearrange("p c (a b f) -> p c a b f", a=2, b=2)[:, 0:nch_, hh, :, :],
                            in_=src_, func=AF.Copy, scale=tb["dkb"][:, h:h + 1]))
                    orot.read(mk)
                    kf.wrote(mk); kb.wrote(mk)
                    store(kf, KFd[c0:c0 + N, gi * 512:(gi + 1) * 512].rearrange("(c p) f -> p c f", p=128),
                          kf.t[:, 0:nch_, :], sd["kf"])
                    store(kb, KBd[c0:c0 + N, gi * 512:(gi + 1) * 512].rearrange("(c p) f -> p c f", p=128),
                          kb.t[:, 0:nch_, :], sd["kb"])

                linear_tok(hn, nchunk, WB, 12288, [[(2048 + 512 * gi, 512)] for gi in range(4)], ev_ktok, ("in", j))

                def mk_ev_fm(ob):
                    def ev(gi, g, nm):
                        xcv = ra["orot"].rearrange("p c a b f -> p c (a b f)")
                        orot.begin()
                        ACT.wait(g.w, orot.prev)
                        mkc = ACT.mark(nc.scalar.activation(out=xcv[:, 0:4, 0:N], in_=g.t[:, 0:4, 0:N], func=AF.Copy))
                        g.read(mkc)
                        orot.wrote(mkc)
                        tmp.begin()
                        DVE.wait(mkc, tmp.prev, cf.w, sf.w, ob.prev)
                        last = None
                        for hh in range(2):
                            x1 = xcv[:, 2 * hh, 0:N]
                            x2 = xcv[:, 2 * hh + 1, 0:N]
                            DVE.wait(last)
                            nc.vector.tensor_tensor(out=ra["ta"][:, 0:N], in0=x1, in1=cf.t[:, 0:N], op=ALU.mult)
                            nc.vector.tensor_tensor(out=ra["tb"][:, 0:N], in0=x2, in1=sf.t[:, 0:N], op=ALU.mult)
                            nc.vector.tensor_tensor(out=ra["tc"][:, 0:N], in0=x2, in1=cf.t[:, 0:N], op=ALU.mult)
                            mm = DVE.mark(nc.vector.tensor_tensor(out=ra["td"][:, 0:N], in0=x1, in1=sf.t[:, 0:N],
                                                                  op=ALU.mult))
                            DVE.wait(mm)
                            kc1 = gi * 4 + 2 * hh
                            nc.vector.tensor_tensor(out=ob.t[:, 0:nchunk, kc1, :],
                                                    in0=ra["ta"][:, 0:N].rearrange("p (c f) -> p c f", f=128),
                                                    in1=ra["tb"][:, 0:N].rearrange("p (c f) -> p c f", f=128),
                                                    op=ALU.subtract)
                            last = DVE.mark(nc.vector.tensor_tensor(
                                out=ob.t[:, 0:nchunk, kc1 + 1, :],
                                in0=ra["tc"][:, 0:N].rearrange("p (c f) -> p c f", f=128),
                                in1=ra["td"][:, 0:N].rearrange("p (c f) -> p c f", f=128), op=ALU.add))
                        orot.read(last)
                        tmp.wrote(last)
                        ob.wrote(last)
                    return ev

                qo.begin(); ko.begin()
                linear_fm(lambda kc: hn.t[:, kc, 0:N], hn, KC, WB, 12288,
                          [[(512 * gi, 512)] for gi in range(4)], N, mk_ev_fm(qo), ("in", j))
                linear_fm(lambda kc: hn.t[:, kc, 0:N], hn, KC, WB, 12288,
                          [[(2048 + 512 * gi, 512)] for gi in range(4)], N, mk_ev_fm(ko), ("in", j))
                cf.read(DVE.mark(nc.vector.engine_nop())) if False else None
                for b_ in (cf, sf):
                    b_.read((DVE, DVE.n))
                store(qo, QTd[cb:cb + nchunk].rearrange("c p (k f) -> p c k f", k=16), qo.t[:, 0:nchunk, :, :], sd["q"])
                store(ko, KTd[cb:cb + nchunk].rearrange("c p (k f) -> p c k f", k=16), ko.t[:, 0:nchunk, :, :], sd["k"])
            phase_sync_all()

            rb = carve2(full=True, spec=[("S", [16, 512], F32), ("so", [16, 512], BF16), ("kb0", [2048], BF16), ("kb1", [2048], BF16),
                         ("v0", [4096], BF16), ("v1", [4096], BF16)])
            S = Buf(rb["S"]); so = Buf(rb["so"])
            kbl = Ring([Buf(rb["kb0"], newds()), Buf(rb["kb1"], newds())])
            vl = Ring([Buf(rb["v0"], newds()), Buf(rb["v1"], newds())])
            so_sds = newds()
            S.begin()
            DVE.wait(S.prev)
            S.wrote(DVE.mark(nc.vector.memset(S.t[:, :, :], 0.0)))

            def state_update(S, kx, vx, cdk, c):
                for hg in range(4):
                    g = pg.next()
                    g.begin()
                    PE.wait(kx.w, vx.w, g.prev)
                    for hl in range(2):
                        h = 2 * hg + hl
                        for dcc in range(2):
                            ins = nc.tensor.matmul(g.t[:, 2 * hl + dcc, :],
                                                   lhsT=kx.t[:, h * 256 + dcc * 128:h * 256 + (dcc + 1) * 128],
                                                   rhs=vx.t[:, h * 512:(h + 1) * 512], start=True, stop=True)
                    pm = PE.mark(ins)
                    g.wrote(pm)
                    kx.read(pm); vx.read(pm)
                    DVE.wait(pm, S.w, S.r)
                    for hl in range(2):
                        h = 2 * hg + hl
                        ins = nc.vector.scalar_tensor_tensor(out=S.t[:, 2 * h:2 * h + 2, :], in0=S.t[:, 2 * h:2 * h + 2, :],
                                                             scalar=cdk[:, c, h:h + 1],
                                                             in1=g.t[:, 2 * hl:2 * hl + 2, :], op0=ALU.mult, op1=ALU.add)
                    m = DVE.mark(ins)
                    g.read(m)
                    S.wrote(m)

            for c in range(NCH - 1, -1, -1):
                kx = kbl.next(); vx = vl.next()
                kx.begin(); vx.begin()
                load(kx, kx.t[:, :], KBd[c * 128:(c + 1) * 128, :])
                load(vx, vx.t[:, :], Vd[c * 128:(c + 1) * 128, :])
                so.begin()
                ACT.wait(S.w, so.prev)
                for q4 in range(4):
                    ins = nc.scalar.activation(out=so.t[:, 4 * q4:4 * q4 + 4, :], in_=S.t[:, 4 * q4:4 * q4 + 4, :],
                                               func=AF.Copy, scale=kp.t[:, NCH + 1 + c:NCH + 2 + c])
                m = ACT.mark(ins)
                so.wrote(m)
                S.read(m)
                store(so, SBd[c], so.t.rearrange("p a b -> p (a b)"), so_sds)
                state_update(S, kx, vx, tb["cdkb"], c)
            phase_sync_all()

            rc = carve2(full=True, spec=[("S", [16, 512], F32), ("sfb", [16, 512], BF16), ("sbt", [16, 512], BF16),
                         ("qt0", [16, 128], BF16), ("qt1", [16, 128], BF16), ("kt0", [16, 128], BF16),
                         ("kt1", [16, 128], BF16), ("kf0", [2048], BF16), ("kf1", [2048], BF16),
                         ("v0", [4096], BF16), ("g0", [4096], BF16),
                         ("qf", [16, 128], BF16), ("qb", [16, 128], BF16), ("sc", [8, 128], BF16),
                         ("on", [4, 512], F32), ("gated", [4096], BF16), ("gT", [32, 128], BF16),
                         ("stt", [8, 6], F32), ("mv", [8, 2], F32), ("rs", [8], F32), ("nb", [8], F32)])
            S = Buf(rc["S"]); sfb = Buf(rc["sfb"]); sbt = Buf(rc["sbt"], newds())
            qtl = Ring([Buf(rc["qt0"], newds()), Buf(rc["qt1"], newds())])
            ktl = Ring([Buf(rc["kt0"], newds()), Buf(rc["kt1"], newds())])
            kfl = Ring([Buf(rc["kf0"], newds()), Buf(rc["kf1"], newds())])
            vl = Ring([Buf(rc["v0"], newds())])
            gl = Ring([Buf(rc["g0"], newds())])
            qf = Buf(rc["qf"]); qb = Buf(rc["qb"]); sc = Buf(rc["sc"]); on = Buf(rc["on"])
            gated = Buf(rc["gated"]); gT = Buf(rc["gT"]); stt = Buf(rc["stt"])
            gt_sds = newds()
            S.begin(); sfb.begin()
            DVE.wait(S.prev, sfb.prev)
            S.wrote(DVE.mark(nc.vector.memset(S.t[:, :, :], 0.0)))
            sfb.wrote(DVE.mark(nc.vector.memset(sfb.t[:, :, :], 0.0)))
            for c in range(NCH):
                qt = qtl.next(); kt = ktl.next(); kx = kfl.next(); vx = vl.next(); gx = gl.next()
                for b_, src_ in ((qt, QTd[c]), (kt, KTd[c])):
                    b_.begin()
                    load(b_, b_.t.rearrange("p a b -> p (a b)"), src_)
                kx.begin(); load(kx, kx.t[:, :], KFd[c * 128:(c + 1) * 128, :])
                vx.begin(); load(vx, vx.t[:, :], Vd[c * 128:(c + 1) * 128, :])
                gx.begin(); load(gx, gx.t[:, :], Gd[c * 128:(c + 1) * 128, :])
                sbt.begin(); load(sbt, sbt.t.rearrange("p a b -> p (a b)"), SBd[c])
                qf.begin(); qb.begin()
                DVE.wait(qt.w, qf.prev, qb.prev)
                nc.vector.tensor_tensor(out=qf.t[:, :, :], in0=qt.t[:, :, :], in1=tb["fq"][:, :, :], op=ALU.mult)
                m = DVE.mark(nc.vector.tensor_tensor(out=qb.t[:, :, :], in0=qt.t[:, :, :], in1=tb["fb"][:, :, :],
                                                     op=ALU.mult))
                qf.wrote(m); qb.wrote(m); qt.read(m)
                g = pg.next(); g.begin()
                PE.wait(qt.w, kt.w, g.prev)
                for h in range(H):
                    for dcc in range(2):
                        ins = nc.tensor.matmul(g.t[:, h // 4, (h % 4) * 128:(h % 4 + 1) * 128],
                                               lhsT=kt.t[:, 2 * h + dcc, :], rhs=qt.t[:, 2 * h + dcc, :],
                                               start=(dcc == 0), stop=(dcc == 1))
                pm = PE.mark(ins)
                g.wrote(pm); qt.read(pm); kt.read(pm)
                sc.begin()
                DVE.wait(pm, sc.prev)
                for half in range(2):
                    ins = nc.vector.tensor_tensor(out=sc.t[:, 4 * half:4 * half + 4, :],
                                                  in0=g.t[:, half, :].rearrange("p (a b) -> p a b", a=4),
                                                  in1=tb["dmt"][:, 4 * half:4 * half + 4, :], op=ALU.mult)
                m = DVE.mark(ins)
                g.read(m); sc.wrote(m)
                gated.begin()
                for hg in range(2):
                    g = pg.next(); g.begin()
                    PE.wait(sc.w, vx.w, qf.w, qb.w, sfb.w, sbt.w, g.prev)
                    for hl in range(4):
                        h = 4 * hg + hl
                        nc.tensor.matmul(g.t[:, hl, :], lhsT=sc.t[:, h, :], rhs=vx.t[:, h * 512:(h + 1) * 512],
                                         start=True, stop=False)
                        for dcc in range(2):
                            nc.tensor.matmul(g.t[:, hl, :], lhsT=qf.t[:, 2 * h + dcc, :], rhs=sfb.t[:, 2 * h + dcc, :],
                                             start=False, stop=False)
                        for dcc in range(2):
                            ins = nc.tensor.matmul(g.t[:, hl, :], lhsT=qb.t[:, 2 * h + dcc, :],
                                                   rhs=sbt.t[:, 2 * h + dcc, :], start=False, stop=(dcc == 1))
                    pm = PE.mark(ins)
                    g.wrote(pm)
                    for b_ in (sc, vx, qf, qb, sfb, sbt):
                        b_.read(pm)
                    stt.begin()
                    DVE.wait(pm, stt.prev)
                    for hl in range(4):
                        ins = nc.vector.bn_stats(out=rc["stt"][:, hl, :], in_=g.t[:, hl, :])
                    m = DVE.mark(ins)
                    DVE.wait(m)
                    for hl in range(4):
                        ins = nc.vector.bn_aggr(out=rc["mv"][:, hl, :], in_=rc["stt"][:, hl, :])
                    m = DVE.mark(ins)
                    DVE.wait(m)
                    ACT.wait(m)
                    m = ACT.mark(nc.scalar.activation(out=rc["rs"][:, 0:4], in_=rc["mv"][:, 0:4, 1], func=AF.Sqrt,
                                                      bias=EPS))
                    DVE.wait(m)
                    nc.vector.reciprocal(out=rc["rs"][:, 0:4], in_=rc["rs"][:, 0:4])
                    m = DVE.mark(nc.vector.tensor_copy(out=rc["rs"][:, 4:8], in_=rc["mv"][:, 0:4, 0]))
                    DVE.wait(m)
                    m = DVE.mark(nc.vector.scalar_tensor_tensor(out=rc["nb"][:, 0:4], in0=rc["rs"][:, 4:8], scalar=-1.0,
                                                                in1=rc["rs"][:, 0:4], op0=ALU.mult, op1=ALU.mult))
                    stt.wrote(m)
                    on.begin()
                    ACT.wait(m, on.prev)
                    for hl in range(4):
                        ins = nc.scalar.activation(out=on.t[:, hl, :], in_=g.t[:, hl, :], func=AF.Identity,
                                                   scale=rc["rs"][:, hl:hl + 1], bias=rc["nb"][:, hl:hl + 1])
                    m = ACT.mark(ins)
                    g.read(m); stt.read(m); on.wrote(m)
                    DVE.wait(m, gx.w, gated.prev)
                    m = DVE.mark(nc.vector.tensor_tensor(
                        out=gated.t[:, hg * 2048:(hg + 1) * 2048].rearrange("p (a b) -> p a b", a=4),
                        in0=on.t[:, :, :], in1=gx.t[:, hg * 2048:(hg + 1) * 2048].rearrange("p (a b) -> p a b", a=4),
                        op=ALU.mult))
                    on.read(m); gx.read(m); gated.wrote(m)
                state_update(S, kx, vx, tb["cdkf"], c)
                sfb.begin()
                ACT.wait(S.w, sfb.prev)
                for q4 in range(4):
                    ins = nc.scalar.activation(out=sfb.t[:, 4 * q4:4 * q4 + 4, :], in_=S.t[:, 4 * q4:4 * q4 + 4, :],
                                               func=AF.Copy, scale=kp.t[:, c + 1:c + 2])
                m = ACT.mark(ins)
                sfb.wrote(m); S.read(m)
                g = pg.next(); g.begin()
                gb = g.t.rearrange("p a b -> p (a b)").bitcast(BF16).rearrange("p (a b) -> p a b", a=32)
                PE.wait(gated.w, g.prev, ident.w)
                for ec in range(32):
                    ins = nc.tensor.transpose(out=gb[:, ec, 0:128], in_=gated.t[:, ec * 128:(ec + 1) * 128],
                                              identity=ident.t[:, :])
                pm = PE.mark(ins)
                g.wrote(pm); gated.read(pm)
                gT.begin()
                ACT.wait(pm, gT.prev)
                DVE.wait(pm, gT.prev)
                m1 = ACT.mark(nc.scalar.activation(out=gT.t[:, 0:16, :], in_=gb[:, 0:16, 0:128], func=AF.Copy))
                m2 = DVE.mark(nc.vector.tensor_copy(out=gT.t[:, 16:32, :], in_=gb[:, 16:32, 0:128]))
                g.read(m1); g.read(m2); gT.wrote(m1); gT.wrote(m2)
                store(gT, GTd[c], gT.t.rearrange("p a b -> p (a b)"), gt_sds)
            phase_sync_all()

            rd = carve2([("gt", [4, 32, 128], BF16)])
            gtb = Buf(rd["gt"], newds())
            for (c0, N) in blocks:
                nchunk = N // 128
                cb = c0 // 128
                pump(1)
                gtb.begin()
                for ci in range(nchunk):
                    load(gtb, gtb.t[:, ci, :, :].rearrange("p e f -> p (e f)"), GTd[cb + ci])
                xm.begin()

                def ev_o(gi, g, nm):
                    ACT.wait(g.w, xm.prev)
                    for m_ in range(nm):
                        ins = nc.scalar.activation(out=xm.t[:, gi * 4 + m_, 0:N], in_=g.t[:, m_, 0:N], func=AF.Copy)
                    mk = ACT.mark(ins)
                    g.read(mk); xm.wrote(mk)

                linear_fm(lambda kc: gtb.t[:, 0:nchunk, kc, :], gtb, 32, WB_OUT[j], D,
                          [[(512 * gi, 512)] for gi in range(4)], N, ev_o, ("out", j))
                back(N, c0, "g_post_mix", li, hin, hout, False)
            phase_sync_all()

        def conformer_layer(j, li, hin, hout):
            HALO = 15
            cw = carve([("u", [KC, 512], BF16), ("sig", [4, 512], F32), ("dg0", [CK, 128], BF16), ("dg1", [CK, 128], BF16)])
            ub = Buf(cw["u"]); sig = Buf(cw["sig"])
            u_sds = newds()
            dgr = Ring([Buf(cw["dg0"]), Buf(cw["dg1"])])
            for b_ in dgr.b:
                b_.sds = newds()
            wc0 = pc[("w_dw", j)]
            for kc in range(KC):
                t_ = dgr.next()
                t_.begin()
                DVE.wait(t_.prev, ident.w, prm.w)
                for k in range(CK):
                    ins = nc.vector.tensor_scalar(out=t_.t[:, k, :], in0=ident.t[:, :],
                                                  scalar1=prm.t[:, wc0 + kc * CK + k:wc0 + kc * CK + k + 1],
                                                  scalar2=None, op0=ALU.mult)
                t_.wrote(DVE.mark(ins))
                store(t_, DG[j][:, kc * CK * 128:(kc + 1) * CK * 128], t_.t.rearrange("p a b -> p (a b)"), t_.sds)
            for (c0, N) in blocks:
                pump(1)
                front(hin, c0, N, "g_pre_mix", li)
                ub.begin()
                gate_g = {}

                def ev_glu(gi, g, nm):
                    sig.begin()
                    ACT.wait(g.w, sig.prev)
                    for m_ in range(2):
                        kc = gi * 2 + m_
                        ins = nc.scalar.activation(out=sig.t[:, m_, 0:N], in_=g.t[:, 2 + m_, 0:N], func=AF.Sigmoid,
                                                   bias=P("b_pw1", j, KC + kc))
                    mk = ACT.mark(ins)
                    sig.wrote(mk)
                    DVE.wait(mk, ub.prev)
                    for m_ in range(2):
                        kc = gi * 2 + m_
                        ins = nc.vector.scalar_tensor_tensor(out=ub.t[:, kc, 0:N], in0=g.t[:, m_, 0:N],
                                                             scalar=P("b_pw1", j, kc), in1=sig.t[:, m_, 0:N],
                                                             op0=ALU.add, op1=ALU.mult)
                    mk2 = DVE.mark(ins)
                    for si, scn in enumerate(SPECIAL):
                        lo = scn * 128 - c0
                        if 0 <= lo < N:
                            DVE.wait(mk2, msk.w)
                            mk2 = DVE.mark(nc.vector.tensor_tensor(
                                out=ub.t[:, gi * 2:gi * 2 + 2, lo:lo + 128], in0=ub.t[:, gi * 2:gi * 2 + 2, lo:lo + 128],
                                in1=msk.t[:, si * 128:(si + 1) * 128].rearrange("p (a f) -> p a f", a=1).broadcast_to([128, 2, 128])
                                if False else msk.t[:, si * 128:(si + 1) * 128], op=ALU.mult)) if False else mk2
                            for m_ in range(2):
                                kc = gi * 2 + m_
                                mk2 = DVE.mark(nc.vector.tensor_tensor(out=ub.t[:, kc, lo:lo + 128],
                                                                       in0=ub.t[:, kc, lo:lo + 128],
                                                                       in1=msk.t[:, si * 128:(si + 1) * 128], op=ALU.mult))
                                DVE.wait(mk2)
                    g.read(mk2); sig.read(mk2); ub.wrote(mk2)

                linear_fm(lambda kc: hn.t[:, kc, 0:N], hn, KC, WB_PW1[j], 2 * D,
                          [[(256 * gi, 256), (D + 256 * gi, 256)] for gi in range(8)], N, ev_glu, ("pw1", j))
                store(ub, UT.rearrange("(kc p) t -> p kc t", p=128)[:, :, c0:c0 + N], ub.t[:, :, 0:N], u_sds)
            phase_sync_all()

            W_ = 512 + 2 * HALO
            cw = carve([("uh", [KC, W_], BF16), ("mu", [512], F32), ("ex2", [512], F32), ("xh", [KC, 512], BF16),
                        ("t", [512], F32)])
            uh = Buf(cw["uh"], newds()); mu = Buf(cw["mu"]); xh = Buf(cw["xh"]); tt = Buf(cw["t"])
            y = xm
            UTv = UT.rearrange("(kc p) t -> p kc t", p=128)
            for (c0, N) in blocks:
                pump(1)
                lo = max(c0 - HALO, 0)
                hi = min(c0 + N + HALO, T)
                uh.begin()
                o0 = lo - (c0 - HALO)
                if o0 > 0:
                    DVE.wait(uh.prev)
                    uh.wrote(DVE.mark(nc.vector.memset(uh.t[:, :, 0:o0], 0.0)))
                if hi < c0 + N + HALO:
                    DVE.wait(uh.prev)
                    uh.wrote(DVE.mark(nc.vector.memset(uh.t[:, :, o0 + hi - lo:N + 2 * HALO], 0.0)))
                load(uh, uh.t[:, :, o0:o0 + hi - lo], UTv[:, :, lo:hi])
                y.begin()
                for g4 in range(4):
                    g = pg.next()
                    g.begin()
                    for half in range(2):
                        kc0 = g4 * 4 + half * 2
                        sl = wring.next()
                        sl.begin()
                        slf = sl.t.rearrange("p a b -> p (a b)")
                        load(sl, slf[:, 0:2 * CK * 128], DG[j][:, kc0 * CK * 128:(kc0 + 2) * CK * 128])
                        PE.wait(sl.w, uh.w)
                        if half == 0:
                            PE.wait(g.prev)
                        for kk in range(2):
                            kc = kc0 + kk
                            for k in range(CK):
                                ins = nc.tensor.matmul(g.t[:, half * 2 + kk, 0:N],
                                                       lhsT=slf[:, (kk * CK + k) * 128:(kk * CK + k + 1) * 128],
                                                       rhs=uh.t[:, kc, k:k + N], start=(k == 0), stop=(k == CK - 1))
                        pm = PE.mark(ins)
                        sl.read(pm)
                    g.wrote(pm)
                    uh.read(pm)
                    ACT.wait(pm, y.prev, prm.w)
                    for m_ in range(4):
                        kc = g4 * 4 + m_
                        ins = nc.scalar.activation(out=y.t[:, kc, 0:N], in_=g.t[:, m_, 0:N], func=AF.Identity,
                                                   bias=P("b_dw", j, kc))
                    mk = ACT.mark(ins)
                    g.read(mk)
                    y.wrote(mk)
                hn.begin()
                xh.begin()
                ACT.wait(y.w, hn.prev, xh.prev)
                for kc in range(KC):
                    nc.scalar.activation(out=hn.t[:, kc, 0:N], in_=y.t[:, kc, 0:N], func=AF.Square)
                    ins = nc.scalar.activation(out=xh.t[:, kc, 0:N], in_=y.t[:, kc, 0:N], func=AF.Copy)
                mk = ACT.mark(ins)
                hn.wrote(mk); xh.wrote(mk)
                g = pg.next(); g.begin()
                PE.wait(mk, g.prev)
                for kc in range(KC):
                    nc.tensor.matmul(g.t[:, 0, 0:N], lhsT=ones.t[:], rhs=xh.t[:, kc, 0:N], start=(kc == 0),
                                     stop=(kc == KC - 1))
                for kc in range(KC):
                    ins = nc.tensor.matmul(g.t[:, 1, 0:N], lhsT=ones.t[:], rhs=hn.t[:, kc, 0:N], start=(kc == 0),
                                           stop=(kc == KC - 1))
                pm = PE.mark(ins)
                hn.read(pm); xh.read(pm); g.wrote(pm)
                mu.begin(); rstd.begin(); tt.begin()
                DVE.wait(pm, mu.prev, rstd.prev, tt.prev)
                nc.vector.tensor_scalar(out=mu.t[:, 0:N], in0=g.t[:, 0, 0:N], scalar1=1.0 / D, scalar2=None, op0=ALU.mult)
                m1 = DVE.mark(nc.vector.tensor_scalar(out=cw["ex2"][:, 0:N], in0=g.t[:, 1, 0:N], scalar1=1.0 / D,
                                                      scalar2=EPS, op0=ALU.mult, op1=ALU.add))
                DVE.wait(m1)
                m1 = DVE.mark(nc.vector.tensor_tensor(out=tt.t[:, 0:N], in0=mu.t[:, 0:N], in1=mu.t[:, 0:N], op=ALU.mult))
                DVE.wait(m1)
                m1 = DVE.mark(nc.vector.tensor_tensor(out=rstd.t[:, 0:N], in0=cw["ex2"][:, 0:N], in1=tt.t[:, 0:N],
                                                      op=ALU.subtract))
                ACT.wait(m1)
                m1 = ACT.mark(nc.scalar.activation(out=rstd.t[:, 0:N], in_=rstd.t[:, 0:N], func=AF.Sqrt))
                DVE.wait(m1)
                m1 = DVE.mark(nc.vector.reciprocal(out=rstd.t[:, 0:N], in_=rstd.t[:, 0:N]))
                g.read(m1); mu.wrote(m1); rstd.wrote(m1)
                xh.begin()
                lastd = m1
                for kc in range(KC):
                    DVE.wait(lastd, y.w)
                    ma = DVE.mark(nc.vector.tensor_tensor(out=y.t[:, kc, 0:N], in0=y.t[:, kc, 0:N], in1=mu.t[:, 0:N],
                                                          op=ALU.subtract))
                    DVE.wait(ma)
                    lastd = DVE.mark(nc.vector.tensor_tensor(out=y.t[:, kc, 0:N], in0=y.t[:, kc, 0:N],
                                                             in1=rstd.t[:, 0:N], op=ALU.mult))
                    ACT.wait(lastd, xh.prev)
                    mk = ACT.mark(nc.scalar.activation(out=xh.t[:, kc, 0:N], in_=y.t[:, kc, 0:N], func=AF.Silu,
                                                       scale=P("ln_g", j, kc), bias=P("ln_b", j, kc)))
                xh.wrote(mk); y.read(mk); mu.read(lastd); rstd.read(lastd)
                xm.begin()

                def ev_p2(gi, g, nm):
                    ACT.wait(g.w, xm.prev)
                    for m_ in range(nm):
                        kc = gi * 4 + m_
                        ins = nc.scalar.activation(out=xm.t[:, kc, 0:N], in_=g.t[:, m_, 0:N], func=AF.Identity,
                                                   bias=P("b_pw2", j, kc))
                    mk_ = ACT.mark(ins)
                    g.read(mk_); xm.wrote(mk_)

                linear_fm(lambda kc: xh.t[:, kc, 0:N], xh, KC, WB_PW2[j], D,
                          [[(512 * gi, 512)] for gi in range(4)], N, ev_p2, ("pw2", j))
                back(N, c0, "g_post_mix", li, hin, hout, True)
            phase_sync_all()

        def ffn_layer(li, hin, hout):
            fw = carve_at(COMMON, ARENA, [("act", [FC, 512], BF16), ("a0", [2, 514], F32), ("a1", [2, 514], F32),
                                          ("y", [2, 512], F32), ("hn2", [KC, 512], BF16), ("rstf", [512], F32)]
                          + [(f"hf{i}", [512], F32) for i in range(7)])
            hres = Ring([Buf(fw[f"hf{i}"], hresf_ds[i]) for i in range(7)])
            act = Buf(fw["act"])
            ar = Ring([Buf(fw["a0"]), Buf(fw["a1"])])
            yb = Buf(fw["y"])
            hns = [hn, Buf(fw["hn2"])]
            rstf = Buf(fw["rstf"])
            fblocks = [(s_, min(s_ + 510, T)) for s_ in range(0, T, 510)]
            NB = len(fblocks)
            hs = hin.rearrange("(kc p) t -> p kc t", p=128)

            def geom(bi):
                s0, e0 = fblocks[bi]
                lo = max(s0 - 1, 0)
                hi = min(e0 + 1, T)
                return dict(s0=s0, e0=e0, lo=lo, hi=hi, N=hi - lo, NO=e0 - s0, o0=lo - (s0 - 1))

            fst = {}

            def front_A(bi, kcs):
                ge = geom(bi)
                H_ = hns[bi % 2]
                if kcs[0] == 0:
                    H_.begin()
                    fst["fa"] = None
                last = fst["fa"]
                for kc in kcs:
                    hb = hres.next()
                    hb.begin()
                    load(hb, hb.t[:, 0:ge["N"]], hs[:, kc, ge["lo"]:ge["hi"]])
                    ACT.wait(hb.w, H_.prev)
                    last = ACT.mark(nc.scalar.activation(out=H_.t[:, kc, 0:ge["N"]], in_=hb.t[:, 0:ge["N"]],
                                                         func=AF.Square))
                    hb.read(last)
                fst["fa"] = last
                if kcs[-1] == KC - 1:
                    H_.wrote(last)

            def front_B(bi):
                ge = geom(bi)
                H_ = hns[bi % 2]
                g = pg.next()
                g.begin()
                PE.wait(H_.w, g.prev, ones.w)
                for kc in range(KC):
                    ins = nc.tensor.matmul(g.t[:, 0, 0:ge["N"]], lhsT=ones.t[:], rhs=H_.t[:, kc, 0:ge["N"]],
                                           start=(kc == 0), stop=(kc == KC - 1))
                pm = PE.mark(ins)
                H_.read(pm)
                g.wrote(pm)
                rstf.begin()
                ACT.wait(pm, rstf.prev)
                m0 = ACT.mark(nc.scalar.activation(out=rstf.t[:, 0:ge["N"]], in_=g.t[:, 0, 0:ge["N"]], func=AF.Sqrt,
                                                   scale=1.0 / D, bias=EPS))
                g.read(m0)
                DVE.wait(m0)
                m1 = DVE.mark(nc.vector.reciprocal(out=rstf.t[:, 0:ge["N"]], in_=rstf.t[:, 0:ge["N"]]))
                rstf.wrote(m1)

            def front_C(bi, kcs):
                ge = geom(bi)
                H_ = hns[bi % 2]
                if kcs[0] == 0:
                    H_.begin()
                    fst["fc"] = None
                last = fst["fc"]
                for kc in kcs:
                    hb = hres.next()
                    hb.begin()
                    load(hb, hb.t[:, 0:ge["N"]], hs[:, kc, ge["lo"]:ge["hi"]])
                    DVE.wait(hb.w, H_.prev, rstf.w, prm.w)
                    last = DVE.mark(nc.vector.scalar_tensor_tensor(out=H_.t[:, kc, 0:ge["N"]], in0=hb.t[:, 0:ge["N"]],
                                                                   scalar=P("g_pre_ffn", li, kc), in1=rstf.t[:, 0:ge["N"]],
                                                                   op0=ALU.mult, op1=ALU.mult))
                    hb.read(last)
                fst["fc"] = last
                if kcs[-1] == KC - 1:
                    H_.wrote(last)
                    rstf.read(last)

            def back_A(bi):
                ge = geom(bi)
                S_ = hns[bi % 2]
                S_.begin()
                ACT.wait(xm.w, S_.prev)
                for kc in range(KC):
                    ins = nc.scalar.activation(out=S_.t[:, kc, 0:ge["NO"]], in_=xm.t[:, kc, 0:ge["NO"]], func=AF.Square)
                m = ACT.mark(ins)
                S_.wrote(m)
                xm.read(m)

            def back_B(bi):
                ge = geom(bi)
                S_ = hns[bi % 2]
                g = pg.next()
                g.begin()
                PE.wait(S_.w, g.prev)
                for kc in range(KC):
                    ins = nc.tensor.matmul(g.t[:, 0, 0:ge["NO"]], lhsT=ones.t[:], rhs=S_.t[:, kc, 0:ge["NO"]],
                                           start=(kc == 0), stop=(kc == KC - 1))
                pm = PE.mark(ins)
                S_.read(pm)
                g.wrote(pm)
                rstd.begin()
                ACT.wait(pm, rstd.prev)
                m0 = ACT.mark(nc.scalar.activation(out=rstd.t[:, 0:ge["NO"]], in_=g.t[:, 0, 0:ge["NO"]], func=AF.Sqrt,
                                                   scale=1.0 / D, bias=EPS))
                g.read(m0)
                DVE.wait(m0)
                m1 = DVE.mark(nc.vector.reciprocal(out=rstd.t[:, 0:ge["NO"]], in_=rstd.t[:, 0:ge["NO"]]))
                rstd.wrote(m1)

            def back_C(bi, kcs):
                ge = geom(bi)
                NO = ge["NO"]
                if kcs[0] == 0:
                    fst["bc"] = None
                last = fst["bc"]
                for kc in kcs:
                    hb = hres.next()
                    hb.begin()
                    load(hb, hb.t[:, 0:NO], hs[:, kc, ge["s0"]:ge["e0"]])
                    DVE.wait(rstd.w, hb.w, xm.w, last)
                    ma = DVE.mark(nc.vector.scalar_tensor_tensor(out=xm.t[:, kc, 0:NO], in0=xm.t[:, kc, 0:NO],
                                                                 scalar=P("g_post_ffn", li, kc), in1=rstd.t[:, 0:NO],
                                                                 op0=ALU.mult, op1=ALU.mult))
                    DVE.wait(ma)
                    last = DVE.mark(nc.vector.tensor_tensor(out=xm.t[:, kc, 0:NO], in0=xm.t[:, kc, 0:NO],
                                                            in1=hb.t[:, 0:NO], op=ALU.add))
                    hb.read(last)
                fst["bc"] = last
                if kcs[-1] == KC - 1:
                    xm.wrote(last)
                    rstd.read(last)
                    store(xm, hout.rearrange("(kc p) t -> p kc t", p=128)[:, :, ge["s0"]:ge["e0"]], xm.t[:, :, 0:NO],
                          xm_sds)

            groups = []
            for gi in range((FC + 1) // 2):
                nj = min(2, FC - 2 * gi)
                groups.append([(256 * gi, 128 * nj), (FH + 256 * gi, 128 * nj)])

            front_A(0, list(range(KC))); front_B(0); front_C(0, list(range(KC)))
            for bi in range(NB):
                pump(1)
                ge = geom(bi)
                N = ge["N"]; NO = ge["NO"]; o0 = ge["o0"]; s0 = ge["s0"]; lo = ge["lo"]
                H_ = hns[bi % 2]
                act.begin()

                def ev_up(gi, g, nm, N=N, NO=NO, o0=o0, s0=s0, lo=lo):
                    nj = nm // 2
                    ab = ar.next()
                    ab.begin()
                    ACT.wait(g.w, ab.prev)
                    if o0 > 0:
                        nc.scalar.activation(out=ab.t[:, 0:nj, 0:1], in_=ab.t[:, 0:nj, 0:1], func=AF.Copy, scale=0.0)
                    if o0 + N < NO + 2:
                        nc.scalar.activation(out=ab.t[:, 0:nj, o0 + N:NO + 2], in_=ab.t[:, 0:nj, o0 + N:NO + 2],
                                             func=AF.Copy, scale=0.0)
                    mk = ACT.mark(nc.scalar.activation(out=ab.t[:, 0:nj, o0:o0 + N], in_=g.t[:, 0:nj, 0:N], func=AF.Copy))
                    ab.wrote(mk)
                    if nj == 2:
                        g.h[0].read(mk)
                    yb.begin()
                    DVE.wait(mk, yb.prev, act.prev, prm.w)
                    wl = pc[("ffn_w_dw", li)]
                    ms = [None] * nj
                    for jj in range(nj):
                        jg = gi * 2 + jj
                        ms[jj] = DVE.mark(nc.vector.tensor_scalar(out=yb.t[:, jj, 0:NO], in0=ab.t[:, jj, 0:NO],
                                                                  scalar1=prm.t[:, wl + jg * 3:wl + jg * 3 + 1],
                                                                  scalar2=P("ffn_b_dw", li, jg), op0=ALU.mult, op1=ALU.add))
                    for tap in (1, 2):
                        for jj in range(nj):
                            jg = gi * 2 + jj
                            DVE.wait(ms[jj])
                            ms[jj] = DVE.mark(nc.vector.scalar_tensor_tensor(
                                out=yb.t[:, jj, 0:NO], in0=ab.t[:, jj, tap:NO + tap],
                                scalar=prm.t[:, wl + jg * 3 + tap:wl + jg * 3 + tap + 1],
                                in1=yb.t[:, jj, 0:NO], op0=ALU.mult, op1=ALU.add))
                    last = ms[nj - 1]
                    ab.read(last)
                    ACT.wait(*ms)
                    m3 = ACT.mark(nc.scalar.activation(out=yb.t[:, 0:nj, 0:NO], in_=yb.t[:, 0:nj, 0:NO],
                                                       func=AF.Gelu_apprx_tanh))
                    DVE.wait(m3)
                    vo = s0 - lo
                    m4 = DVE.mark(nc.vector.tensor_tensor(out=act.t[:, gi * 2:gi * 2 + nj, 0:NO], in0=yb.t[:, 0:nj, 0:NO],
                                                          in1=g.t[:, nj:2 * nj, vo:vo + NO], op=ALU.mult))
                    g.read(m4); yb.wrote(m4); act.wrote(m4)

                def hook(gi, bi=bi):
                    if bi > 0:
                        if gi == 0:
                            back_A(bi - 1)
                        elif gi == 2:
                            back_B(bi - 1)
                        elif 3 <= gi < 11:
                            back_C(bi - 1, [2 * (gi - 3), 2 * (gi - 3) + 1])
                    if bi + 1 < NB:
                        if 11 <= gi < 15:
                            front_A(bi + 1, list(range(4 * (gi - 11), 4 * (gi - 11) + 4)))
                        elif gi == 16:
                            front_B(bi + 1)
                        elif 17 <= gi < 21:
                            front_C(bi + 1, list(range(4 * (gi - 17), 4 * (gi - 17) + 4)))

                linear_fm(lambda kc, H_=H_, N=N: H_.t[:, kc, 0:N], H_, KC, WB_UP[li], 2 * FH, groups, N, ev_up,
                          ("up", li), hook=hook)
                xm.begin()

                def ev_dn(gi, g, nm, NO=NO):
                    ACT.wait(g.w, xm.prev)
                    for m_ in range(nm):
                        ins = nc.scalar.activation(out=xm.t[:, gi * 4 + m_, 0:NO], in_=g.t[:, m_, 0:NO], func=AF.Copy)
                    mk_ = ACT.mark(ins)
                    g.read(mk_); xm.wrote(mk_)

                linear_fm(lambda kc, NO=NO: act.t[:, kc, 0:NO], act, FC, WB_DN[li], D,
                          [[(512 * gi, 512)] for gi in range(4)], NO, ev_dn, ("dn", li))
            back_A(NB - 1); back_B(NB - 1); back_C(NB - 1, list(range(KC)))
            phase_sync_all()

        for q in (ACT, DVE, POOL, PE):
            q.wait(prm.w, dec.w, kp.w, msk.w, ident.w, ones.w)
        cur = XT
        for li in range(DEPTH):
            j = li // 2
            mid = HB
            if li % 2 == 0:
                retention_layer(j, li, cur, mid)
            else:
                conformer_layer(j, li, cur, mid)
            nxt = YT if li == DEPTH - 1 else HA
            ffn_layer(li, mid, nxt)
            cur = nxt
        pump(10 ** 6)
        for q in (POOL,):
            q.wait(store_marks)
    return nc, pc, NP, SPECIAL, NCH, T


def _fm(vec):
    return np.ascontiguousarray(vec.reshape(-1, 128).T)


def make_core_inputs(seqs, meta, SEGC, SPECIAL, NCH, T):
    xt = np.zeros((T, D), np.float32)
    keep = np.ones(NCH + 1, np.float32)
    valid = np.zeros(T, np.float32)
    pos = np.zeros(T, np.float64)
    places = []
    c = 0
    for s in seqs:
        L = s.shape[0]
        ncs = L // 128
        keep[c] = 0.0
        r0 = c * 128 + 128 - NMETA
        xt[r0:r0 + NMETA] = meta
        xt[r0 + NMETA:r0 + NMETA + L] = s
        valid[r0:r0 + NMETA + L] = 1.0
        pos[c * 128:(c + 1 + ncs) * 128] = np.arange((1 + ncs) * 128)
        places.append((r0 + NMETA, L))
        c += 1 + ncs
    while c < NCH:
        keep[c] = 0.0
        c += 1
    keep[NCH] = 0.0
    keepb = np.concatenate([keep[1:NCH + 1], [0.0]]).astype(np.float32)
    kpv = np.concatenate([keep, keepb]).astype(np.float32)
    kp = np.ascontiguousarray(np.broadcast_to(kpv[None, :], (128, kpv.size))).astype(np.float32)
    mskv = np.concatenate([valid[sc * 128:(sc + 1) * 128] for sc in SPECIAL])
    msk = np.ascontiguousarray(np.broadcast_to(mskv[None, :], (128, mskv.size))).astype(np.float32)
    inv = 10000.0 ** (-np.arange(128, dtype=np.float32) / np.float32(128))
    ang = pos.astype(np.float32)[:, None] * inv[None, :].astype(np.float32)
    cost = np.cos(ang.astype(np.float64)).astype(np.float32)
    sint = np.sin(ang.astype(np.float64)).astype(np.float32)
    return dict(xt=np.ascontiguousarray(xt.T), kp=kp, msk=msk, cost=cost, sint=sint,
                cosf=np.ascontiguousarray(cost.T), sinf=np.ascontiguousarray(sint.T)), places


def pack_params(inp, pc, NP, DEPTH):
    prm = np.zeros((128, NP), np.float32)
    for (nm, idx), o in pc.items():
        if nm == "g_pre_mix":
            a = _fm(inp["norm_pre_mix"][idx])
        elif nm == "g_post_mix":
            a = _fm(inp["norm_post_mix"][idx])
        elif nm == "g_pre_ffn":
            a = _fm(inp["norm_pre_ffn"][idx])
        elif nm == "g_post_ffn":
            a = _fm(inp["norm_post_ffn"][idx])
        elif nm == "ffn_w_dw":
            w = inp["ffn_w_dw"][idx]
            a = np.ascontiguousarray(w.T.reshape(FC, 128, 3).transpose(1, 0, 2).reshape(128, FC * 3))
        elif nm == "ffn_b_dw":
            a = _fm(inp["ffn_b_dw"][idx])
        elif nm == "b_pw1":
            a = _fm(inp["conv_b_pw1"][idx])
        elif nm == "w_dw":
            w = inp["conv_w_dw"][idx]
            a = np.ascontiguousarray(w.T.reshape(KC, 128, CK).transpose(1, 0, 2).reshape(128, KC * CK))
        elif nm == "b_dw":
            a = _fm(inp["conv_b_dw"][idx])
        elif nm == "ln_g":
            a = _fm(inp["conv_ln_g"][idx])
        elif nm == "ln_b":
            a = _fm(inp["conv_ln_b"][idx])
        elif nm == "b_pw2":
            a = _fm(inp["conv_b_pw2"][idx])
        prm[:, o:o + a.shape[1]] = a
    return prm


_CACHE = {}


def run_model(inp, core_seqs, SEGC, DEPTH):
    key = (SEGC, DEPTH)
    if key not in _CACHE:
        _CACHE[key] = build_program(SEGC, DEPTH)
    nc, pc, NP, SPECIAL, NCH, T = _CACHE[key]
    NRET = (DEPTH + 1) // 2
    NCONV = DEPTH // 2
    prm = pack_params(inp, pc, NP, DEPTH)
    decv = np.zeros((NRET * 16,), np.float32)
    for j in range(NRET):
        decv[j * 16:j * 16 + 8] = inp["ret_decay_fwd"][j]
        decv[j * 16 + 8:j * 16 + 16] = inp["ret_decay_bwd"][j]
    dec = np.ascontiguousarray(np.broadcast_to(decv[None, :], (128, decv.size))).astype(np.float32)

    def flat(a, n):
        a = np.ascontiguousarray(a[:n]) if n > 0 else np.zeros((1,) + a.shape[1:], np.float32)
        return a.reshape(-1, 2048)

    shared = dict(prm=prm, dec=dec,
                  ret_w_in=flat(inp["ret_w_in"], NRET), ret_w_out=flat(inp["ret_w_out"], NRET),
                  conv_w_pw1=flat(inp["conv_w_pw1"], NCONV), conv_w_pw2=flat(inp["conv_w_pw2"], NCONV),
                  ffn_w_up=flat(inp["ffn_w_up"], DEPTH), ffn_w_down=flat(inp["ffn_w_down"], DEPTH))
    in_maps = []
    places = []
    for seqs in core_seqs:
        d, pl = make_core_inputs(seqs, inp["meta_tokens"], SEGC, SPECIAL, NCH, T)
        d.update(shared)
        in_maps.append(d)
        places.append(pl)
    res = run_bass_kernel_spmd(nc, in_maps, core_ids=list(range(len(core_seqs))))
    outs = []
    for ci, pl in enumerate(places):
        yt = res.results[ci]["yt"]
        y = yt.T
        outs.append([np.ascontiguousarray(y[r0:r0 + L]) for (r0, L) in pl])
    return outs


def kernel(**inp):
    xp = np.asarray(inp["x_prompt"], np.float32)
    xs = np.asarray(inp["x_sample"], np.float32)
    SEGC = xp.shape[1] // 128
    DEPTH = 4
    core_seqs = []
    for c in range(4):
        core_seqs.append([xs[c], xp[c]])
    for c in range(4):
        core_seqs.append([xp[4 + 3 * c + k] for k in range(3)])
    outs = run_model(inp, core_seqs, SEGC, DEPTH)
    yp = np.zeros_like(xp)
    ys = np.zeros_like(xs)
    for c in range(4):
        ys[c] = outs[c][0]
        yp[c] = outs[c][1]
    for c in range(4):
        for k in range(3):
            yp[4 + 3 * c + k] = outs[4 + c][k]
    return (yp, ys)
```

```python
import numpy as np
from contextlib import ExitStack
import concourse.bass as bass
import concourse.mybir as mybir
from concourse.bass_utils import run_bass_kernel_spmd

F32 = mybir.dt.float32
BF16 = mybir.dt.bfloat16
I32 = mybir.dt.int32
AF = mybir.ActivationFunctionType
ALU = mybir.AluOpType

D = 2048
KC = 16
H = 8
HV = 4096
FH = 5504
FC = 43
CK = 31
NMETA = 16
EPS = 1e-6
LN16 = float(np.log(1.0 / 16.0))


_UID = [0]


def _next_uid():
    _UID[0] += 1
    return _UID[0]


class Q:
    def __init__(self, nc, eng, name, es, step=1):
        self.uid = _next_uid()
        self.e = eng
        self.sem = es.enter_context(nc.semaphore(name))
        self.n = 0
        self.step = step
        self.seen = {}

    def mark(self, ins):
        ins.then_inc(self.sem, self.step)
        self.n += self.step
        return (self, self.n)

    def wait(self, *marks):
        for m in marks:
            if m is None:
                continue
            if isinstance(m, dict):
                self.wait(*m.values())
                continue
            src, n = m
            if self.seen.get(src.uid, 0) >= n:
                continue
            self.e.wait_ge(src.sem, n)
            self.seen[src.uid] = n


class DS:
    def __init__(self, nc, name, es):
        self.uid = _next_uid()
        self.sem = es.enter_context(nc.semaphore(name))
        self.n = 0

    def mark(self, ins):
        ins.then_inc(self.sem, 16)
        self.n += 16
        return (self, self.n)


def _merge(d, m):
    if m is None:
        return
    src, n = m
    k = src.uid
    if k not in d or d[k][1] < n:
        d[k] = (src, n)


class Buf:
    def __init__(self, t, ds=None):
        self.t = t
        self.w = {}
        self.r = {}
        self.prev = {}
        self.ds = ds

    def begin(self):
        self.prev = {}
        for m in list(self.w.values()) + list(self.r.values()):
            _merge(self.prev, m)
        self.w = {}
        self.r = {}

    def wrote(self, m):
        _merge(self.w, m)

    def read(self, m):
        _merge(self.r, m)


class PGroup:
    def __init__(self, t):
        self.t = t
        self.h = [Buf(t[:, 0:2, :]), Buf(t[:, 2:4, :])]

    def begin(self):
        for h in self.h:
            h.begin()

    def _m(self, attr):
        d = {}
        for h in self.h:
            for m in getattr(h, attr).values():
                _merge(d, m)
        return d

    @property
    def prev(self):
        return self._m("prev")

    @property
    def w(self):
        return self._m("w")

    def wrote(self, m):
        for h in self.h:
            h.wrote(m)

    def read(self, m):
        for h in self.h:
            h.read(m)


class Ring:
    def __init__(self, bufs):
        self.b = bufs
        self.i = 0

    def next(self):
        b = self.b[self.i % len(self.b)]
        self.i += 1
        return b


def build_program(SEGC, DEPTH):
    NCH = 3 * (SEGC + 1)
    T = NCH * 128
    NRET = (DEPTH + 1) // 2
    NCONV = DEPTH // 2
    SPECIAL = sorted({0, 1 + SEGC, 2 + 2 * SEGC, 1 + 2 * SEGC, 2 + 3 * SEGC})
    pc = {}
    off = 0
    for i in range(DEPTH):
        for nm, w in (("g_pre_mix", KC), ("g_post_mix", KC), ("g_pre_ffn", KC), ("g_post_ffn", KC),
                      ("ffn_w_dw", FC * 3), ("ffn_b_dw", FC)):
            pc[(nm, i)] = off
            off += w
    for j in range(NCONV):
        for nm, w in (("b_pw1", 2 * KC), ("w_dw", KC * CK), ("b_dw", KC), ("ln_g", KC), ("ln_b", KC), ("b_pw2", KC)):
            pc[(nm, j)] = off
            off += w
    NP = off

    nc = bass.Bass("TRN2", target_bir_lowering=False)
    dt = nc.dram_tensor

    def din(name, shape, dtp=F32):
        return dt(name, shape, dtp, kind="ExternalInput").ap()

    def dsc(name, shape, dtp):
        return dt(name, shape, dtp, kind="Internal").ap()

    XT = din("xt", [D, T])
    PRM = din("prm", [128, NP])
    DEC = din("dec", [128, NRET * 16])
    KPD = din("kp", [128, 2 * NCH + 2])
    MSKD = din("msk", [128, len(SPECIAL) * 128])
    COSF = din("cosf", [128, T])
    SINF = din("sinf", [128, T])
    COST = din("cost", [T, 128])
    SINT = din("sint", [T, 128])
    W_IN = din("ret_w_in", [NRET * D * 12288 // 2048, 2048])
    W_OUT = din("ret_w_out", [NRET * HV * D // 2048, 2048])
    W_PW1 = din("conv_w_pw1", [max(NCONV, 1) * D * 2 * D // 2048, 2048])
    W_PW2 = din("conv_w_pw2", [max(NCONV, 1) * D * D // 2048, 2048])
    W_UP = din("ffn_w_up", [DEPTH * D * 2 * FH // 2048, 2048])
    W_DN = din("ffn_w_down", [DEPTH * FH * D // 2048, 2048])
    YT = dt("yt", [D, T], F32, kind="ExternalOutput").ap()

    WB_IN = [dsc(f"wb_in{j}", [D * 12288 // 2048, 2048], BF16) for j in range(NRET)]
    WB_OUT = [dsc(f"wb_out{j}", [HV * D // 2048, 2048], BF16) for j in range(NRET)]
    WB_PW1 = [dsc(f"wb_pw1{j}", [D * 2 * D // 2048, 2048], BF16) for j in range(NCONV)]
    WB_PW2 = [dsc(f"wb_pw2{j}", [D * D // 2048, 2048], BF16) for j in range(NCONV)]
    WB_UP = [dsc(f"wb_up{i}", [D * 2 * FH // 2048, 2048], BF16) for i in range(DEPTH)]
    WB_DN = [dsc(f"wb_dn{i}", [FH * D // 2048, 2048], BF16) for i in range(DEPTH)]
    HA = dsc("ha", [D, T], F32)
    HB = dsc("hb", [D, T], F32)
    UT = dsc("ut", [D, T], BF16)
    DG = [dsc(f"dg{j}", [128, KC * CK * 128], BF16) for j in range(NCONV)]
    QTd = dsc("qtd", [NCH, 128, D], BF16)
    KTd = dsc("ktd", [NCH, 128, D], BF16)
    KFd = dsc("kfd", [T, D], BF16)
    KBd = dsc("kbd", [T, D], BF16)
    Vd = dsc("vd", [T, HV], BF16)
    Gd = dsc("gd", [T, HV], BF16)
    SBd = dsc("sbd", [NCH, 128, 16 * 512], BF16)
    GTd = dsc("gtd", [NCH, 128, HV], BF16)

    es = ExitStack()
    with es:
        def sb(name, shape, dtp):
            return es.enter_context(nc.sbuf_tensor("sb_" + name, shape, dtp))

        PE = Q(nc, nc.tensor, "s_pe", es)
        ACT = Q(nc, nc.scalar, "s_act", es)
        DVE = Q(nc, nc.vector, "s_dve", es)
        POOL = Q(nc, nc.gpsimd, "s_pool", es)
        SP = Q(nc, nc.sync, "s_sp", es)
        dcount = [0]

        ds_all = {False: [], True: []}
        ds_i = {False: 0, True: 0}

        def newds(st=False):
            pool_ = ds_all[st]
            if ds_i[st] < len(pool_):
                d_ = pool_[ds_i[st]]
            else:
                dcount[0] += 1
                d_ = DS(nc, f"ds{dcount[0]}", es)
                pool_.append(d_)
            ds_i[st] += 1
            return d_

        def load(buf, out_ap, in_ap, extra=()):
            SP.wait(buf.prev, *extra)
            m = buf.ds.mark(nc.sync.dma_start(out=out_ap, in_=in_ap))
            buf.wrote(m)
            return m

        store_marks = {}

        def store(buf, out_ap, in_ap, sds):
            POOL.wait(buf.w)
            m = sds.mark(nc.gpsimd.dma_start(out=out_ap, in_=in_ap))
            buf.read(m)
            _merge(store_marks, m)
            return m

        def phase_barrier():
            SP.wait(store_marks)

        cast_q = []
        wready = {}

        def plan_cast(key, src, row0, nrows, dst):
            ds_ = newds(True)
            n = 0
            for r0 in range(0, nrows, 2048):
                rn = min(2048, nrows - r0)
                cast_q.append((ds_, dst[r0:r0 + rn, :], src[row0 + r0:row0 + r0 + rn, :]))
                n += 16
            wready[key] = (ds_, n)

        for i in range(DEPTH):
            j = i // 2
            if i % 2 == 0:
                plan_cast(("in", j), W_IN, j * (D * 12288 // 2048), D * 12288 // 2048, WB_IN[j])
                plan_cast(("out", j), W_OUT, j * (HV * D // 2048), HV * D // 2048, WB_OUT[j])
            else:
                plan_cast(("pw1", j), W_PW1, j * (D * 2 * D // 2048), D * 2 * D // 2048, WB_PW1[j])
                plan_cast(("pw2", j), W_PW2, j * (D * D // 2048), D * D // 2048, WB_PW2[j])
            plan_cast(("up", i), W_UP, i * (D * 2 * FH // 2048), D * 2 * FH // 2048, WB_UP[i])
            plan_cast(("dn", i), W_DN, i * (FH * D // 2048), FH * D // 2048, WB_DN[i])
        cast_pos = [0]

        def pump(n):
            for _ in range(n):
                if cast_pos[0] >= len(cast_q):
                    return
                ds_, o, i_ = cast_q[cast_pos[0]]
                cast_pos[0] += 1
                ds_.mark(nc.gpsimd.dma_start(out=o, in_=i_))

        def wwait(key):
            ds_, n = wready[key]
            while ds_.n < n:
                pump(1)
            SP.wait((ds_, n))

        prm = Buf(sb("prm", [128, NP], F32), newds())
        dec = Buf(sb("dec", [128, NRET * 16], F32), newds())
        kp = Buf(sb("kp", [128, 2 * NCH + 2], F32), newds())
        msk = Buf(sb("msk", [128, len(SPECIAL) * 128], F32), newds())
        ones = Buf(sb("ones", [128, 128], BF16))
        ident = Buf(sb("ident", [128, 128], BF16))
        iof = Buf(sb("iof", [128, 128], F32))
        iop = Buf(sb("iop", [128, 1], F32))
        ioi = Buf(sb("ioi", [128, 128], I32))
        ipi = Buf(sb("ipi", [128, 1], I32))
        load(prm, prm.t[:], PRM)
        load(dec, dec.t[:], DEC)
        load(kp, kp.t[:], KPD)
        load(msk, msk.t[:], MSKD)
        ones.wrote(DVE.mark(nc.vector.memset(ones.t[:], 1.0)))
        ioi.wrote(POOL.mark(nc.gpsimd.iota(ioi.t[:], pattern=[[1, 128]], base=0, channel_multiplier=0)))
        ipi.wrote(POOL.mark(nc.gpsimd.iota(ipi.t[:], pattern=[[0, 1]], base=0, channel_multiplier=1)))
        DVE.wait(ioi.w, ipi.w)
        iof.wrote(DVE.mark(nc.vector.tensor_copy(out=iof.t[:], in_=ioi.t[:])))
        iop.wrote(DVE.mark(nc.vector.tensor_copy(out=iop.t[:], in_=ipi.t[:])))
        DVE.wait(iof.w, iop.w)
        ident.wrote(DVE.mark(nc.vector.tensor_scalar(out=ident.t[:], in0=iof.t[:], scalar1=iop.t[:, 0:1],
                                                     scalar2=None, op0=ALU.is_equal)))

        def P(nm, idx, col=0, n=1):
            o = pc[(nm, idx)] + col
            return prm.t[:, o:o + n]

        pg = Ring([PGroup(es.enter_context(nc.psum_tensor(f"pg{i}", [128, 4, 512], F32))) for i in range(2)])
        ARENA = 194 * 1024
        COMMON = 104 * 1024
        TBL = 20 * 1024
        work = sb("arena", [128, ARENA // 4], F32)

        def carve_at(base, limit, spec):
            res = {}
            o = base // 4
            for nm, shp, dtp in spec:
                nel = int(np.prod(shp))
                nbytes = nel * (4 if dtp in (F32, I32) else 2)
                nw = (nbytes + 3) // 4
                ap = work[:, o:o + nw]
                if dtp != F32:
                    ap = ap.bitcast(dtp)[:, 0:nel]
                names = "abcd"[:len(shp)]
                if len(shp) > 1:
                    kw = {names[i]: shp[i] for i in range(len(shp) - 1)}
                    ap = ap.rearrange("p (" + " ".join(names) + ") -> p " + " ".join(names), **kw)
                res[nm] = ap
                o += nw
            assert o * 4 <= limit, (o * 4, limit)
            return res

        cm = carve_at(0, COMMON, [("xm", [KC, 512], F32), ("hn", [KC, 512], BF16), ("w0", [16, 512], BF16),
                                  ("w1", [16, 512], BF16), ("w2", [16, 512], BF16), ("rstd", [512], F32),
                                  ("h0", [512], F32), ("h1", [512], F32), ("h2", [512], F32)])
        wring = Ring([Buf(cm[f"w{i}"], newds()) for i in range(3)])
        xm = Buf(cm["xm"], newds())
        hn = Buf(cm["hn"])
        rstd = Buf(cm["rstd"])
        hres = Ring([Buf(cm[f"h{i}"], newds()) for i in range(3)])
        xm_sds = newds(True)
        hresf_ds = [newds() for _ in range(7)]

        def carve(spec):
            return carve_at(COMMON, ARENA - TBL, spec)

        work_guard = Buf(work)

        def phase_sync_all():
            marks = []
            for q in (PE, ACT, DVE, POOL):
                if q.n > 0:
                    q.e.wait_ge(q.sem, q.n)
                marks.append(q.mark(q.e.nop(nofuse=True)))
            for q in (PE, ACT, DVE, POOL, SP):
                q.wait(*marks)
                q.wait(store_marks)

        def front(src, c0, N, gname, li):
            xm.begin()
            load(xm, xm.t[:, :, 0:N], src.rearrange("(kc p) t -> p kc t", p=128)[:, :, c0:c0 + N])
            hn.begin()
            ACT.wait(xm.w, hn.prev)
            for kc in range(KC):
                ins = nc.scalar.activation(out=hn.t[:, kc, 0:N], in_=xm.t[:, kc, 0:N], func=AF.Square)
            m = ACT.mark(ins)
            hn.wrote(m)
            g = pg.next()
            g.begin()
            PE.wait(hn.w, g.prev, ones.w)
            for kc in range(KC):
                ins = nc.tensor.matmul(g.t[:, 0, 0:N], lhsT=ones.t[:], rhs=hn.t[:, kc, 0:N],
                                       start=(kc == 0), stop=(kc == KC - 1))
            pm = PE.mark(ins)
            hn.read(pm)
            rstd.begin()
            ACT.wait(pm, rstd.prev)
            m0 = ACT.mark(nc.scalar.activation(out=rstd.t[:, 0:N], in_=g.t[:, 0, 0:N], func=AF.Sqrt, scale=1.0 / D,
                                               bias=EPS))
            DVE.wait(m0)
            m1 = DVE.mark(nc.vector.reciprocal(out=rstd.t[:, 0:N], in_=rstd.t[:, 0:N]))
            g.read(m0)
            g.read(m1)
            rstd.wrote(m1)
            hn.begin()
            DVE.wait(m1, hn.prev, prm.w)
            for kc in range(KC):
                ins = nc.vector.scalar_tensor_tensor(out=hn.t[:, kc, 0:N], in0=xm.t[:, kc, 0:N],
                                                     scalar=P(gname, li, kc), in1=rstd.t[:, 0:N],
                                                     op0=ALU.mult, op1=ALU.mult)
            m2 = DVE.mark(ins)
            hn.wrote(m2)
            xm.read(m2)
            rstd.read(m2)

        def back(N, c0, gname, li, hsrc, dst, special_mask):
            hn.begin()
            ACT.wait(xm.w, hn.prev)
            for kc in range(KC):
                ins = nc.scalar.activation(out=hn.t[:, kc, 0:N], in_=xm.t[:, kc, 0:N], func=AF.Square)
            m = ACT.mark(ins)
            hn.wrote(m)
            xm.read(m)
            g = pg.next()
            g.begin()
            PE.wait(hn.w, g.prev)
            for kc in range(KC):
                ins = nc.tensor.matmul(g.t[:, 0, 0:N], lhsT=ones.t[:], rhs=hn.t[:, kc, 0:N],
                                       start=(kc == 0), stop=(kc == KC - 1))
            pm = PE.mark(ins)
            hn.read(pm)
            rstd.begin()
            ACT.wait(pm, rstd.prev)
            m0 = ACT.mark(nc.scalar.activation(out=rstd.t[:, 0:N], in_=g.t[:, 0, 0:N], func=AF.Sqrt, scale=1.0 / D,
                                               bias=EPS))
            DVE.wait(m0)
            m1 = DVE.mark(nc.vector.reciprocal(out=rstd.t[:, 0:N], in_=rstd.t[:, 0:N]))
            g.read(m0)
            g.read(m1)
            rstd.wrote(m1)
            hs = hsrc.rearrange("(kc p) t -> p kc t", p=128)
            last = None
            for kc in range(KC):
                hb = hres.next()
                hb.begin()
                load(hb, hb.t[:, 0:N], hs[:, kc, c0:c0 + N])
                DVE.wait(m1, hb.w, xm.w, last)
                ma = DVE.mark(nc.vector.scalar_tensor_tensor(out=xm.t[:, kc, 0:N], in0=xm.t[:, kc, 0:N],
                                                             scalar=P(gname, li, kc), in1=rstd.t[:, 0:N],
                                                             op0=ALU.mult, op1=ALU.mult))
                DVE.wait(ma)
                last = DVE.mark(nc.vector.tensor_tensor(out=xm.t[:, kc, 0:N], in0=xm.t[:, kc, 0:N],
                                                        in1=hb.t[:, 0:N], op=ALU.add))
                hb.read(last)
                if special_mask:
                    for si, sc in enumerate(SPECIAL):
                        lo = sc * 128 - c0
                        if 0 <= lo < N:
                            DVE.wait(last, msk.w)
                            last = DVE.mark(nc.vector.tensor_tensor(
                                out=xm.t[:, kc, lo:lo + 128], in0=xm.t[:, kc, lo:lo + 128],
                                in1=msk.t[:, si * 128:(si + 1) * 128], op=ALU.mult))
            xm.wrote(last)
            rstd.read(last)
            store(xm, dst.rearrange("(kc p) t -> p kc t", p=128)[:, :, c0:c0 + N], xm.t[:, :, 0:N], xm_sds)

        def load_slab(WB, ncolsW, k0, kn, pieces, wkey):
            W2 = WB.rearrange("a b -> (a b)").rearrange("(k n) -> k n", n=ncolsW)
            sl = wring.next()
            sl.begin()
            o = 0
            for (cc, ncol) in pieces:
                load(sl, sl.t[:, 0:kn, o:o + ncol],
                     W2[k0 * 128:(k0 + kn) * 128, cc:cc + ncol].rearrange("(kc p) n -> p kc n", p=128))
                o += ncol
            return sl

        def linear_fm(xrhs, xbuf, KCin, WB, ncolsW, groups, N, evac, wkey, hook=None):
            wwait(wkey)
            for gi, pieces in enumerate(groups):
                if hook is not None:
                    hook(gi)
                ncols = sum(p[1] for p in pieces)
                nm = ncols // 128
                g = pg.next()
                g.begin()
                first = True
                for k0 in range(0, KCin, 16):
                    kn = min(16, KCin - k0)
                    sl = load_slab(WB, ncolsW, k0, kn, pieces, wkey)
                    PE.wait(sl.w, xbuf.w)
                    for m in range(nm):
                        if first and m == 0:
                            PE.wait(g.h[0].prev)
                        if first and (m == 2 or (m == 0 and nm <= 2 and False)):
                            PE.wait(g.h[1].prev)
                        for kc in range(kn):
                            ins = nc.tensor.matmul(g.t[:, m, 0:N], lhsT=sl.t[:, kc, m * 128:(m + 1) * 128],
                                                   rhs=xrhs(k0 + kc), start=(k0 + kc == 0),
                                                   stop=(k0 + kc == KCin - 1))
                    pm = PE.mark(ins)
                    sl.read(pm)
                    first = False
                xbuf.read(pm)
                g.wrote(pm)
                evac(gi, g, nm)

        def linear_tok(xbuf, nchunk, WB, ncolsW, groups, evac, wkey):
            wwait(wkey)
            for gi, pieces in enumerate(groups):
                g = pg.next()
                g.begin()
                sl = load_slab(WB, ncolsW, 0, KC, pieces, wkey)
                PE.wait(sl.w, xbuf.w, g.prev)
                for ci in range(nchunk):
                    for kc in range(KC):
                        ins = nc.tensor.matmul(g.t[:, ci, :], lhsT=xbuf.t[:, kc, ci * 128:(ci + 1) * 128],
                                               rhs=sl.t[:, kc, :], start=(kc == 0), stop=(kc == KC - 1))
                pm = PE.mark(ins)
                sl.read(pm)
                xbuf.read(pm)
                g.wrote(pm)
                evac(gi, g, nchunk)

        blocks = [(c0, min(512, T - c0)) for c0 in range(0, T, 512)]

        def retention_layer(j, li, hin, hout):
            tb = carve_at(ARENA - TBL, ARENA, [("lg", [16], F32), ("nlg", [16], F32), ("bq", [16], F32), ("t1", [128], F32),
                        ("t2", [128], F32), ("t3", [128], F32), ("dmt", [8, 128], F32),
                        ("fq", [16, 128], BF16), ("fb", [16, 128], BF16), ("dkf", [8], F32), ("dkb", [8], F32),
                        ("cd", [16], F32), ("cdkf", [NCH, 8], F32), ("cdkb", [NCH, 8], F32), ("pj", [2], F32),
                        ("fq32", [128], F32)])
            TB = Buf(work)
            TB.begin()
            d0 = j * 16
            for q in (ACT, DVE):
                q.wait(dec.w, kp.w, iof.w, iop.w)
            a1 = ACT.mark(nc.scalar.activation(out=tb["lg"][:, :], in_=dec.t[:, d0:d0 + 16], func=AF.Exp, scale=-1.0))
            ACT.wait(a1)
            a2 = ACT.mark(nc.scalar.activation(out=tb["nlg"][:, :], in_=tb["lg"][:, :], func=AF.Ln, bias=1.0))
            DVE.wait(a2)
            v1 = DVE.mark(nc.vector.tensor_scalar(out=tb["lg"][:, :], in0=tb["nlg"][:, :], scalar1=-1.0,
                                                  scalar2=None, op0=ALU.mult))
            DVE.wait(v1)
            nc.vector.tensor_scalar(out=tb["pj"][:, 0:1], in0=iop.t[:, 0:1], scalar1=-1.0, scalar2=127.0,
                                    op0=ALU.mult, op1=ALU.add)
            nc.vector.tensor_copy(out=tb["pj"][:, 1:2], in_=iop.t[:, 0:1])
            nc.vector.tensor_scalar(out=tb["bq"][:, 0:8], in0=tb["lg"][:, 0:8], scalar1=LN16, scalar2=None, op0=ALU.add)
            nc.vector.tensor_scalar(out=tb["bq"][:, 8:16], in0=tb["lg"][:, 8:16], scalar1=128.0, scalar2=LN16,
                                    op0=ALU.mult, op1=ALU.add)
            v2 = DVE.mark(nc.vector.tensor_scalar(out=tb["cd"][:, :], in0=tb["lg"][:, :], scalar1=128.0, scalar2=None,
                                                  op0=ALU.mult))
            DVE.wait(v2)
            nc.vector.tensor_scalar(out=tb["dkf"][:, :], in0=tb["lg"][:, 0:8], scalar1=tb["pj"][:, 0:1], scalar2=None,
                                    op0=ALU.mult)
            nc.vector.tensor_scalar(out=tb["dkb"][:, :], in0=tb["lg"][:, 8:16], scalar1=tb["pj"][:, 1:2], scalar2=None,
                                    op0=ALU.mult)
            v3 = DVE.mark(nc.vector.tensor_scalar(out=tb["t3"][:, :], in0=iof.t[:, :], scalar1=iop.t[:, 0:1],
                                                  scalar2=None, op0=ALU.subtract))
            DVE.wait(v3)
            nc.vector.tensor_scalar(out=tb["t1"][:, :], in0=tb["t3"][:, :], scalar1=0.0, scalar2=None, op0=ALU.max)
            v4 = DVE.mark(nc.vector.tensor_scalar(out=tb["t2"][:, :], in0=tb["t3"][:, :], scalar1=-1.0, scalar2=0.0,
                                                  op0=ALU.mult, op1=ALU.max))
            ACT.wait(v4)
            e1 = ACT.mark(nc.scalar.activation(out=tb["cd"][:, :], in_=tb["cd"][:, :], func=AF.Exp))
            nc.scalar.activation(out=tb["dkf"][:, :], in_=tb["dkf"][:, :], func=AF.Exp)
            e2 = ACT.mark(nc.scalar.activation(out=tb["dkb"][:, :], in_=tb["dkb"][:, :], func=AF.Exp))
            for h in range(H):
                DVE.wait(v4, e2)
                nc.vector.tensor_scalar(out=tb["t3"][:, :], in0=tb["t1"][:, :], scalar1=tb["lg"][:, h:h + 1],
                                        scalar2=None, op0=ALU.mult)
                vv = DVE.mark(nc.vector.tensor_scalar(out=tb["fq32"][:, :], in0=tb["t2"][:, :],
                                                      scalar1=tb["lg"][:, 8 + h:9 + h], scalar2=None, op0=ALU.mult))
                DVE.wait(vv)
                vv = DVE.mark(nc.vector.tensor_tensor(out=tb["t3"][:, :], in0=tb["t3"][:, :], in1=tb["fq32"][:, :],
                                                      op=ALU.add))
                ACT.wait(vv)
                e2 = ACT.mark(nc.scalar.activation(out=tb["dmt"][:, h, :], in_=tb["t3"][:, :], func=AF.Exp,
                                                   bias=LN16 if False else 0.0))
                for dcc in range(2):
                    nc.scalar.activation(out=tb["fq"][:, 2 * h + dcc, :], in_=iof.t[:, :], func=AF.Exp,
                                         scale=tb["lg"][:, h:h + 1], bias=tb["bq"][:, h:h + 1])
                    e2 = ACT.mark(nc.scalar.activation(out=tb["fb"][:, 2 * h + dcc, :], in_=iof.t[:, :], func=AF.Exp,
                                                       scale=tb["nlg"][:, 8 + h:9 + h], bias=tb["bq"][:, 8 + h:9 + h]))
            DVE.wait(e1, e2)
            vv = DVE.mark(nc.vector.tensor_scalar(out=tb["dmt"][:, :, :], in0=tb["dmt"][:, :, :], scalar1=1.0 / 16.0,
                                                  scalar2=None, op0=ALU.mult))
            for c in range(NCH):
                nc.vector.tensor_scalar(out=tb["cdkf"][:, c, :], in0=tb["cd"][:, 0:8], scalar1=kp.t[:, c:c + 1],
                                        scalar2=None, op0=ALU.mult)
                vv = DVE.mark(nc.vector.tensor_scalar(out=tb["cdkb"][:, c, :], in0=tb["cd"][:, 8:16],
                                                      scalar1=kp.t[:, NCH + 1 + c:NCH + 2 + c], scalar2=None,
                                                      op0=ALU.mult))
            TBM = vv
            for q in (ACT, DVE, POOL, PE):
                q.wait(TBM, e2)
            def carve2(spec, full=False):
                return carve_at(0 if full else COMMON, ARENA - TBL, spec)

            ra = carve2([("qo", [4, 16, 128], BF16), ("ko", [4, 16, 128], BF16),
                         ("vt0", [4, 512], BF16),
                         ("kf0", [4, 512], BF16),
                         ("kb0", [4, 512], BF16),
                         ("orot", [4, 2, 2, 128], F32), ("ta", [512], F32), ("tb", [512], F32),
                         ("tc", [512], F32), ("td", [512], F32),
                         ("cf", [512], F32), ("sf", [512], F32), ("ct", [4, 128], F32), ("st", [4, 128], F32)])
            qo = Buf(ra["qo"]); ko = Buf(ra["ko"])
            vtr = Ring([Buf(ra["vt0"])])
            kfr = Ring([Buf(ra["kf0"])])
            kbr = Ring([Buf(ra["kb0"])])
            orot = Buf(ra["orot"])
            tmp = Buf(ra["ta"])
            cf = Buf(ra["cf"], newds()); sf = Buf(ra["sf"], newds())
            ct = Buf(ra["ct"], newds()); st = Buf(ra["st"], newds())
            sd = {k: newds(True) for k in ("q", "k", "v", "g", "kf", "kb")}
            WB = WB_IN[j]
            for (c0, N) in blocks:
                nchunk = N // 128
                cb = c0 // 128
                pump(1)
                front(hin, c0, N, "g_pre_mix", li)
                for b_, src_ in ((cf, COSF), (sf, SINF)):
                    b_.begin()
                    load(b_, b_.t[:, 0:N], src_[:, c0:c0 + N])
                for b_, src_ in ((ct, COST), (st, SINT)):
                    b_.begin()
                    load(b_, b_.t[:, 0:nchunk, :], src_[c0:c0 + N, :].rearrange("(c p) f -> p c f", p=128))

                def ev_v(gi, g, nch_, dst=Vd, ring=vtr, func=AF.Copy, key="v"):
                    vt = ring.next()
                    vt.begin()
                    ACT.wait(g.w, vt.prev)
                    m = ACT.mark(nc.scalar.activation(out=vt.t[:, 0:nch_, :], in_=g.t[:, 0:nch_, :], func=func))
                    g.read(m)
                    vt.wrote(m)
                    store(vt, dst[c0:c0 + N, gi * 512:(gi + 1) * 512].rearrange("(c p) f -> p c f", p=128),
                          vt.t[:, 0:nch_, :], sd[key])

                linear_tok(hn, nchunk, WB, 12288, [[(4096 + 512 * gi, 512)] for gi in range(8)], ev_v, ("in", j))
                linear_tok(hn, nchunk, WB, 12288, [[(8192 + 512 * gi, 512)] for gi in range(8)],
                           lambda gi, g, n_: ev_v(gi, g, n_, Gd, vtr, AF.Silu, "g"), ("in", j))

                def ev_ktok(gi, g, nch_):
                    gv = g.t.rearrange("p c (a b f) -> p c a b f", a=2, b=2)
                    orot.begin()
                    tmp.begin()
                    DVE.wait(g.w, orot.prev, tmp.prev, ct.w, st.w)
                    last = None
                    for hh in range(2):
                        x1 = gv[:, 0:nch_, hh, 0, :]
                        x2 = gv[:, 0:nch_, hh, 1, :]
                        cs = ct.t[:, 0:nch_, :]
                        sn = st.t[:, 0:nch_, :]
                        ta = ra["ta"].rearrange("p (c f) -> p c f", c=4)[:, 0:nch_, :]
                        tbb = ra["tb"].rearrange("p (c f) -> p c f", c=4)[:, 0:nch_, :]
                        tcc = ra["tc"].rearrange("p (c f) -> p c f", c=4)[:, 0:nch_, :]
                        tdd = ra["td"].rearrange("p (c f) -> p c f", c=4)[:, 0:nch_, :]
                        DVE.wait(last)
                        nc.vector.tensor_tensor(out=ta, in0=x1, in1=cs, op=ALU.mult)
                        nc.vector.tensor_tensor(out=tbb, in0=x2, in1=sn, op=ALU.mult)
                        nc.vector.tensor_tensor(out=tcc, in0=x2, in1=cs, op=ALU.mult)
                        mm = DVE.mark(nc.vector.tensor_tensor(out=tdd, in0=x1, in1=sn, op=ALU.mult))
                        DVE.wait(mm)
                        nc.vector.tensor_tensor(out=orot.t[:, 0:nch_, hh, 0, :], in0=ta, in1=tbb, op=ALU.subtract)
                        last = DVE.mark(nc.vector.tensor_tensor(out=orot.t[:, 0:nch_, hh, 1, :], in0=tcc, in1=tdd,
                                                                op=ALU.add))
                    g.read(last)
                    ct.read(last); st.read(last)
                    orot.wrote(last)
                    tmp.wrote(last)
                    kf = kfr.next(); kb = kbr.next()
                    kf.begin(); kb.begin()
                    ACT.wait(orot.w, kf.prev, kb.prev, TBM)
                    for hh in range(2):
                        h = 2 * gi + hh
                        src_ = orot.t[:, 0:nch_, hh, :, :]
                        nc.scalar.activation(out=kf.t.rearrange("p c (a b f) -> p c a b f", a=2, b=2)[:, 0:nch_, hh, :, :],
                                             in_=src_, func=AF.Copy, scale=tb["dkf"][:, h:h + 1])
                        mk = ACT.mark(nc.scalar.activation(
                            out=kb.t.rearrange("p c (a b f) -> p c a b f", a=2, b=2)[:, 0:nch_, hh, :, :],
                            in_=src_, func=AF.Copy, scale=tb["dkb"][:, h:h + 1]))
                    orot.read(mk)
                    kf.wrote(mk); kb.wrote(mk)
                    store(kf, KFd[c0:c0 + N, gi * 512:(gi + 1) * 512].rearrange("(c p) f -> p c f", p=128),
                          kf.t[:, 0:nch_, :], sd["kf"])
                    store(kb, KBd[c0:c0 + N, gi * 512:(gi + 1) * 512].rearrange("(c p) f -> p c f", p=128),
                          kb.t[:, 0:nch_, :], sd["kb"])

                linear_tok(hn, nchunk, WB, 12288, [[(2048 + 512 * gi, 512)] for gi in range(4)], ev_ktok, ("in", j))

                def mk_ev_fm(ob):
                    def ev(gi, g, nm):
                        xcv = ra["orot"].rearrange("p c a b f -> p c (a b f)")
                        orot.begin()
                        ACT.wait(g.w, orot.prev)
                        mkc = ACT.mark(nc.scalar.activation(out=xcv[:, 0:4, 0:N], in_=g.t[:, 0:4, 0:N], func=AF.Copy))
                        g.read(mkc)
                        orot.wrote(mkc)
                        tmp.begin()
                        DVE.wait(mkc, tmp.prev, cf.w, sf.w, ob.prev)
                        last = None
                        for hh in range(2):
                            x1 = xcv[:, 2 * hh, 0:N]
                            x2 = xcv[:, 2 * hh + 1, 0:N]
                            DVE.wait(last)
                            nc.vector.tensor_tensor(out=ra["ta"][:, 0:N], in0=x1, in1=cf.t[:, 0:N], op=ALU.mult)
                            nc.vector.tensor_tensor(out=ra["tb"][:, 0:N], in0=x2, in1=sf.t[:, 0:N], op=ALU.mult)
                            nc.vector.tensor_tensor(out=ra["tc"][:, 0:N], in0=x2, in1=cf.t[:, 0:N], op=ALU.mult)
                            mm = DVE.mark(nc.vector.tensor_tensor(out=ra["td"][:, 0:N], in0=x1, in1=sf.t[:, 0:N],
                                                                  op=ALU.mult))
                            DVE.wait(mm)
                            kc1 = gi * 4 + 2 * hh
                            nc.vector.tensor_tensor(out=ob.t[:, 0:nchunk, kc1, :],
                                                    in0=ra["ta"][:, 0:N].rearrange("p (c f) -> p c f", f=128),
                                                    in1=ra["tb"][:, 0:N].rearrange("p (c f) -> p c f", f=128),
                                                    op=ALU.subtract)
                            last = DVE.mark(nc.vector.tensor_tensor(
                                out=ob.t[:, 0:nchunk, kc1 + 1, :],
                                in0=ra["tc"][:, 0:N].rearrange("p (c f) -> p c f", f=128),
                                in1=ra["td"][:, 0:N].rearrange("p (c f) -> p c f", f=128), op=ALU.add))
                        orot.read(last)
                        tmp.wrote(last)
                        ob.wrote(last)
                    return ev

                qo.begin(); ko.begin()
                linear_fm(lambda kc: hn.t[:, kc, 0:N], hn, KC, WB, 12288,
                          [[(512 * gi, 512)] for gi in range(4)], N, mk_ev_fm(qo), ("in", j))
                linear_fm(lambda kc: hn.t[:, kc, 0:N], hn, KC, WB, 12288,
                          [[(2048 + 512 * gi, 512)] for gi in range(4)], N, mk_ev_fm(ko), ("in", j))
                cf.read(DVE.mark(nc.vector.engine_nop())) if False else None
                for b_ in (cf, sf):
                    b_.read((DVE, DVE.n))
                store(qo, QTd[cb:cb + nchunk].rearrange("c p (k f) -> p c k f", k=16), qo.t[:, 0:nchunk, :, :], sd["q"])
                store(ko, KTd[cb:cb + nchunk].rearrange("c p (k f) -> p c k f", k=16), ko.t[:, 0:nchunk, :, :], sd["k"])
            phase_sync_all()

            rb = carve2(full=True, spec=[("S", [16, 512], F32), ("so", [16, 512], BF16), ("kb0", [2048], BF16), ("kb1", [2048], BF16),
                         ("v0", [4096], BF16), ("v1", [4096], BF16)])
            S = Buf(rb["S"]); so = Buf(rb["so"])
            kbl = Ring([Buf(rb["kb0"], newds()), Buf(rb["kb1"], newds())])
            vl = Ring([Buf(rb["v0"], newds()), Buf(rb["v1"], newds())])
            so_sds = newds(True)
            S.begin()
            DVE.wait(S.prev)
            S.wrote(DVE.mark(nc.vector.memset(S.t[:, :, :], 0.0)))

            def state_update(S, kx, vx, cdk, c):
                for hg in range(4):
                    g = pg.next()
                    g.begin()
                    PE.wait(kx.w, vx.w, g.prev)
                    for hl in range(2):
                        h = 2 * hg + hl
                        for dcc in range(2):
                            ins = nc.tensor.matmul(g.t[:, 2 * hl + dcc, :],
                                                   lhsT=kx.t[:, h * 256 + dcc * 128:h * 256 + (dcc + 1) * 128],
                                                   rhs=vx.t[:, h * 512:(h + 1) * 512], start=True, stop=True)
                    pm = PE.mark(ins)
                    g.wrote(pm)
                    kx.read(pm); vx.read(pm)
                    DVE.wait(pm, S.w, S.r)
                    for hl in range(2):
                        h = 2 * hg + hl
                        ins = nc.vector.scalar_tensor_tensor(out=S.t[:, 2 * h:2 * h + 2, :], in0=S.t[:, 2 * h:2 * h + 2, :],
                                                             scalar=cdk[:, c, h:h + 1],
                                                             in1=g.t[:, 2 * hl:2 * hl + 2, :], op0=ALU.mult, op1=ALU.add)
                    m = DVE.mark(ins)
                    g.read(m)
                    S.wrote(m)

            for c in range(NCH - 1, -1, -1):
                kx = kbl.next(); vx = vl.next()
                kx.begin(); vx.begin()
                load(kx, kx.t[:, :], KBd[c * 128:(c + 1) * 128, :])
                load(vx, vx.t[:, :], Vd[c * 128:(c + 1) * 128, :])
                so.begin()
                ACT.wait(S.w, so.prev)
                for q4 in range(4):
                    ins = nc.scalar.activation(out=so.t[:, 4 * q4:4 * q4 + 4, :], in_=S.t[:, 4 * q4:4 * q4 + 4, :],
                                               func=AF.Copy, scale=kp.t[:, NCH + 1 + c:NCH + 2 + c])
                m = ACT.mark(ins)
                so.wrote(m)
                S.read(m)
                store(so, SBd[c], so.t.rearrange("p a b -> p (a b)"), so_sds)
                state_update(S, kx, vx, tb["cdkb"], c)
            phase_sync_all()

            rc = carve2(full=True, spec=[("S", [16, 512], F32), ("sfb", [16, 512], BF16), ("sbt", [16, 512], BF16),
                         ("qt0", [16, 128], BF16), ("qt1", [16, 128], BF16), ("kt0", [16, 128], BF16),
                         ("kt1", [16, 128], BF16), ("kf0", [2048], BF16), ("kf1", [2048], BF16),
                         ("v0", [4096], BF16), ("g0", [4096], BF16),
                         ("qf", [16, 128], BF16), ("qb", [16, 128], BF16), ("sc", [8, 128], BF16),
                         ("on", [4, 512], F32), ("gated", [4096], BF16), ("gated1", [4096], BF16), ("gT", [32, 128], BF16),
                         ("v1", [4096], BF16), ("g1", [4096], BF16),
                         ("stt", [8, 6], F32), ("mv", [8, 2], F32), ("rs", [8], F32), ("nb", [8], F32)])
            S = Buf(rc["S"]); sfb = Buf(rc["sfb"]); sbt = Buf(rc["sbt"], newds())
            qtl = Ring([Buf(rc["qt0"], newds()), Buf(rc["qt1"], newds())])
            ktl = Ring([Buf(rc["kt0"], newds()), Buf(rc["kt1"], newds())])
            kfl = Ring([Buf(rc["kf0"], newds()), Buf(rc["kf1"], newds())])
            vl = Ring([Buf(rc["v0"], newds()), Buf(rc["v1"], newds())])
            gl = Ring([Buf(rc["g0"], newds()), Buf(rc["g1"], newds())])
            qf = Buf(rc["qf"]); qb = Buf(rc["qb"]); sc = Buf(rc["sc"]); on = Buf(rc["on"])
            gring = Ring([Buf(rc["gated"]), Buf(rc["gated1"])]); gT = Buf(rc["gT"]); stt = Buf(rc["stt"])
            pend = []
            gt_sds = newds(True)
            S.begin(); sfb.begin()
            DVE.wait(S.prev, sfb.prev)
            S.wrote(DVE.mark(nc.vector.memset(S.t[:, :, :], 0.0)))
            sfb.wrote(DVE.mark(nc.vector.memset(sfb.t[:, :, :], 0.0)))
            for c in range(NCH):
                qt = qtl.next(); kt = ktl.next(); kx = kfl.next(); vx = vl.next(); gx = gl.next()
                for b_, src_ in ((qt, QTd[c]), (kt, KTd[c])):
                    b_.begin()
                    load(b_, b_.t.rearrange("p a b -> p (a b)"), src_)
                kx.begin(); load(kx, kx.t[:, :], KFd[c * 128:(c + 1) * 128, :])
                vx.begin(); load(vx, vx.t[:, :], Vd[c * 128:(c + 1) * 128, :])
                gx.begin(); load(gx, gx.t[:, :], Gd[c * 128:(c + 1) * 128, :])
                sbt.begin(); load(sbt, sbt.t.rearrange("p a b -> p (a b)"), SBd[c])
                qf.begin(); qb.begin()
                DVE.wait(qt.w, qf.prev, qb.prev)
                nc.vector.tensor_tensor(out=qf.t[:, :, :], in0=qt.t[:, :, :], in1=tb["fq"][:, :, :], op=ALU.mult)
                m = DVE.mark(nc.vector.tensor_tensor(out=qb.t[:, :, :], in0=qt.t[:, :, :], in1=tb["fb"][:, :, :],
                                                     op=ALU.mult))
                qf.wrote(m); qb.wrote(m); qt.read(m)
                g = pg.next(); g.begin()
                PE.wait(qt.w, kt.w, g.prev)
                for h in range(H):
                    for dcc in range(2):
                        ins = nc.tensor.matmul(g.t[:, h // 4, (h % 4) * 128:(h % 4 + 1) * 128],
                                               lhsT=kt.t[:, 2 * h + dcc, :], rhs=qt.t[:, 2 * h + dcc, :],
                                               start=(dcc == 0), stop=(dcc == 1))
                pm = PE.mark(ins)
                g.wrote(pm); qt.read(pm); kt.read(pm)
                sc.begin()
                DVE.wait(pm, sc.prev)
                for half in range(2):
                    ins = nc.vector.tensor_tensor(out=sc.t[:, 4 * half:4 * half + 4, :],
                                                  in0=g.t[:, half, :].rearrange("p (a b) -> p a b", a=4),
                                                  in1=tb["dmt"][:, 4 * half:4 * half + 4, :], op=ALU.mult)
                m = DVE.mark(ins)
                g.read(m); sc.wrote(m)
                gated = gring.next()
                gated.begin()
                for hg in range(2):
                    g = pg.next(); g.begin()
                    PE.wait(sc.w, vx.w, qf.w, qb.w, sfb.w, sbt.w, g.prev)
                    for hl in range(4):
                        h = 4 * hg + hl
                        nc.tensor.matmul(g.t[:, hl, :], lhsT=sc.t[:, h, :], rhs=vx.t[:, h * 512:(h + 1) * 512],
                                         start=True, stop=False)
                        for dcc in range(2):
                            nc.tensor.matmul(g.t[:, hl, :], lhsT=qf.t[:, 2 * h + dcc, :], rhs=sfb.t[:, 2 * h + dcc, :],
                                             start=False, stop=False)
                        for dcc in range(2):
                            ins = nc.tensor.matmul(g.t[:, hl, :], lhsT=qb.t[:, 2 * h + dcc, :],
                                                   rhs=sbt.t[:, 2 * h + dcc, :], start=False, stop=(dcc == 1))
                    pm = PE.mark(ins)
                    g.wrote(pm)
                    for b_ in (sc, vx, qf, qb, sfb, sbt):
                        b_.read(pm)
                    stt.begin()
                    DVE.wait(pm, stt.prev)
                    for hl in range(4):
                        ins = nc.vector.bn_stats(out=rc["stt"][:, hl, :], in_=g.t[:, hl, :])
                    m = DVE.mark(ins)
                    DVE.wait(m)
                    for hl in range(4):
                        ins = nc.vector.bn_aggr(out=rc["mv"][:, hl, :], in_=rc["stt"][:, hl, :])
                    m = DVE.mark(ins)
                    DVE.wait(m)
                    ACT.wait(m)
                    m = ACT.mark(nc.scalar.activation(out=rc["rs"][:, 0:4], in_=rc["mv"][:, 0:4, 1], func=AF.Sqrt,
                                                      bias=EPS))
                    DVE.wait(m)
                    nc.vector.reciprocal(out=rc["rs"][:, 0:4], in_=rc["rs"][:, 0:4])
                    m = DVE.mark(nc.vector.tensor_copy(out=rc["rs"][:, 4:8], in_=rc["mv"][:, 0:4, 0]))
                    DVE.wait(m)
                    m = DVE.mark(nc.vector.scalar_tensor_tensor(out=rc["nb"][:, 0:4], in0=rc["rs"][:, 4:8], scalar=-1.0,
                                                                in1=rc["rs"][:, 0:4], op0=ALU.mult, op1=ALU.mult))
                    stt.wrote(m)
                    on.begin()
                    ACT.wait(m, on.prev)
                    for hl in range(4):
                        ins = nc.scalar.activation(out=on.t[:, hl, :], in_=g.t[:, hl, :], func=AF.Identity,
                                                   scale=rc["rs"][:, hl:hl + 1], bias=rc["nb"][:, hl:hl + 1])
                    m = ACT.mark(ins)
                    g.read(m); stt.read(m); on.wrote(m)
                    DVE.wait(m, gx.w, gated.prev)
                    m = DVE.mark(nc.vector.tensor_tensor(
                        out=gated.t[:, hg * 2048:(hg + 1) * 2048].rearrange("p (a b) -> p a b", a=4),
                        in0=on.t[:, :, :], in1=gx.t[:, hg * 2048:(hg + 1) * 2048].rearrange("p (a b) -> p a b", a=4),
                        op=ALU.mult))
                    on.read(m); gx.read(m); gated.wrote(m)
                state_update(S, kx, vx, tb["cdkf"], c)
                sfb.begin()
                ACT.wait(S.w, sfb.prev)
                for q4 in range(4):
                    ins = nc.scalar.activation(out=sfb.t[:, 4 * q4:4 * q4 + 4, :], in_=S.t[:, 4 * q4:4 * q4 + 4, :],
                                               func=AF.Copy, scale=kp.t[:, c + 1:c + 2])
                m = ACT.mark(ins)
                sfb.wrote(m); S.read(m)
                def do_tr(c=c, gated=gated):
                    g = pg.next(); g.begin()
                    gb = g.t.rearrange("p a b -> p (a b)").bitcast(BF16).rearrange("p (a b) -> p a b", a=32)
                    PE.wait(gated.w, g.prev, ident.w)
                    for ec in range(32):
                        ins = nc.tensor.transpose(out=gb[:, ec, 0:128], in_=gated.t[:, ec * 128:(ec + 1) * 128],
                                                  identity=ident.t[:, :])
                    pm = PE.mark(ins)
                    g.wrote(pm); gated.read(pm)
                    gT.begin()
                    ACT.wait(pm, gT.prev)
                    DVE.wait(pm, gT.prev)
                    m1 = ACT.mark(nc.scalar.activation(out=gT.t[:, 0:16, :], in_=gb[:, 0:16, 0:128], func=AF.Copy))
                    m2 = DVE.mark(nc.vector.tensor_copy(out=gT.t[:, 16:32, :], in_=gb[:, 16:32, 0:128]))
                    g.read(m1); g.read(m2); gT.wrote(m1); gT.wrote(m2)
                    store(gT, GTd[c], gT.t.rearrange("p a b -> p (a b)"), gt_sds)
                if pend:
                    pend.pop(0)()
                pend.append(do_tr)
            while pend:
                pend.pop(0)()
            phase_sync_all()

            rd = carve2([("gt", [4, 32, 128], BF16)])
            gtb = Buf(rd["gt"], newds())
            for (c0, N) in blocks:
                nchunk = N // 128
                cb = c0 // 128
                pump(1)
                gtb.begin()
                for ci in range(nchunk):
                    load(gtb, gtb.t[:, ci, :, :].rearrange("p e f -> p (e f)"), GTd[cb + ci])
                xm.begin()

                def ev_o(gi, g, nm):
                    ACT.wait(g.w, xm.prev)
                    for m_ in range(nm):
                        ins = nc.scalar.activation(out=xm.t[:, gi * 4 + m_, 0:N], in_=g.t[:, m_, 0:N], func=AF.Copy)
                    mk = ACT.mark(ins)
                    g.read(mk); xm.wrote(mk)

                linear_fm(lambda kc: gtb.t[:, 0:nchunk, kc, :], gtb, 32, WB_OUT[j], D,
                          [[(512 * gi, 512)] for gi in range(4)], N, ev_o, ("out", j))
                back(N, c0, "g_post_mix", li, hin, hout, False)
            phase_sync_all()

        def conformer_layer(j, li, hin, hout):
            HALO = 15
            cw = carve([("u", [KC, 512], BF16), ("sig", [4, 512], F32), ("dg0", [CK, 128], BF16), ("dg1", [CK, 128], BF16)])
            ub = Buf(cw["u"]); sig = Buf(cw["sig"])
            u_sds = newds(True)
            dgr = Ring([Buf(cw["dg0"]), Buf(cw["dg1"])])
            for b_ in dgr.b:
                b_.sds = newds(True)
            wc0 = pc[("w_dw", j)]
            for kc in range(KC):
                t_ = dgr.next()
                t_.begin()
                DVE.wait(t_.prev, ident.w, prm.w)
                for k in range(CK):
                    ins = nc.vector.tensor_scalar(out=t_.t[:, k, :], in0=ident.t[:, :],
                                                  scalar1=prm.t[:, wc0 + kc * CK + k:wc0 + kc * CK + k + 1],
                                                  scalar2=None, op0=ALU.mult)
                t_.wrote(DVE.mark(ins))
                store(t_, DG[j][:, kc * CK * 128:(kc + 1) * CK * 128], t_.t.rearrange("p a b -> p (a b)"), t_.sds)
            for (c0, N) in blocks:
                pump(1)
                front(hin, c0, N, "g_pre_mix", li)
                ub.begin()
                gate_g = {}

                def ev_glu(gi, g, nm):
                    sig.begin()
                    ACT.wait(g.w, sig.prev)
                    for m_ in range(2):
                        kc = gi * 2 + m_
                        ins = nc.scalar.activation(out=sig.t[:, m_, 0:N], in_=g.t[:, 2 + m_, 0:N], func=AF.Sigmoid,
                                                   bias=P("b_pw1", j, KC + kc))
                    mk = ACT.mark(ins)
                    sig.wrote(mk)
                    DVE.wait(mk, ub.prev)
                    for m_ in range(2):
                        kc = gi * 2 + m_
                        ins = nc.vector.scalar_tensor_tensor(out=ub.t[:, kc, 0:N], in0=g.t[:, m_, 0:N],
                                                             scalar=P("b_pw1", j, kc), in1=sig.t[:, m_, 0:N],
                                                             op0=ALU.add, op1=ALU.mult)
                    mk2 = DVE.mark(ins)
                    for si, scn in enumerate(SPECIAL):
                        lo = scn * 128 - c0
                        if 0 <= lo < N:
                            DVE.wait(mk2, msk.w)
                            mk2 = DVE.mark(nc.vector.tensor_tensor(
                                out=ub.t[:, gi * 2:gi * 2 + 2, lo:lo + 128], in0=ub.t[:, gi * 2:gi * 2 + 2, lo:lo + 128],
                                in1=msk.t[:, si * 128:(si + 1) * 128].rearrange("p (a f) -> p a f", a=1).broadcast_to([128, 2, 128])
                                if False else msk.t[:, si * 128:(si + 1) * 128], op=ALU.mult)) if False else mk2
                            for m_ in range(2):
                                kc = gi * 2 + m_
                                mk2 = DVE.mark(nc.vector.tensor_tensor(out=ub.t[:, kc, lo:lo + 128],
                                                                       in0=ub.t[:, kc, lo:lo + 128],
                                                                       in1=msk.t[:, si * 128:(si + 1) * 128], op=ALU.mult))
                                DVE.wait(mk2)
                    g.read(mk2); sig.read(mk2); ub.wrote(mk2)

                linear_fm(lambda kc: hn.t[:, kc, 0:N], hn, KC, WB_PW1[j], 2 * D,
                          [[(256 * gi, 256), (D + 256 * gi, 256)] for gi in range(8)], N, ev_glu, ("pw1", j))
                store(ub, UT.rearrange("(kc p) t -> p kc t", p=128)[:, :, c0:c0 + N], ub.t[:, :, 0:N], u_sds)
            phase_sync_all()

            W_ = 512 + 2 * HALO
            cw = carve([("uh", [KC, W_], BF16), ("mu", [512], F32), ("ex2", [512], F32), ("xh", [KC, 512], BF16),
                        ("t", [512], F32)])
            uh = Buf(cw["uh"], newds()); mu = Buf(cw["mu"]); xh = Buf(cw["xh"]); tt = Buf(cw["t"])
            y = xm
            UTv = UT.rearrange("(kc p) t -> p kc t", p=128)
            for (c0, N) in blocks:
                pump(1)
                lo = max(c0 - HALO, 0)
                hi = min(c0 + N + HALO, T)
                uh.begin()
                o0 = lo - (c0 - HALO)
                if o0 > 0:
                    DVE.wait(uh.prev)
                    uh.wrote(DVE.mark(nc.vector.memset(uh.t[:, :, 0:o0], 0.0)))
                if hi < c0 + N + HALO:
                    DVE.wait(uh.prev)
                    uh.wrote(DVE.mark(nc.vector.memset(uh.t[:, :, o0 + hi - lo:N + 2 * HALO], 0.0)))
                load(uh, uh.t[:, :, o0:o0 + hi - lo], UTv[:, :, lo:hi])
                y.begin()
                for g4 in range(4):
                    g = pg.next()
                    g.begin()
                    for half in range(2):
                        kc0 = g4 * 4 + half * 2
                        sl = wring.next()
                        sl.begin()
                        slf = sl.t.rearrange("p a b -> p (a b)")
                        load(sl, slf[:, 0:2 * CK * 128], DG[j][:, kc0 * CK * 128:(kc0 + 2) * CK * 128])
                        PE.wait(sl.w, uh.w)
                        if half == 0:
                            PE.wait(g.prev)
                        for kk in range(2):
                            kc = kc0 + kk
                            for k in range(CK):
                                ins = nc.tensor.matmul(g.t[:, half * 2 + kk, 0:N],
                                                       lhsT=slf[:, (kk * CK + k) * 128:(kk * CK + k + 1) * 128],
                                                       rhs=uh.t[:, kc, k:k + N], start=(k == 0), stop=(k == CK - 1))
                        pm = PE.mark(ins)
                        sl.read(pm)
                    g.wrote(pm)
                    uh.read(pm)
                    ACT.wait(pm, y.prev, prm.w)
                    for m_ in range(4):
                        kc = g4 * 4 + m_
                        ins = nc.scalar.activation(out=y.t[:, kc, 0:N], in_=g.t[:, m_, 0:N], func=AF.Identity,
                                                   bias=P("b_dw", j, kc))
                    mk = ACT.mark(ins)
                    g.read(mk)
                    y.wrote(mk)
                hn.begin()
                xh.begin()
                ACT.wait(y.w, hn.prev, xh.prev)
                for kc in range(KC):
                    nc.scalar.activation(out=hn.t[:, kc, 0:N], in_=y.t[:, kc, 0:N], func=AF.Square)
                    ins = nc.scalar.activation(out=xh.t[:, kc, 0:N], in_=y.t[:, kc, 0:N], func=AF.Copy)
                mk = ACT.mark(ins)
                hn.wrote(mk); xh.wrote(mk)
                g = pg.next(); g.begin()
                PE.wait(mk, g.prev)
                for kc in range(KC):
                    nc.tensor.matmul(g.t[:, 0, 0:N], lhsT=ones.t[:], rhs=xh.t[:, kc, 0:N], start=(kc == 0),
                                     stop=(kc == KC - 1))
                for kc in range(KC):
                    ins = nc.tensor.matmul(g.t[:, 1, 0:N], lhsT=ones.t[:], rhs=hn.t[:, kc, 0:N], start=(kc == 0),
                                           stop=(kc == KC - 1))
                pm = PE.mark(ins)
                hn.read(pm); xh.read(pm); g.wrote(pm)
                mu.begin(); rstd.begin(); tt.begin()
                DVE.wait(pm, mu.prev, rstd.prev, tt.prev)
                nc.vector.tensor_scalar(out=mu.t[:, 0:N], in0=g.t[:, 0, 0:N], scalar1=1.0 / D, scalar2=None, op0=ALU.mult)
                m1 = DVE.mark(nc.vector.tensor_scalar(out=cw["ex2"][:, 0:N], in0=g.t[:, 1, 0:N], scalar1=1.0 / D,
                                                      scalar2=EPS, op0=ALU.mult, op1=ALU.add))
                DVE.wait(m1)
                m1 = DVE.mark(nc.vector.tensor_tensor(out=tt.t[:, 0:N], in0=mu.t[:, 0:N], in1=mu.t[:, 0:N], op=ALU.mult))
                DVE.wait(m1)
                m1 = DVE.mark(nc.vector.tensor_tensor(out=rstd.t[:, 0:N], in0=cw["ex2"][:, 0:N], in1=tt.t[:, 0:N],
                                                      op=ALU.subtract))
                ACT.wait(m1)
                m1 = ACT.mark(nc.scalar.activation(out=rstd.t[:, 0:N], in_=rstd.t[:, 0:N], func=AF.Sqrt))
                DVE.wait(m1)
                m1 = DVE.mark(nc.vector.reciprocal(out=rstd.t[:, 0:N], in_=rstd.t[:, 0:N]))
                g.read(m1); mu.wrote(m1); rstd.wrote(m1)
                xh.begin()
                lastd = m1
                for kc in range(KC):
                    DVE.wait(lastd, y.w)
                    ma = DVE.mark(nc.vector.tensor_tensor(out=y.t[:, kc, 0:N], in0=y.t[:, kc, 0:N], in1=mu.t[:, 0:N],
                                                          op=ALU.subtract))
                    DVE.wait(ma)
                    lastd = DVE.mark(nc.vector.tensor_tensor(out=y.t[:, kc, 0:N], in0=y.t[:, kc, 0:N],
                                                             in1=rstd.t[:, 0:N], op=ALU.mult))
                    ACT.wait(lastd, xh.prev)
                    mk = ACT.mark(nc.scalar.activation(out=xh.t[:, kc, 0:N], in_=y.t[:, kc, 0:N], func=AF.Silu,
                                                       scale=P("ln_g", j, kc), bias=P("ln_b", j, kc)))
                xh.wrote(mk); y.read(mk); mu.read(lastd); rstd.read(lastd)
                xm.begin()

                def ev_p2(gi, g, nm):
                    ACT.wait(g.w, xm.prev)
                    for m_ in range(nm):
                        kc = gi * 4 + m_
                        ins = nc.scalar.activation(out=xm.t[:, kc, 0:N], in_=g.t[:, m_, 0:N], func=AF.Identity,
                                                   bias=P("b_pw2", j, kc))
                    mk_ = ACT.mark(ins)
                    g.read(mk_); xm.wrote(mk_)

                linear_fm(lambda kc: xh.t[:, kc, 0:N], xh, KC, WB_PW2[j], D,
                          [[(512 * gi, 512)] for gi in range(4)], N, ev_p2, ("pw2", j))
                back(N, c0, "g_post_mix", li, hin, hout, True)
            phase_sync_all()

        def ffn_layer(li, hin, hout):
            fw = carve_at(COMMON, ARENA, [("act", [FC, 512], BF16), ("a0", [2, 514], F32), ("a1", [2, 514], F32),
                                          ("y", [2, 512], F32), ("hn2", [KC, 512], BF16), ("rstf", [512], F32)]
                          + [(f"hf{i}", [512], F32) for i in range(7)])
            hres = Ring([Buf(fw[f"hf{i}"], hresf_ds[i]) for i in range(7)])
            act = Buf(fw["act"])
            ar = Ring([Buf(fw["a0"]), Buf(fw["a1"])])
            yb = Buf(fw["y"])
            hns = [hn, Buf(fw["hn2"])]
            rstf = Buf(fw["rstf"])
            fblocks = [(s_, min(s_ + 510, T)) for s_ in range(0, T, 510)]
            NB = len(fblocks)
            hs = hin.rearrange("(kc p) t -> p kc t", p=128)

            def geom(bi):
                s0, e0 = fblocks[bi]
                lo = max(s0 - 1, 0)
                hi = min(e0 + 1, T)
                return dict(s0=s0, e0=e0, lo=lo, hi=hi, N=hi - lo, NO=e0 - s0, o0=lo - (s0 - 1))

            fst = {}

            def front_A(bi, kcs):
                ge = geom(bi)
                H_ = hns[bi % 2]
                if kcs[0] == 0:
                    H_.begin()
                    fst["fa"] = None
                last = fst["fa"]
                for kc in kcs:
                    hb = hres.next()
                    hb.begin()
                    load(hb, hb.t[:, 0:ge["N"]], hs[:, kc, ge["lo"]:ge["hi"]])
                    ACT.wait(hb.w, H_.prev)
                    last = ACT.mark(nc.scalar.activation(out=H_.t[:, kc, 0:ge["N"]], in_=hb.t[:, 0:ge["N"]],
                                                         func=AF.Square))
                    hb.read(last)
                fst["fa"] = last
                if kcs[-1] == KC - 1:
                    H_.wrote(last)

            def front_B(bi):
                ge = geom(bi)
                H_ = hns[bi % 2]
                g = pg.next()
                g.begin()
                PE.wait(H_.w, g.prev, ones.w)
                for kc in range(KC):
                    ins = nc.tensor.matmul(g.t[:, 0, 0:ge["N"]], lhsT=ones.t[:], rhs=H_.t[:, kc, 0:ge["N"]],
                                           start=(kc == 0), stop=(kc == KC - 1))
                pm = PE.mark(ins)
                H_.read(pm)
                g.wrote(pm)
                rstf.begin()
                ACT.wait(pm, rstf.prev)
                m0 = ACT.mark(nc.scalar.activation(out=rstf.t[:, 0:ge["N"]], in_=g.t[:, 0, 0:ge["N"]], func=AF.Sqrt,
                                                   scale=1.0 / D, bias=EPS))
                g.read(m0)
                DVE.wait(m0)
                m1 = DVE.mark(nc.vector.reciprocal(out=rstf.t[:, 0:ge["N"]], in_=rstf.t[:, 0:ge["N"]]))
                rstf.wrote(m1)

            def front_C(bi, kcs):
                ge = geom(bi)
                H_ = hns[bi % 2]
                if kcs[0] == 0:
                    H_.begin()
                    fst["fc"] = None
                last = fst["fc"]
                for kc in kcs:
                    hb = hres.next()
                    hb.begin()
                    load(hb, hb.t[:, 0:ge["N"]], hs[:, kc, ge["lo"]:ge["hi"]])
                    DVE.wait(hb.w, H_.prev, rstf.w, prm.w)
                    last = DVE.mark(nc.vector.scalar_tensor_tensor(out=H_.t[:, kc, 0:ge["N"]], in0=hb.t[:, 0:ge["N"]],
                                                                   scalar=P("g_pre_ffn", li, kc), in1=rstf.t[:, 0:ge["N"]],
                                                                   op0=ALU.mult, op1=ALU.mult))
                    hb.read(last)
                fst["fc"] = last
                if kcs[-1] == KC - 1:
                    H_.wrote(last)
                    rstf.read(last)

            def back_A(bi):
                ge = geom(bi)
                S_ = hns[bi % 2]
                S_.begin()
                ACT.wait(xm.w, S_.prev)
                for kc in range(KC):
                    ins = nc.scalar.activation(out=S_.t[:, kc, 0:ge["NO"]], in_=xm.t[:, kc, 0:ge["NO"]], func=AF.Square)
                m = ACT.mark(ins)
                S_.wrote(m)
                xm.read(m)

            def back_B(bi):
                ge = geom(bi)
                S_ = hns[bi % 2]
                g = pg.next()
                g.begin()
                PE.wait(S_.w, g.prev)
                for kc in range(KC):
                    ins = nc.tensor.matmul(g.t[:, 0, 0:ge["NO"]], lhsT=ones.t[:], rhs=S_.t[:, kc, 0:ge["NO"]],
                                           start=(kc == 0), stop=(kc == KC - 1))
                pm = PE.mark(ins)
                S_.read(pm)
                g.wrote(pm)
                rstd.begin()
                ACT.wait(pm, rstd.prev)
                m0 = ACT.mark(nc.scalar.activation(out=rstd.t[:, 0:ge["NO"]], in_=g.t[:, 0, 0:ge["NO"]], func=AF.Sqrt,
                                                   scale=1.0 / D, bias=EPS))
                g.read(m0)
                DVE.wait(m0)
                m1 = DVE.mark(nc.vector.reciprocal(out=rstd.t[:, 0:ge["NO"]], in_=rstd.t[:, 0:ge["NO"]]))
                rstd.wrote(m1)

            def back_C(bi, kcs):
                ge = geom(bi)
                NO = ge["NO"]
                if kcs[0] == 0:
                    fst["bc"] = None
                last = fst["bc"]
                for kc in kcs:
                    hb = hres.next()
                    hb.begin()
                    load(hb, hb.t[:, 0:NO], hs[:, kc, ge["s0"]:ge["e0"]])
                    DVE.wait(rstd.w, hb.w, xm.w, last)
                    ma = DVE.mark(nc.vector.scalar_tensor_tensor(out=xm.t[:, kc, 0:NO], in0=xm.t[:, kc, 0:NO],
                                                                 scalar=P("g_post_ffn", li, kc), in1=rstd.t[:, 0:NO],
                                                                 op0=ALU.mult, op1=ALU.mult))
                    DVE.wait(ma)
                    last = DVE.mark(nc.vector.tensor_tensor(out=xm.t[:, kc, 0:NO], in0=xm.t[:, kc, 0:NO],
                                                            in1=hb.t[:, 0:NO], op=ALU.add))
                    hb.read(last)
                fst["bc"] = last
                if kcs[-1] == KC - 1:
                    xm.wrote(last)
                    rstd.read(last)
                    store(xm, hout.rearrange("(kc p) t -> p kc t", p=128)[:, :, ge["s0"]:ge["e0"]], xm.t[:, :, 0:NO],
                          xm_sds)

            groups = []
            for gi in range((FC + 1) // 2):
                nj = min(2, FC - 2 * gi)
                groups.append([(256 * gi, 128 * nj), (FH + 256 * gi, 128 * nj)])

            front_A(0, list(range(KC))); front_B(0); front_C(0, list(range(KC)))
            for bi in range(NB):
                pump(1)
                ge = geom(bi)
                N = ge["N"]; NO = ge["NO"]; o0 = ge["o0"]; s0 = ge["s0"]; lo = ge["lo"]
                H_ = hns[bi % 2]
                act.begin()

                def ev_up(gi, g, nm, N=N, NO=NO, o0=o0, s0=s0, lo=lo):
                    nj = nm // 2
                    ab = ar.next()
                    ab.begin()
                    ACT.wait(g.w, ab.prev)
                    if o0 > 0:
                        nc.scalar.activation(out=ab.t[:, 0:nj, 0:1], in_=ab.t[:, 0:nj, 0:1], func=AF.Copy, scale=0.0)
                    if o0 + N < NO + 2:
                        nc.scalar.activation(out=ab.t[:, 0:nj, o0 + N:NO + 2], in_=ab.t[:, 0:nj, o0 + N:NO + 2],
                                             func=AF.Copy, scale=0.0)
                    mk = ACT.mark(nc.scalar.activation(out=ab.t[:, 0:nj, o0:o0 + N], in_=g.t[:, 0:nj, 0:N], func=AF.Copy))
                    ab.wrote(mk)
                    if nj == 2:
                        g.h[0].read(mk)
                    yb.begin()
                    DVE.wait(mk, yb.prev, act.prev, prm.w)
                    wl = pc[("ffn_w_dw", li)]
                    ms = [None] * nj
                    for jj in range(nj):
                        jg = gi * 2 + jj
                        ms[jj] = DVE.mark(nc.vector.tensor_scalar(out=yb.t[:, jj, 0:NO], in0=ab.t[:, jj, 0:NO],
                                                                  scalar1=prm.t[:, wl + jg * 3:wl + jg * 3 + 1],
                                                                  scalar2=P("ffn_b_dw", li, jg), op0=ALU.mult, op1=ALU.add))
                    for tap in (1, 2):
                        for jj in range(nj):
                            jg = gi * 2 + jj
                            DVE.wait(ms[jj])
                            ms[jj] = DVE.mark(nc.vector.scalar_tensor_tensor(
                                out=yb.t[:, jj, 0:NO], in0=ab.t[:, jj, tap:NO + tap],
                                scalar=prm.t[:, wl + jg * 3 + tap:wl + jg * 3 + tap + 1],
                                in1=yb.t[:, jj, 0:NO], op0=ALU.mult, op1=ALU.add))
                    last = ms[nj - 1]
                    ab.read(last)
                    ACT.wait(*ms)
                    m3 = ACT.mark(nc.scalar.activation(out=yb.t[:, 0:nj, 0:NO], in_=yb.t[:, 0:nj, 0:NO],
                                                       func=AF.Gelu_apprx_tanh))
                    DVE.wait(m3)
                    vo = s0 - lo
                    m4 = DVE.mark(nc.vector.tensor_tensor(out=act.t[:, gi * 2:gi * 2 + nj, 0:NO], in0=yb.t[:, 0:nj, 0:NO],
                                                          in1=g.t[:, nj:2 * nj, vo:vo + NO], op=ALU.mult))
                    g.read(m4); yb.wrote(m4); act.wrote(m4)

                def hook(gi, bi=bi):
                    if bi > 0:
                        if gi == 0:
                            back_A(bi - 1)
                        elif gi == 2:
                            back_B(bi - 1)
                        elif 3 <= gi < 11:
                            back_C(bi - 1, [2 * (gi - 3), 2 * (gi - 3) + 1])
                    if bi + 1 < NB:
                        if 11 <= gi < 15:
                            front_A(bi + 1, list(range(4 * (gi - 11), 4 * (gi - 11) + 4)))
                        elif gi == 16:
                            front_B(bi + 1)
                        elif 17 <= gi < 21:
                            front_C(bi + 1, list(range(4 * (gi - 17), 4 * (gi - 17) + 4)))

                linear_fm(lambda kc, H_=H_, N=N: H_.t[:, kc, 0:N], H_, KC, WB_UP[li], 2 * FH, groups, N, ev_up,
                          ("up", li), hook=hook)
                xm.begin()

                def ev_dn(gi, g, nm, NO=NO):
                    ACT.wait(g.w, xm.prev)
                    for m_ in range(nm):
                        ins = nc.scalar.activation(out=xm.t[:, gi * 4 + m_, 0:NO], in_=g.t[:, m_, 0:NO], func=AF.Copy)
                    mk_ = ACT.mark(ins)
                    g.read(mk_); xm.wrote(mk_)

                linear_fm(lambda kc, NO=NO: act.t[:, kc, 0:NO], act, FC, WB_DN[li], D,
                          [[(512 * gi, 512)] for gi in range(4)], NO, ev_dn, ("dn", li))
            back_A(NB - 1); back_B(NB - 1); back_C(NB - 1, list(range(KC)))
            phase_sync_all()

        for q in (ACT, DVE, POOL, PE):
            q.wait(prm.w, dec.w, kp.w, msk.w, ident.w, ones.w)
        cur = XT
        ds_base = dict(ds_i)
        for li in range(DEPTH):
            j = li // 2
            mid = HB
            ds_i.update(ds_base)
            if li % 2 == 0:
                retention_layer(j, li, cur, mid)
            else:
                conformer_layer(j, li, cur, mid)
            nxt = YT if li == DEPTH - 1 else HA
            ds_i.update(ds_base)
            ffn_layer(li, mid, nxt)
            cur = nxt
        pump(10 ** 6)
        for q in (POOL,):
            q.wait(store_marks)
    return nc, pc, NP, SPECIAL, NCH, T


def _fm(vec):
    return np.ascontiguousarray(vec.reshape(-1, 128).T)


def make_core_inputs(seqs, meta, SEGC, SPECIAL, NCH, T):
    xt = np.zeros((T, D), np.float32)
    keep = np.ones(NCH + 1, np.float32)
    valid = np.zeros(T, np.float32)
    pos = np.zeros(T, np.float64)
    places = []
    c = 0
    for s in seqs:
        L = s.shape[0]
        ncs = L // 128
        keep[c] = 0.0
        r0 = c * 128 + 128 - NMETA
        xt[r0:r0 + NMETA] = meta
        xt[r0 + NMETA:r0 + NMETA + L] = s
        valid[r0:r0 + NMETA + L] = 1.0
        pos[c * 128:(c + 1 + ncs) * 128] = np.arange((1 + ncs) * 128)
        places.append((r0 + NMETA, L))
        c += 1 + ncs
    while c < NCH:
        keep[c] = 0.0
        c += 1
    keep[NCH] = 0.0
    keepb = np.concatenate([keep[1:NCH + 1], [0.0]]).astype(np.float32)
    kpv = np.concatenate([keep, keepb]).astype(np.float32)
    kp = np.ascontiguousarray(np.broadcast_to(kpv[None, :], (128, kpv.size))).astype(np.float32)
    mskv = np.concatenate([valid[sc * 128:(sc + 1) * 128] for sc in SPECIAL])
    msk = np.ascontiguousarray(np.broadcast_to(mskv[None, :], (128, mskv.size))).astype(np.float32)
    inv = 10000.0 ** (-np.arange(128, dtype=np.float32) / np.float32(128))
    ang = pos.astype(np.float32)[:, None] * inv[None, :].astype(np.float32)
    cost = np.cos(ang.astype(np.float64)).astype(np.float32)
    sint = np.sin(ang.astype(np.float64)).astype(np.float32)
    return dict(xt=np.ascontiguousarray(xt.T), kp=kp, msk=msk, cost=cost, sint=sint,
                cosf=np.ascontiguousarray(cost.T), sinf=np.ascontiguousarray(sint.T)), places


def pack_params(inp, pc, NP, DEPTH):
    prm = np.zeros((128, NP), np.float32)
    for (nm, idx), o in pc.items():
        if nm == "g_pre_mix":
            a = _fm(inp["norm_pre_mix"][idx])
        elif nm == "g_post_mix":
            a = _fm(inp["norm_post_mix"][idx])
        elif nm == "g_pre_ffn":
            a = _fm(inp["norm_pre_ffn"][idx])
        elif nm == "g_post_ffn":
            a = _fm(inp["norm_post_ffn"][idx])
        elif nm == "ffn_w_dw":
            w = inp["ffn_w_dw"][idx]
            a = np.ascontiguousarray(w.T.reshape(FC, 128, 3).transpose(1, 0, 2).reshape(128, FC * 3))
        elif nm == "ffn_b_dw":
            a = _fm(inp["ffn_b_dw"][idx])
        elif nm == "b_pw1":
            a = _fm(inp["conv_b_pw1"][idx])
        elif nm == "w_dw":
            w = inp["conv_w_dw"][idx]
            a = np.ascontiguousarray(w.T.reshape(KC, 128, CK).transpose(1, 0, 2).reshape(128, KC * CK))
        elif nm == "b_dw":
            a = _fm(inp["conv_b_dw"][idx])
        elif nm == "ln_g":
            a = _fm(inp["conv_ln_g"][idx])
        elif nm == "ln_b":
            a = _fm(inp["conv_ln_b"][idx])
        elif nm == "b_pw2":
            a = _fm(inp["conv_b_pw2"][idx])
        prm[:, o:o + a.shape[1]] = a
    return prm


_CACHE = {}


def run_model(inp, core_seqs, SEGC, DEPTH):
    key = (SEGC, DEPTH)
    if key not in _CACHE:
        _CACHE[key] = build_program(SEGC, DEPTH)
    nc, pc, NP, SPECIAL, NCH, T = _CACHE[key]
    NRET = (DEPTH + 1) // 2
    NCONV = DEPTH // 2
    prm = pack_params(inp, pc, NP, DEPTH)
    decv = np.zeros((NRET * 16,), np.float32)
    for j in range(NRET):
        decv[j * 16:j * 16 + 8] = inp["ret_decay_fwd"][j]
        decv[j * 16 + 8:j * 16 + 16] = inp["ret_decay_bwd"][j]
    dec = np.ascontiguousarray(np.broadcast_to(decv[None, :], (128, decv.size))).astype(np.float32)

    def flat(a, n):
        a = np.ascontiguousarray(a[:n]) if n > 0 else np.zeros((1,) + a.shape[1:], np.float32)
        return a.reshape(-1, 2048)

    shared = dict(prm=prm, dec=dec,
                  ret_w_in=flat(inp["ret_w_in"], NRET), ret_w_out=flat(inp["ret_w_out"], NRET),
                  conv_w_pw1=flat(inp["conv_w_pw1"], NCONV), conv_w_pw2=flat(inp["conv_w_pw2"], NCONV),
                  ffn_w_up=flat(inp["ffn_w_up"], DEPTH), ffn_w_down=flat(inp["ffn_w_down"], DEPTH))
    in_maps = []
    places = []
    for seqs in core_seqs:
        d, pl = make_core_inputs(seqs, inp["meta_tokens"], SEGC, SPECIAL, NCH, T)
        d.update(shared)
        in_maps.append(d)
        places.append(pl)
    res = run_bass_kernel_spmd(nc, in_maps, core_ids=list(range(len(core_seqs))))
    outs = []
    for ci, pl in enumerate(places):
        yt = res.results[ci]["yt"]
        y = yt.T
        outs.append([np.ascontiguousarray(y[r0:r0 + L]) for (r0, L) in pl])
    return outs


def kernel(**inp):
    xp = np.asarray(inp["x_prompt"], np.float32)
    xs = np.asarray(inp["x_sample"], np.float32)
    SEGC = xp.shape[1] // 128
    DEPTH = 4
    core_seqs = []
    for c in range(4):
        core_seqs.append([xs[c], xp[c]])
    for c in range(4):
        core_seqs.append([xp[4 + 3 * c + k] for k in range(3)])
    outs = run_model(inp, core_seqs, SEGC, DEPTH)
    yp = np.zeros_like(xp)
    ys = np.zeros_like(xs)
    for c in range(4):
        ys[c] = outs[c][0]
        yp[c] = outs[c][1]
    for c in range(4):
        for k in range(3):
            yp[4 + 3 * c + k] = outs[4 + c][k]
    return (yp, ys)
```
